# Optimizing a Trainium2 kernel written in Bass

```python
import jax, jax.numpy as jnp
from jax import lax
import numpy as np

D_MODEL = 1024
BATCH = 8
SEQ = 4096
DEPTH = 4

N_MIXERS = 2
HEAD_DIM = 64
FOX_HEADS = D_MODEL // HEAD_DIM
DIL_HEADS = D_MODEL // HEAD_DIM
DIL_CONFIGS = ((128, 1), (512, 4), (2048, 16))
N_DIL_GROUPS = len(DIL_CONFIGS)
ROPE_THETA = 500000.0
ROPE_DIM = HEAD_DIM // 4
D_FF = 2816
Q_BLOCK = 128
EPS = 1e-6
MACARON_WEIGHT = 0.5
N_SUBLAYERS = 3
N_FOX_LAYERS = (DEPTH + 1) // 2
N_DIL_LAYERS = DEPTH // 2
FOX_IN = 3 * FOX_HEADS * HEAD_DIM + FOX_HEADS
DIL_IN = N_DIL_GROUPS * 3 * DIL_HEADS * HEAD_DIM

kernel_name = "hybrid_fox_dilated_macaron_adaln"


def rms_norm(x, g):
    xf = x.astype(jnp.float32)
    y = xf * lax.rsqrt(jnp.mean(xf * xf, axis=-1, keepdims=True) + EPS)
    return (y * g.astype(jnp.float32)).astype(x.dtype)


def modulate(h, shift, scale):
    return h * (1.0 + scale[:, None, :]) + shift[:, None, :]


def swiglu(h, w_gate, w_up, w_down):
    return (jax.nn.silu(h @ w_gate) * (h @ w_up)) @ w_down


def rope_tables(positions):
    half = ROPE_DIM // 2
    inv_freq = ROPE_THETA ** (-(jnp.arange(half, dtype=jnp.float32) * 2.0 / ROPE_DIM))
    ang = positions.astype(jnp.float32)[..., None] * inv_freq
    return jnp.cos(ang)[:, :, None, :], jnp.sin(ang)[:, :, None, :]


def apply_rope(x, cos, sin):
    half = ROPE_DIM // 2
    xr = x[..., :ROPE_DIM].astype(jnp.float32)
    x1, x2 = xr[..., :half], xr[..., half:]
    rot = jnp.concatenate([x1 * cos - x2 * sin, x2 * cos + x1 * sin], axis=-1).astype(x.dtype)
    return jnp.concatenate([rot, x[..., ROPE_DIM:]], axis=-1)


def fox_attention(h, w_in, b_f, q_gain, k_gain, w_out):
    B, S, _ = h.shape
    H, hd = FOX_HEADS, HEAD_DIM
    proj = h @ w_in
    qkv = proj[..., : 3 * H * hd].reshape(B, S, 3, H, hd)
    f_logit = proj[..., 3 * H * hd:].astype(jnp.float32) + b_f.astype(jnp.float32)
    q = rms_norm(qkv[:, :, 0], q_gain).transpose(0, 2, 1, 3)
    k = rms_norm(qkv[:, :, 1], k_gain).transpose(0, 2, 1, 3)
    v = qkv[:, :, 2].transpose(0, 2, 1, 3)
    cum = jnp.cumsum(jax.nn.log_sigmoid(f_logit), axis=1).transpose(0, 2, 1)
    scale = hd ** -0.5
    outs = []
    for blk in range(S // Q_BLOCK):
        q0 = blk * Q_BLOCK
        kv_len = q0 + Q_BLOCK
        logits = jnp.einsum("bhqd,bhkd->bhqk", q[:, :, q0:kv_len], k[:, :, :kv_len]).astype(jnp.float32) * scale
        logits = logits + cum[:, :, q0:kv_len, None] - cum[:, :, None, :kv_len]
        q_idx = q0 + jnp.arange(Q_BLOCK)
        k_idx = jnp.arange(kv_len)
        mask = k_idx[None, :] <= q_idx[:, None]
        logits = jnp.where(mask[None, None], logits, -jnp.inf)
        p = jax.nn.softmax(logits, axis=-1).astype(v.dtype)
        outs.append(jnp.einsum("bhqk,bhkd->bhqd", p, v[:, :, :kv_len]))
    o = jnp.concatenate(outs, axis=2).transpose(0, 2, 1, 3).reshape(B, S, H * hd)
    return o @ w_out


def dilated_group(q, k, v, window, dilation):
    B, S, H, hd = q.shape
    n_keys = window // dilation + 1
    offsets = dilation * jnp.arange(n_keys)
    scale = hd ** -0.5

    def block_fn(q0):
        qb = lax.dynamic_slice_in_dim(q, q0, Q_BLOCK, axis=1)
        t = q0 + jnp.arange(Q_BLOCK)
        idx = t[:, None] - offsets[None, :]
        valid = idx >= 0
        idx_c = jnp.maximum(idx, 0)
        kb = jnp.take(k, idx_c, axis=1)
        vb = jnp.take(v, idx_c, axis=1)
        logits = jnp.einsum("bqhd,bqkhd->bhqk", qb, kb).astype(jnp.float32) * scale
        logits = jnp.where(valid[None, None], logits, -jnp.inf)
        lse = jax.nn.logsumexp(logits, axis=-1)
        p = jnp.exp(logits - lse[..., None]).astype(v.dtype)
        ob = jnp.einsum("bhqk,bqkhd->bqhd", p, vb)
        return ob, lse

    starts = jnp.arange(S // Q_BLOCK) * Q_BLOCK
    o, lse = lax.map(block_fn, starts)
    o = o.transpose(1, 0, 2, 3, 4).reshape(B, S, H, hd)
    lse = lse.transpose(1, 0, 3, 2).reshape(B, S, H)
    return o, lse


def dilated_attention(h, cos, sin, w_in, q_gain, k_gain, w_out):
    B, S, _ = h.shape
    H, hd = DIL_HEADS, HEAD_DIM
    proj = (h @ w_in).reshape(B, S, N_DIL_GROUPS, 3, H, hd)
    outs, lses = [], []
    for g, (window, dilation) in enumerate(DIL_CONFIGS):
        q = apply_rope(rms_norm(proj[:, :, g, 0], q_gain[g]), cos, sin)
        k = apply_rope(rms_norm(proj[:, :, g, 1], k_gain[g]), cos, sin)
        o, lse = dilated_group(q, k, proj[:, :, g, 2], window, dilation)
        outs.append(o)
        lses.append(lse)
    alpha = jax.nn.softmax(jnp.stack(lses, axis=0), axis=0)
    o = jnp.sum(alpha[..., None] * jnp.stack(outs, axis=0).astype(jnp.float32), axis=0).astype(h.dtype)
    return o.reshape(B, S, H * hd) @ w_out


def setup_inputs(seed: int = 0) -> dict:
    key = jax.random.key(seed)
    ks = jax.random.split(key, 20)
    D = D_MODEL
    nrm = jax.random.normal
    x = nrm(ks[0], (BATCH, SEQ, D), jnp.float32)
    c = nrm(ks[1], (BATCH, D), jnp.float32)
    offs = jax.random.randint(ks[2], (BATCH, 1), 0, 4096, dtype=jnp.int32)
    positions = (offs + jnp.arange(SEQ, dtype=jnp.int32)[None, :]).astype(jnp.int32)
    mod_w = nrm(ks[3], (DEPTH, D, N_SUBLAYERS * 3 * D), jnp.float32) * (0.5 * D ** -0.5)
    mod_b = nrm(ks[4], (DEPTH, N_SUBLAYERS * 3 * D), jnp.float32) * 0.01
    norm_g = 1.0 + 0.02 * nrm(ks[5], (DEPTH, N_SUBLAYERS, D), jnp.float32)
    ffn_w_gate = nrm(ks[6], (DEPTH, 2, D, D_FF), jnp.float32) * D ** -0.5
    ffn_w_up = nrm(ks[7], (DEPTH, 2, D, D_FF), jnp.float32) * D ** -0.5
    ffn_w_down = nrm(ks[8], (DEPTH, 2, D_FF, D), jnp.float32) * D_FF ** -0.5
    fox_w_in = nrm(ks[9], (N_FOX_LAYERS, D, FOX_IN), jnp.float32) * D ** -0.5
    fox_b_f = jax.random.uniform(ks[10], (N_FOX_LAYERS, FOX_HEADS), jnp.float32, 1.0, 6.0)
    fox_q_g = 1.0 + 0.02 * nrm(ks[11], (N_FOX_LAYERS, HEAD_DIM), jnp.float32)
    fox_k_g = 1.0 + 0.02 * nrm(ks[12], (N_FOX_LAYERS, HEAD_DIM), jnp.float32)
    fox_w_out = nrm(ks[13], (N_FOX_LAYERS, FOX_HEADS * HEAD_DIM, D), jnp.float32) * (FOX_HEADS * HEAD_DIM) ** -0.5
    dil_w_in = nrm(ks[14], (N_DIL_LAYERS, D, DIL_IN), jnp.float32) * D ** -0.5
    dil_q_g = 1.0 + 0.02 * nrm(ks[15], (N_DIL_LAYERS, N_DIL_GROUPS, HEAD_DIM), jnp.float32)
    dil_k_g = 1.0 + 0.02 * nrm(ks[16], (N_DIL_LAYERS, N_DIL_GROUPS, HEAD_DIM), jnp.float32)
    dil_w_out = nrm(ks[17], (N_DIL_LAYERS, DIL_HEADS * HEAD_DIM, D), jnp.float32) * (DIL_HEADS * HEAD_DIM) ** -0.5
    return {"x": x, "c": c, "positions": positions, "mod_w": mod_w, "mod_b": mod_b,
            "norm_g": norm_g, "ffn_w_gate": ffn_w_gate, "ffn_w_up": ffn_w_up,
            "ffn_w_down": ffn_w_down, "fox_w_in": fox_w_in, "fox_b_f": fox_b_f,
            "fox_q_g": fox_q_g, "fox_k_g": fox_k_g, "fox_w_out": fox_w_out,
            "dil_w_in": dil_w_in, "dil_q_g": dil_q_g, "dil_k_g": dil_k_g, "dil_w_out": dil_w_out}


def reference(x, c, positions, mod_w, mod_b, norm_g, ffn_w_gate, ffn_w_up, ffn_w_down,
              fox_w_in, fox_b_f, fox_q_g, fox_k_g, fox_w_out,
              dil_w_in, dil_q_g, dil_k_g, dil_w_out):
    B = x.shape[0]
    D = x.shape[-1]
    cos, sin = rope_tables(positions)
    c_act = jax.nn.silu(c)
    for i in range(DEPTH):
        mod = (c_act @ mod_w[i] + mod_b[i]).reshape(B, N_SUBLAYERS, 3, D)
        shift, scale, gate = mod[:, :, 0], mod[:, :, 1], mod[:, :, 2]
        h = modulate(rms_norm(x, norm_g[i, 0]), shift[:, 0], scale[:, 0])
        x = x + MACARON_WEIGHT * gate[:, 0, None, :] * swiglu(h, ffn_w_gate[i, 0], ffn_w_up[i, 0], ffn_w_down[i, 0])
        h = modulate(rms_norm(x, norm_g[i, 1]), shift[:, 1], scale[:, 1])
        j = i // N_MIXERS
        if i % N_MIXERS == 0:
            y = fox_attention(h, fox_w_in[j], fox_b_f[j], fox_q_g[j], fox_k_g[j], fox_w_out[j])
        else:
            y = dilated_attention(h, cos, sin, dil_w_in[j], dil_q_g[j], dil_k_g[j], dil_w_out[j])
        x = x + gate[:, 1, None, :] * y
        h = modulate(rms_norm(x, norm_g[i, 2]), shift[:, 2], scale[:, 2])
        x = x + MACARON_WEIGHT * gate[:, 2, None, :] * swiglu(h, ffn_w_gate[i, 1], ffn_w_up[i, 1], ffn_w_down[i, 1])
    return x
```

```python
import math
import os
from contextlib import ExitStack

import numpy as np
import concourse.bass as bass
import concourse.mybir as mybir
from concourse.bass_utils import run_bass_kernel_spmd

F32 = mybir.dt.float32
BF16 = mybir.dt.bfloat16
I32 = mybir.dt.int32
AF = mybir.ActivationFunctionType
ALU = mybir.AluOpType

D = 1024
DC = 8
HD = 64
NH = 16
DFF = 2816
FC = 22
EPS = 1e-6
FOX_IN = 3 * D + NH
DIL_IN = 9 * D
DIL_CFG = ((128, 1), (512, 4), (2048, 16))
ROPE_THETA = 500000.0

C_IDENT = 0
C_ONES = 128
C_BLK = 256
C_UTRI = 384
C_MDIAG = 512
C_MOFF = 640
C_PERM = 768
C_ROPE = 896
C_EPS = 898
C_MNEG = 900
NCF = 1028


def make_consts():
    cf = np.zeros((128, NCF), np.float32)
    cf[:, C_IDENT:C_IDENT + 128] = np.eye(128, dtype=np.float32)
    cf[:, C_ONES:C_ONES + 128] = 1.0
    for a in range(2):
        cf[a * 64:(a + 1) * 64, C_BLK + a * 64:C_BLK + (a + 1) * 64] = 1.0
    k = np.arange(128)[:, None]
    q = np.arange(128)[None, :]
    cf[:, C_UTRI:C_UTRI + 128] = (k <= q)
    cf[:, C_MDIAG:C_MDIAG + 128] = (q >= k)
    cf[:, C_MOFF:C_MOFF + 128] = (q <= k)
    half = 8
    inv_freq = (np.float32(ROPE_THETA) ** (-(np.arange(half, dtype=np.float32) * np.float32(2.0) / np.float32(16.0)))).astype(np.float32)
    for p in range(128):
        d = p % 64
        a = p // 64
        if d < 8:
            cf[a * 64 + d + 8, C_PERM + p] = 1.0
            cf[p, C_ROPE] = inv_freq[d]
            cf[p, C_ROPE + 1] = -1.0
        elif d < 16:
            cf[a * 64 + d - 8, C_PERM + p] = 1.0
            cf[p, C_ROPE] = inv_freq[d - 8]
            cf[p, C_ROPE + 1] = 1.0
    cf[:, C_EPS] = EPS
    cf[:, C_MNEG:C_MNEG + 128] = np.where(q < k, -30000.0, 0.0)
    return cf


class StopBuild(Exception):
    pass


class Tok:
    __slots__ = ("w", "r", "dsem", "persist", "name")

    def __init__(self, name="", persist=False):
        self.w = None
        self.r = {}
        self.dsem = None
        self.persist = persist
        self.name = name


class K:
    ENG = ("pe", "act", "dve", "pool", "sp")

    def __init__(self, nc):
        self.nc = nc
        self.e = dict(pe=nc.tensor, act=nc.scalar, dve=nc.vector, pool=nc.gpsimd, sp=nc.sync)
        self.sems = {}
        self.cnt = {}
        self.seen = {e: {} for e in self.ENG}
        self.toks = []
        self.bgkeys = set()
        self.uid = 0
        self.free_dsems = []
        self.log = {e: [] for e in self.ENG}
        for e in self.ENG:
            self._mk(e)
        self._mk("bar")

    def _mk(self, key):
        self.sems[key] = self.nc.alloc_semaphore(name="s_" + key)
        self.cnt[key] = 0

    def tok(self, name="", persist=False):
        t = Tok(name, persist)
        self.toks.append(t)
        return t

    def toks_n(self, n, name=""):
        return [self.tok(f"{name}{i}") for i in range(n)]

    def name(self, base):
        self.uid += 1
        return f"{base}_{self.uid}"

    def _wait(self, eng, deps):
        for key, val in deps:
            if key == "pe" and eng == "pe":
                continue
            if self.seen[eng].get(key, 0) >= val:
                continue
            self.e[eng].wait_ge(self.sems[key], val)
            self.log[eng].append(("w", key, val))
            self.seen[eng][key] = val

    def check(self):
        val = {key: 0 for key in self.cnt}
        pc = {e: 0 for e in self.ENG}
        progress = True
        while progress:
            progress = False
            for e in self.ENG:
                lg = self.log[e]
                while pc[e] < len(lg):
                    ev = lg[pc[e]]
                    if ev[0] == "w":
                        if val[ev[1]] >= ev[2]:
                            pc[e] += 1
                            progress = True
                        else:
                            break
                    else:
                        val[ev[1]] += ev[2]
                        pc[e] += 1
                        progress = True
        bad = {e: (pc[e], len(self.log[e]), self.log[e][pc[e]], val[self.log[e][pc[e]][1]]) for e in self.ENG if pc[e] < len(self.log[e])}
        return bad

    @staticmethod
    def _deps(r, w):
        d = []
        for b in r:
            if b.w is not None:
                d.append(b.w)
        for b in w:
            if b.w is not None:
                d.append(b.w)
            d.extend(b.r.items())
        return d

    def op(self, eng, fn, r=(), w=(), inc=True):
        self._wait(eng, self._deps(r, w))
        ins = fn(self.e[eng])
        if inc:
            self.cnt[eng] += 1
            ins.then_inc(self.sems[eng], 1)
            self.log[eng].append(("i", eng, 1))
            tag = (eng, self.cnt[eng])
        else:
            tag = (eng, self.cnt[eng] + 1)
        for b in w:
            b.w = tag
            b.r = {}
        for b in r:
            if b.r.get(tag[0], 0) < tag[1]:
                b.r[tag[0]] = tag[1]
        return ins

    def dma(self, q, out, in_, r=(), w=(), st=None, **kw):
        self._wait(q, self._deps(r, w))
        if st.dsem is None:
            if self.free_dsems:
                st.dsem = self.free_dsems.pop()
            else:
                self.uid += 1
                st.dsem = f"d{self.uid}"
                self._mk(st.dsem)
            if st.persist:
                self.bgkeys.add(st.dsem)
        key = st.dsem
        self.cnt[key] += 16
        ins = self.e[q].dma_start(out=out, in_=in_, **kw)
        ins.then_inc(self.sems[key], 16)
        self.log[q].append(("i", key, 16))
        tag = (key, self.cnt[key])
        for b in w:
            b.w = tag
            b.r = {}
        for b in r:
            if b.r.get(key, 0) < tag[1]:
                b.r[key] = tag[1]
        return ins

    def barrier(self):
        sp = self.e["sp"]
        for key in list(self.cnt.keys()):
            if key in ("sp", "bar") or key in self.bgkeys:
                continue
            if self.seen["sp"].get(key, 0) < self.cnt[key]:
                sp.wait_ge(self.sems[key], self.cnt[key])
                self.log["sp"].append(("w", key, self.cnt[key]))
                self.seen["sp"][key] = self.cnt[key]
        self.cnt["bar"] += 1
        sp.sem_inc(self.sems["bar"], 1)
        self.log["sp"].append(("i", "bar", 1))
        for e in self.ENG:
            if e != "sp":
                self.e[e].wait_ge(self.sems["bar"], self.cnt["bar"])
                self.log[e].append(("w", "bar", self.cnt["bar"]))
            for key in self.cnt:
                if key in self.bgkeys:
                    continue
                self.seen[e][key] = self.cnt[key]
        keep = []
        for t in self.toks:
            if t.persist:
                keep.append(t)
            else:
                t.w = None
                t.r = {}
                if t.dsem is not None:
                    self.free_dsems.append(t.dsem)
                    t.dsem = None
        self.toks = keep


def build(S=4096, depth=4, debug_stop=None):
    nc = bass.Bass("TRN2", target_bir_lowering=False)
    NTB = S // 128
    n_fox = (depth + 1) // 2
    n_dil = depth // 2

    def din(name, shape, dt=F32):
        return nc.dram_tensor(name, list(shape), dt, kind="ExternalInput")

    x_in = din("x", [S, D]).ap()
    c_in = din("c", [DC, 128]).ap()
    pos_in = din("positions", [1, S], I32).ap()
    mod_w = din("mod_w", [depth, D, 9 * D]).ap()
    mod_b = din("mod_b", [depth * 72, 128]).ap()
    norm_g = din("norm_g", [depth * 24, 128]).ap()
    w_gate = din("ffn_w_gate", [depth, 2, D, DFF]).ap()
    w_up = din("ffn_w_up", [depth, 2, D, DFF]).ap()
    w_down = din("ffn_w_down", [depth, 2, DFF, D]).ap()
    fox_w_in = din("fox_w_in", [max(n_fox, 1), D, FOX_IN]).ap()
    fox_b_f = din("fox_b_f", [max(n_fox, 1), NH]).ap()
    fox_q_g = din("fox_q_g", [max(n_fox, 1), HD]).ap()
    fox_k_g = din("fox_k_g", [max(n_fox, 1), HD]).ap()
    fox_w_out = din("fox_w_out", [max(n_fox, 1), D, D]).ap()
    dil_w_in = din("dil_w_in", [max(n_dil, 1), D, DIL_IN]).ap()
    dil_q_g = din("dil_q_g", [max(n_dil, 1), 3, HD]).ap()
    dil_k_g = din("dil_k_g", [max(n_dil, 1), 3, HD]).ap()
    dil_w_out = din("dil_w_out", [max(n_dil, 1), D, D]).ap()
    cf_in = din("cf", [128, NCF]).ap()
    y_out = nc.dram_tensor("y", [S, D], F32, kind="ExternalOutput").ap()

    def dscr(name, shape, dt):
        if debug_stop is not None and not name.startswith("w"):
            return nc.dram_tensor(name, list(shape), dt, kind="ExternalOutput")
        return nc.dram_tensor(name, list(shape), dt)

    xT_h = dscr("xT_s", [D, S], F32)
    xT = xT_h.ap()
    hT = dscr("hT_s", [D, S], BF16).ap()
    qk_s = [[dscr(f"qk_s{g}_{a}", [D, S], BF16).ap() for a in range(2)] for g in range(3)]
    aug_s = [dscr(f"aug_s{a}", [NH, 6, S], BF16).ap() for a in range(2)]
    oun_o = [dscr(f"oun_o{g}", [D, S], F32).ap() for g in range(3)]
    oun_d_h = [dscr(f"oun_d{g}", [NH, S], F32) for g in range(3)]
    oun_d = [h.ap() for h in oun_d_h]
    cs_s = [dscr(f"cs_s{a}", [128, S], F32).ap() for a in range(2)]
    wg_s = [[dscr(f"wg_s{i}_{s}", [D, DFF], BF16).ap() for s in range(2)] for i in range(depth)]
    wu_s = [[dscr(f"wu_s{i}_{s}", [D, DFF], BF16).ap() for s in range(2)] for i in range(depth)]
    wd_s = [[dscr(f"wd_s{i}_{s}", [DFF, D], BF16).ap() for s in range(2)] for i in range(depth)]
    win_s = [dscr(f"win_s{i}", [D, FOX_IN if i % 2 == 0 else DIL_IN], BF16).ap() for i in range(depth)]
    wout_s = [dscr(f"wout_s{i}", [D, D], BF16).ap() for i in range(depth)]

    k = K(nc)
    top = ExitStack()

    def sb(st, name, shape, dt):
        return st.enter_context(nc.sbuf_tensor(k.name(name), list(shape), dt))

    def ps(st, name, shape=(128, 512), dt=F32):
        return st.enter_context(nc.psum_tensor(k.name(name), list(shape), dt))

    cf = sb(top, "cf", [128, NCF], F32)
    cb = sb(top, "cb", [128, NCF], BF16)
    modT = sb(top, "modT", [128, depth * 72], F32)
    ngT = sb(top, "ngT", [128, depth * 24], F32)
    mA = sb(top, "mA", [128, depth * 24], F32)
    mG = sb(top, "mG", [128, depth * 24], F32)
    t_cf = k.tok("cf", True)
    t_mod = k.tok("mod", True)

    ident = cf[:, C_IDENT:C_IDENT + 128]
    ones_f = cf[:, C_ONES:C_ONES + 128]
    blk_f = cf[:, C_BLK:C_BLK + 128]
    utri_f = cf[:, C_UTRI:C_UTRI + 128]
    eps_col = cf[:, C_EPS:C_EPS + 1]

    conv_tok = {}

    def conv(key, dst, src, rows, cols):
        t = conv_tok.setdefault(key, k.tok("cv" + key, True))
        a = src.rearrange("(p a) c -> p (a c)", p=128)
        b = dst.rearrange("(p a) c -> p (a c)", p=128)
        n = (rows // 128) * cols
        step = 8192
        for o in range(0, n, step):
            e = min(n, o + step)
            k.dma("pool", out=b[:, o:e], in_=a[:, o:e], st=t)
            t.w = (t.dsem, k.cnt[t.dsem])

    k.dma("sp", out=cf[:, :], in_=cf_in[:, :], w=[t_cf], st=t_cf)
    k.dma("pool", out=cb[:, :], in_=cf_in[:, :], w=[t_cf], st=t_cf)
    for i in range(depth):
        j = i // 2
        conv(f"f{i}0", wg_s[i][0], w_gate[i, 0], D, DFF)
        conv(f"f{i}0", wu_s[i][0], w_up[i, 0], D, DFF)
        conv(f"f{i}0", wd_s[i][0], w_down[i, 0], DFF, D)
        if i % 2 == 0:
            conv(f"a{i}", win_s[i], fox_w_in[j], D, FOX_IN)
            conv(f"a{i}", wout_s[i], fox_w_out[j], D, D)
        else:
            conv(f"a{i}", win_s[i], dil_w_in[j], D, DIL_IN)
            conv(f"a{i}", wout_s[i], dil_w_out[j], D, D)
        conv(f"f{i}1", wg_s[i][1], w_gate[i, 1], D, DFF)
        conv(f"f{i}1", wu_s[i][1], w_up[i, 1], D, DFF)
        conv(f"f{i}1", wd_s[i][1], w_down[i, 1], DFF, D)

    def setup_mods():
        with ExitStack() as st:
            craw = sb(st, "craw", [DC, 128], F32)
            cT = sb(st, "cT", [128, DC], F32)
            mb = sb(st, "mb", [72, 128], F32)
            ng = sb(st, "ng", [24, 128], F32)
            wbuf = [sb(st, "mw", [128, DC, 512], F32) for _ in range(2)]
            pmod = ps(st, "pmod")
            pmisc = ps(st, "pmisc")
            t_c, t_cT, t_mb, t_ng, t_pm, t_pmisc = (k.tok() for _ in range(6))
            t_w = k.toks_n(2)
            k.dma("sp", out=craw[:, :], in_=c_in[:, :], w=[t_c], st=t_c)
            k.op("pe", lambda e: e.matmul(pmisc[:, 0:DC], lhsT=craw[:, :], rhs=ident[0:DC, 0:DC], start=True, stop=True),
                 r=[t_c, t_cf], w=[t_pmisc])
            k.op("act", lambda e: e.activation(out=cT[:, :], in_=pmisc[:, 0:DC], func=AF.Silu), r=[t_pmisc], w=[t_cT])
            for i in range(depth):
                k.dma("sp", out=mb[:, :], in_=mod_b[i * 72:(i + 1) * 72, :], w=[t_mb], st=t_mb)
                k.dma("sp", out=ng[:, :], in_=norm_g[i * 24:(i + 1) * 24, :], w=[t_ng], st=t_ng)
                for jg in range(18):
                    n = i * 18 + jg
                    wb = wbuf[n % 2]
                    k.dma("sp", out=wb[:, :, :],
                          in_=mod_w[i].rearrange("(kc p) f -> p kc f", p=128)[:, :, jg * 512:(jg + 1) * 512],
                          w=[t_w[n % 2]], st=t_w[n % 2])
                    for jj in range(4):
                        col = jg * 4 + jj
                        for kc in range(DC):
                            k.op("pe", lambda e: e.matmul(pmod[:, col:col + 1], lhsT=wb[:, kc, jj * 128:(jj + 1) * 128],
                                                          rhs=cT[:, kc:kc + 1], start=(kc == 0), stop=(kc == DC - 1)),
                                 r=[t_w[n % 2], t_cT], w=[t_pm], inc=(kc == DC - 1))
                k.op("pe", lambda e: e.matmul(pmisc[:, 0:72], lhsT=mb[:, :], rhs=ident[0:72, 0:72], start=True, stop=True),
                     r=[t_mb, t_cf], w=[t_pmisc])
                k.op("act", lambda e: e.activation(out=modT[:, i * 72:(i + 1) * 72], in_=pmisc[:, 0:72], func=AF.Identity),
                     r=[t_pmisc], w=[t_mod])
                k.op("dve", lambda e: e.tensor_tensor(out=modT[:, i * 72:(i + 1) * 72], in0=modT[:, i * 72:(i + 1) * 72],
                                                      in1=pmod[:, 0:72], op=ALU.add), r=[t_pm, t_mod], w=[t_mod])
                k.op("pe", lambda e: e.matmul(pmisc[:, 0:24], lhsT=ng[:, :], rhs=ident[0:24, 0:24], start=True, stop=True),
                     r=[t_ng, t_cf], w=[t_pmisc])
                k.op("act", lambda e: e.activation(out=ngT[:, i * 24:(i + 1) * 24], in_=pmisc[:, 0:24], func=AF.Identity),
                     r=[t_pmisc], w=[t_mod])
                for s in range(3):
                    base = i * 72 + s * 24
                    o = i * 24 + s * 8
                    k.op("dve", lambda e: e.scalar_tensor_tensor(out=mA[:, o:o + 8], in0=modT[:, base + 8:base + 16], scalar=1.0,
                                                                 in1=ngT[:, o:o + 8], op0=ALU.add, op1=ALU.mult),
                         r=[t_mod], w=[t_mod])
                    gsc = 1.0 if s == 1 else 0.5
                    k.op("dve", lambda e: e.tensor_scalar(out=mG[:, o:o + 8], in0=modT[:, base + 16:base + 24], scalar1=gsc, scalar2=None,
                                                          op0=ALU.mult), r=[t_mod], w=[t_mod])
        k.barrier()

    def shift_col(i, s, dc):
        c = i * 72 + s * 24 + dc
        return modT[:, c:c + 1]

    def a_col(i, s, dc):
        c = i * 24 + s * 8 + dc
        return mA[:, c:c + 1]

    def g_col(i, s, dc):
        c = i * 24 + s * 8 + dc
        return mG[:, c:c + 1]

    def transpose_in():
        with ExitStack() as st:
            xin = [sb(st, "xin", [128, D], F32) for _ in range(2)]
            xst = [sb(st, "xst", [128, DC, 512], F32) for _ in range(2)]
            pt = [ps(st, "ptr") for _ in range(4)]
            t_xin = k.toks_n(2)
            t_xst = k.toks_n(2)
            t_pt = k.toks_n(4)
            n = 0
            for tb in range(NTB):
                xb = xin[tb % 2]
                k.dma("sp", out=xb[:, :], in_=x_in[tb * 128:(tb + 1) * 128, :], w=[t_xin[tb % 2]], st=t_xin[tb % 2])
                g4 = tb // 4
                sbuf = xst[g4 % 2]
                for dg in range(2):
                    p = pt[n % 4]
                    tp = t_pt[n % 4]
                    for j in range(4):
                        dc = dg * 4 + j
                        k.op("pe", lambda e: e.matmul(p[:, j * 128:(j + 1) * 128], lhsT=xb[:, dc * 128:(dc + 1) * 128], rhs=ident,
                                                      start=True, stop=True), r=[t_xin[tb % 2], t_cf], w=[tp], inc=(j == 3))
                    eng = "act" if n % 2 == 0 else "dve"
                    dst = sbuf[:, dg * 4:(dg + 1) * 4, (tb % 4) * 128:(tb % 4 + 1) * 128]
                    src = p[:, :].rearrange("p (j t) -> p j t", j=4)
                    if eng == "act":
                        k.op("act", lambda e: e.activation(out=dst, in_=src, func=AF.Identity), r=[tp], w=[t_xst[g4 % 2]])
                    else:
                        k.op("dve", lambda e: e.tensor_copy(out=dst, in_=src), r=[tp], w=[t_xst[g4 % 2]])
                    n += 1
                if tb % 4 == 3:
                    k.dma("pool", out=xT.rearrange("(dc p) s -> p dc s", p=128)[:, :, g4 * 512:(g4 + 1) * 512], in_=sbuf[:, :, :],
                          r=[t_xst[g4 % 2]], st=t_xst[g4 % 2])
        k.barrier()

    def transpose_out():
        with ExitStack() as st:
            xt = [sb(st, "xo", [128, DC, 512], F32) for _ in range(2)]
            yst = [sb(st, "yst", [128, D], F32) for _ in range(2)]
            pt = [ps(st, "pto") for _ in range(4)]
            t_xt = k.toks_n(2)
            t_y = k.toks_n(2)
            t_pt = k.toks_n(4)
            n = 0
            for g4 in range(S // 512):
                xb = xt[g4 % 2]
                k.dma("sp", out=xb[:, :, :], in_=xT.rearrange("(dc p) s -> p dc s", p=128)[:, :, g4 * 512:(g4 + 1) * 512],
                      w=[t_xt[g4 % 2]], st=t_xt[g4 % 2])
                for b4 in range(4):
                    tb = g4 * 4 + b4
                    yb = yst[tb % 2]
                    for dg in range(2):
                        p = pt[n % 4]
                        tp = t_pt[n % 4]
                        for j in range(4):
                            dc = dg * 4 + j
                            k.op("pe", lambda e: e.matmul(p[:, j * 128:(j + 1) * 128], lhsT=xb[:, dc, b4 * 128:(b4 + 1) * 128], rhs=ident,
                                                          start=True, stop=True), r=[t_xt[g4 % 2], t_cf], w=[tp], inc=(j == 3))
                        dst = yb[:, dg * 512:(dg + 1) * 512]
                        if n % 2 == 0:
                            k.op("act", lambda e: e.activation(out=dst, in_=p[:, :], func=AF.Identity), r=[tp], w=[t_y[tb % 2]])
                        else:
                            k.op("dve", lambda e: e.tensor_copy(out=dst, in_=p[:, :]), r=[tp], w=[t_y[tb % 2]])
                        n += 1
                    k.dma("pool", out=y_out[tb * 128:(tb + 1) * 128, :], in_=yb[:, :], r=[t_y[tb % 2]], st=t_y[tb % 2])
        k.barrier()

    class NormRes:
        def __init__(self, st, T):
            self.T = T
            self.sq = [sb(st, "sq", [128, 512], F32) for _ in range(3)]
            self.t_sq = k.toks_n(3)
            self.rstd = sb(st, "rstd", [128, T], F32)
            self.t_rstd = k.tok()
            self.tmp = [sb(st, "ntmp", [128, T], F32) for _ in range(2)]
            self.t_tmp = k.toks_n(2)
            self.pss = ps(st, "pss")
            self.t_pss = k.tok()
            self.n = 0

    def norm_tile(nr, xb, t_x, i, s, hdst, t_h):
        T = nr.T
        for hf in range(T // 512):
            for dc in range(DC):
                q = nr.n % 3
                nr.n += 1
                k.op("act", lambda e: e.activation(out=nr.sq[q][:, :], in_=xb[:, dc, hf * 512:(hf + 1) * 512], func=AF.Square),
                     r=[t_x], w=[nr.t_sq[q]])
                k.op("pe", lambda e: e.matmul(nr.pss[:, :], lhsT=ones_f, rhs=nr.sq[q][:, :], start=(dc == 0), stop=(dc == DC - 1)),
                     r=[nr.t_sq[q], t_cf], w=[nr.t_pss])
            rs_ = nr.rstd[:, hf * 512:(hf + 1) * 512]
            k.op("act", lambda e: e.activation(out=rs_, in_=nr.pss[:, :], func=AF.Sqrt, scale=1.0 / D, bias=eps_col),
                 r=[nr.t_pss, t_cf], w=[nr.t_rstd])
            k.op("dve", lambda e: e.reciprocal(out=rs_, in_=rs_), r=[nr.t_rstd], w=[nr.t_rstd])
        for dc in range(DC):
            q = dc % 2
            k.op("dve", lambda e: e.tensor_tensor(out=nr.tmp[q][:, :], in0=xb[:, dc, :], in1=nr.rstd[:, :], op=ALU.mult),
                 r=[t_x, nr.t_rstd], w=[nr.t_tmp[q]])
            k.op("act", lambda e: e.activation(out=hdst(dc), in_=nr.tmp[q][:, :], func=AF.Identity, scale=a_col(i, s, dc),
                                               bias=shift_col(i, s, dc)), r=[nr.t_tmp[q], t_mod], w=[t_h])

    def ffn_phase(i, s):
        sl = 0 if s == 0 else 2
        T = 1024 if S % 1024 == 0 else 512
        NH2 = T // 512
        cv = conv_tok[f"f{i}{s}"]
        wgv = wg_s[i][s].rearrange("(kc p) f -> p kc f", p=128)
        wuv = wu_s[i][s].rearrange("(kc p) f -> p kc f", p=128)
        wdv = wd_s[i][s].rearrange("(fc p) d -> p fc d", p=128)
        xTv = xT.rearrange("(dc p) s -> p dc s", p=128)
        with ExitStack() as st:
            xt = [sb(st, "fx", [128, DC, T], F32) for _ in range(2)]
            t_x = k.toks_n(2)
            nr = NormRes(st, T)
            hT_sb = sb(st, "fh", [128, DC, T], BF16)
            t_h = k.tok()
            aT = sb(st, "fa", [128, FC, T], BF16)
            t_a = k.tok()
            NWB = 3
            wg = [sb(st, "fwg", [128, DC, 256], BF16) for _ in range(NWB)]
            wu = [sb(st, "fwu", [128, DC, 256], BF16) for _ in range(NWB)]
            t_wg = k.toks_n(NWB)
            t_wu = k.toks_n(NWB)
            wd = [sb(st, "fwd", [128, FC, 128], BF16) for _ in range(2)]
            t_wd = k.toks_n(2)
            sg = [sb(st, "fsg", [128, 512], F32) for _ in range(2)]
            t_sg = k.toks_n(2)
            pg = [ps(st, "pg") for _ in range(2)]
            pu = [ps(st, "pu") for _ in range(2)]
            t_pg = k.toks_n(2)
            t_pu = k.toks_n(2)
            po = [ps(st, "po") for _ in range(2)]
            t_po = k.toks_n(2)
            NT = S // T
            NFG = FC // 2
            wcount = [0]

            def load_w(fg):
                q = wcount[0] % NWB
                wcount[0] += 1
                k.dma("sp", out=wg[q][:, :, :], in_=wgv[:, :, fg * 256:(fg + 1) * 256], r=[cv], w=[t_wg[q]], st=t_wg[q])
                k.dma("sp", out=wu[q][:, :, :], in_=wuv[:, :, fg * 256:(fg + 1) * 256], r=[cv], w=[t_wu[q]], st=t_wu[q])
                return q

            dcount = [0]

            def load_wd(dc):
                q = dcount[0] % 2
                dcount[0] += 1
                k.dma("sp", out=wd[q][:, :, :], in_=wdv[:, :, dc * 128:(dc + 1) * 128], r=[cv], w=[t_wd[q]], st=t_wd[q])
                return q

            k.dma("sp", out=xt[0][:, :, :], in_=xTv[:, :, 0:T], w=[t_x[0]], st=t_x[0])
            n1 = 0
            n2 = 0
            for t in range(NT):
                xb = xt[t % 2]
                tx = t_x[t % 2]
                wq = [load_w(0), load_w(1)]
                if t + 1 < NT:
                    k.dma("sp", out=xt[(t + 1) % 2][:, :, :], in_=xTv[:, :, (t + 1) * T:(t + 2) * T], w=[t_x[(t + 1) % 2]],
                          st=t_x[(t + 1) % 2])
                norm_tile(nr, xb, tx, i, sl, lambda dc: hT_sb[:, dc, :], t_h)
                for fg in range(NFG):
                    if fg + 2 < NFG:
                        wq.append(load_w(fg + 2))
                    q = wq[fg]
                    for fl in range(2):
                        fc = fg * 2 + fl
                        for hf in range(NH2):
                            b = n1 % 2
                            n1 += 1
                            cs = slice(hf * 512, (hf + 1) * 512)
                            for kc in range(DC):
                                k.op("pe", lambda e: e.matmul(pg[b][:, :], lhsT=wg[q][:, kc, fl * 128:(fl + 1) * 128], rhs=hT_sb[:, kc, cs],
                                                              start=(kc == 0), stop=(kc == DC - 1)),
                                     r=[t_wg[q], t_h], w=[t_pg[b]], inc=(kc == DC - 1))
                            for kc in range(DC):
                                k.op("pe", lambda e: e.matmul(pu[b][:, :], lhsT=wu[q][:, kc, fl * 128:(fl + 1) * 128], rhs=hT_sb[:, kc, cs],
                                                              start=(kc == 0), stop=(kc == DC - 1)),
                                     r=[t_wu[q], t_h], w=[t_pu[b]], inc=(kc == DC - 1))
                            k.op("act", lambda e: e.activation(out=sg[b][:, :], in_=pg[b][:, :], func=AF.Silu), r=[t_pg[b]], w=[t_sg[b]])
                            k.op("dve", lambda e: e.tensor_tensor(out=aT[:, fc, cs], in0=sg[b][:, :], in1=pu[b][:, :], op=ALU.mult),
                                 r=[t_sg[b], t_pu[b]], w=[t_a])
                dq = [load_wd(0), load_wd(1)]
                for dc in range(DC):
                    q = dq[dc]
                    for hf in range(NH2):
                        b = n2 % 2
                        n2 += 1
                        cs = slice(hf * 512, (hf + 1) * 512)
                        for fc in range(FC):
                            k.op("pe", lambda e: e.matmul(po[b][:, :], lhsT=wd[q][:, fc, :], rhs=aT[:, fc, cs],
                                                          start=(fc == 0), stop=(fc == FC - 1)),
                                 r=[t_wd[q], t_a], w=[t_po[b]], inc=(fc == FC - 1))
                        k.op("dve", lambda e: e.scalar_tensor_tensor(out=xb[:, dc, cs], in0=po[b][:, :], scalar=g_col(i, sl, dc),
                                                                     in1=xb[:, dc, cs], op0=ALU.mult, op1=ALU.add),
                             r=[t_po[b], t_mod, tx], w=[tx])
                    if dc + 2 < DC:
                        dq.append(load_wd(dc + 2))
                k.dma("pool", out=xTv[:, :, t * T:(t + 1) * T], in_=xb[:, :, :], r=[tx], st=tx)
        k.barrier()


    hTv = hT.rearrange("(kc p) s -> p kc s", p=128)
    xTv_g = xT.rearrange("(dc p) s -> p dc s", p=128)
    mdiag_b = cb[:, C_MDIAG:C_MDIAG + 128]
    mneg_b = cb[:, C_MNEG:C_MNEG + 128]
    ident_b = cb[:, C_IDENT:C_IDENT + 128]
    moff_b = cb[:, C_MOFF:C_MOFF + 128]
    perm_b = cb[:, C_PERM:C_PERM + 128]
    ones_col = cf[:, C_ONES:C_ONES + 1]

    def stop_at(name):
        if debug_stop == name:
            raise StopBuild()

    def h_phase(i):
        T = 1024 if S % 1024 == 0 else 512
        with ExitStack() as st:
            xt = [sb(st, "hx", [128, DC, T], F32) for _ in range(2)]
            t_x = k.toks_n(2)
            nr = NormRes(st, T)
            hs = [sb(st, "hh", [128, DC, T], BF16) for _ in range(2)]
            t_hs = k.toks_n(2)
            NT = S // T
            k.dma("sp", out=xt[0][:, :, :], in_=xTv_g[:, :, 0:T], w=[t_x[0]], st=t_x[0])
            for t in range(NT):
                if t + 1 < NT:
                    k.dma("sp", out=xt[(t + 1) % 2][:, :, :], in_=xTv_g[:, :, (t + 1) * T:(t + 2) * T], w=[t_x[(t + 1) % 2]],
                          st=t_x[(t + 1) % 2])
                hb = hs[t % 2]
                norm_tile(nr, xt[t % 2], t_x[t % 2], i, 1, lambda dc: hb[:, dc, :], t_hs[t % 2])
                k.dma("pool", out=hTv[:, :, t * T:(t + 1) * T], in_=hb[:, :, :], r=[t_hs[t % 2]], st=t_hs[t % 2])
        k.barrier()

    def rope_tables():
        TWO_PI = 2.0 * math.pi
        C1 = 6.28125
        C2 = TWO_PI - C1
        W = 2048 if S >= 2048 else S
        with ExitStack() as st:
            pi_ = sb(st, "rp_i", [128, W], I32)
            ang = sb(st, "rp_f", [128, W], F32)
            kf = sb(st, "rp_kf", [128, W], F32)
            m = sb(st, "rp_m", [128, W], F32)
            m2 = sb(st, "rp_m2", [128, W], F32)
            gt = sb(st, "rp_gt", [128, W], F32)
            t_pi, t_ang, t_kf, t_m, t_m2, t_gt = (k.tok() for _ in range(6))

            def wrap(buf, tb):
                k.op("dve", lambda e: e.tensor_scalar(out=gt[:, :], in0=buf[:, :], scalar1=math.pi, scalar2=None, op0=ALU.is_gt),
                     r=[tb], w=[t_gt])
                k.op("dve", lambda e: e.scalar_tensor_tensor(out=buf[:, :], in0=gt[:, :], scalar=-TWO_PI, in1=buf[:, :],
                                                             op0=ALU.mult, op1=ALU.add), r=[t_gt, tb], w=[tb])
                k.op("dve", lambda e: e.tensor_scalar(out=gt[:, :], in0=buf[:, :], scalar1=-math.pi, scalar2=None, op0=ALU.is_lt),
                     r=[tb], w=[t_gt])
                k.op("dve", lambda e: e.scalar_tensor_tensor(out=buf[:, :], in0=gt[:, :], scalar=TWO_PI, in1=buf[:, :],
                                                             op0=ALU.mult, op1=ALU.add), r=[t_gt, tb], w=[tb])
                k.op("dve", lambda e: e.tensor_scalar(out=buf[:, :], in0=buf[:, :], scalar1=-math.pi, scalar2=math.pi,
                                                      op0=ALU.max, op1=ALU.min), r=[tb], w=[tb])

            for hf in range(S // W):
                cs = slice(hf * W, (hf + 1) * W)
                k.dma("sp", out=pi_[:, :], in_=pos_in[0:1, cs].partition_broadcast(128), w=[t_pi], st=t_pi)
                k.op("dve", lambda e: e.tensor_copy(out=ang[:, :], in_=pi_[:, :]), r=[t_pi], w=[t_ang])
                k.op("dve", lambda e: e.tensor_scalar(out=ang[:, :], in0=ang[:, :], scalar1=cf[:, C_ROPE:C_ROPE + 1], scalar2=None,
                                                      op0=ALU.mult), r=[t_ang, t_cf], w=[t_ang])
                k.op("dve", lambda e: e.tensor_scalar(out=kf[:, :], in0=ang[:, :], scalar1=1.0 / TWO_PI, scalar2=None, op0=ALU.mult),
                     r=[t_ang], w=[t_kf])
                k.op("dve", lambda e: e.tensor_copy(out=pi_[:, :], in_=kf[:, :]), r=[t_kf, t_pi], w=[t_pi])
                k.op("dve", lambda e: e.tensor_copy(out=kf[:, :], in_=pi_[:, :]), r=[t_pi], w=[t_kf])
                k.op("dve", lambda e: e.scalar_tensor_tensor(out=m[:, :], in0=kf[:, :], scalar=-C1, in1=ang[:, :], op0=ALU.mult, op1=ALU.add),
                     r=[t_kf, t_ang], w=[t_m])
                k.op("dve", lambda e: e.scalar_tensor_tensor(out=m[:, :], in0=kf[:, :], scalar=-C2, in1=m[:, :], op0=ALU.mult, op1=ALU.add),
                     r=[t_kf, t_m], w=[t_m])
                wrap(m, t_m)
                k.op("dve", lambda e: e.tensor_scalar(out=m2[:, :], in0=m[:, :], scalar1=0.5 * math.pi, scalar2=None, op0=ALU.add),
                     r=[t_m], w=[t_m2])
                wrap(m2, t_m2)
                k.op("act", lambda e: e.activation(out=m[:, :], in_=m[:, :], func=AF.Sin), r=[t_m], w=[t_m])
                k.op("dve", lambda e: e.tensor_scalar(out=m[:, :], in0=m[:, :], scalar1=cf[:, C_ROPE + 1:C_ROPE + 2], scalar2=None,
                                                      op0=ALU.mult), r=[t_m, t_cf], w=[t_m])
                k.dma("pool", out=cs_s[1][:, cs], in_=m[:, :], r=[t_m], st=t_m)
                k.op("act", lambda e: e.activation(out=m2[:, :], in_=m2[:, :], func=AF.Sin), r=[t_m2], w=[t_m2])
                k.dma("pool", out=cs_s[0][:, cs], in_=m2[:, :], r=[t_m2], st=t_m2)
        k.barrier()

    def mixer_phase(i):
        is_fox = (i % 2 == 0)
        j = i // 2
        cv = conv_tok[f"a{i}"]
        winv = win_s[i].rearrange("(kc p) f -> p kc f", p=128)
        dils = [1] if is_fox else [c[1] for c in DIL_CFG]
        NG = len(dils)
        NHALF = S // 2048 if S >= 2048 else 1
        HL = S // NHALF
        h_phase(i)
        stop_at(f"hph{i}")
        for g, d in enumerate(dils):
            nb = S // d // 128
            with ExitStack() as stg_:
                v_all = sb(stg_, "v_all", [128, NTB, NH, HD + 1], BF16)
                t_v = k.toks_n(NTB, "v")
                t_vone = k.tok()
                k.op("pool", lambda e: e.memset(v_all[:, :, :, HD:HD + 1], 1.0), w=[t_vone])
                sp_all = sb(stg_, "sp_all", [128, NTB, NH], F32) if is_fox else None
                t_sp = k.toks_n(NTB, "sp") if is_fox else None
                with ExitStack() as st:
                    qc0 = 0 if is_fox else g * 3 * D
                    wv = sb(st, "wv", [128, DC, D], BF16)
                    t_wv = k.tok()
                    k.dma("sp", out=wv[:, :, :], in_=winv[:, :, qc0 + 2 * D:qc0 + 3 * D], r=[cv], w=[t_wv], st=t_wv)
                    gq = sb(st, "gq", [128, 1], F32)
                    gk = sb(st, "gk", [128, 1], F32)
                    t_g = k.tok()
                    if is_fox:
                        srcq = fox_q_g[j:j + 1, :].rearrange("o d -> d o")
                        srck = fox_k_g[j:j + 1, :].rearrange("o d -> d o")
                    else:
                        srcq = dil_q_g[j, g:g + 1, :].rearrange("o d -> d o")
                        srck = dil_k_g[j, g:g + 1, :].rearrange("o d -> d o")
                    for a in range(2):
                        k.dma("sp", out=gq[a * 64:(a + 1) * 64, :], in_=srcq, w=[t_g], st=t_g)
                        k.dma("sp", out=gk[a * 64:(a + 1) * 64, :], in_=srck, w=[t_g], st=t_g)
                    k.op("dve", lambda e: e.tensor_scalar(out=gq[:, :], in0=gq[:, :], scalar1=0.125, scalar2=None, op0=ALU.mult),
                         r=[t_g], w=[t_g])
                    if is_fox:
                        wf = sb(st, "wf", [128, DC, NH], BF16)
                        bfb = sb(st, "bfb", [128, NH], F32)
                        t_wf = k.tok()
                        k.dma("sp", out=wf[:, :, :], in_=winv[:, :, 3 * D:3 * D + NH], r=[cv], w=[t_wf], st=t_wf)
                        k.dma("sp", out=bfb[:, :], in_=fox_b_f[j:j + 1, :].partition_broadcast(128), w=[t_wf], st=t_wf)
                        zt = [sb(st, "zt", [128, NH], F32) for _ in range(2)]
                        t_zt = k.toks_n(2)
                        pf = ps(st, "pf")
                        t_pf = k.tok()
                    else:
                        ct = sb(st, "ct", [128, HL], F32)
                        stt = sb(st, "stt", [128, HL], F32)
                        t_cs = k.tok()
                        qn = [sb(st, "qn", [128, 512], F32) for _ in range(2)]
                        qnb = [sb(st, "qnb", [128, 512], BF16) for _ in range(2)]
                        t1 = [sb(st, "t1", [128, 512], F32) for _ in range(2)]
                        t2 = [sb(st, "t2", [128, 512], F32) for _ in range(2)]
                        t_qn = k.toks_n(2)
                        t_qnb = k.toks_n(2)
                        t_t1 = k.toks_n(2)
                        t_t2 = k.toks_n(2)
                        pperm = ps(st, "pperm")
                        t_pperm = k.tok()
                    hh = sb(st, "hhalf", [128, DC, HL], BF16)
                    t_hh = k.tok()
                    wqk = [sb(st, "wqk", [128, DC, 128], BF16) for _ in range(3)]
                    t_wqk = k.toks_n(3)
                    stg = [sb(st, "stg", [128, HL], BF16) for _ in range(2)]
                    t_stg = k.toks_n(2)
                    sq = [sb(st, "bsq", [128, 512], F32) for _ in range(2)]
                    t_sq = k.toks_n(2)
                    rs = [sb(st, "brs", [128, 512], F32) for _ in range(2)]
                    t_rs = k.toks_n(2)
                    pq = [ps(st, "pq") for _ in range(2)]
                    t_pq = k.toks_n(2)
                    pssq = [ps(st, "pssq") for _ in range(2)]
                    t_pssq = k.toks_n(2)
                    pv = [ps(st, "pv") for _ in range(2)]
                    t_pvp = k.toks_n(2)
                    nchunk = 0
                    nq = 0
                    for hf in range(NHALF):
                        k.dma("sp", out=hh[:, :, :], in_=hTv[:, :, hf * HL:(hf + 1) * HL], w=[t_hh], st=t_hh)
                        if not is_fox:
                            k.dma("sp", out=ct[:, :], in_=cs_s[0][:, hf * HL:(hf + 1) * HL], w=[t_cs], st=t_cs)
                            k.dma("sp", out=stt[:, :], in_=cs_s[1][:, hf * HL:(hf + 1) * HL], w=[t_cs], st=t_cs)
                        for c in range(16):
                            a = c // 8
                            cc = c % 8
                            col0 = qc0 + a * D + cc * 128
                            wq_ = nchunk % 3
                            sg_ = nchunk % 2
                            nchunk += 1
                            k.dma("sp", out=wqk[wq_][:, :, :], in_=winv[:, :, col0:col0 + 128], r=[cv], w=[t_wqk[wq_]], st=t_wqk[wq_])
                            gain = gq if a == 0 else gk
                            for tt in range(HL // 512):
                                b = nq % 2
                                nq += 1
                                cs = slice(tt * 512, (tt + 1) * 512)
                                for kc in range(DC):
                                    k.op("pe", lambda e: e.matmul(pq[b][:, :], lhsT=wqk[wq_][:, kc, :], rhs=hh[:, kc, cs],
                                                                  start=(kc == 0), stop=(kc == DC - 1)),
                                         r=[t_wqk[wq_], t_hh], w=[t_pq[b]], inc=(kc == DC - 1))
                                k.op("act", lambda e: e.activation(out=sq[b][:, :], in_=pq[b][:, :], func=AF.Square), r=[t_pq[b]], w=[t_sq[b]])
                                k.op("pe", lambda e: e.matmul(pssq[b][:, :], lhsT=blk_f, rhs=sq[b][:, :], start=True, stop=True),
                                     r=[t_sq[b], t_cf], w=[t_pssq[b]])
                                k.op("act", lambda e: e.activation(out=rs[b][:, :], in_=pssq[b][:, :], func=AF.Sqrt, scale=1.0 / HD, bias=eps_col),
                                     r=[t_pssq[b], t_cf], w=[t_rs[b]])
                                k.op("dve", lambda e: e.reciprocal(out=rs[b][:, :], in_=rs[b][:, :]), r=[t_rs[b]], w=[t_rs[b]])
                                if is_fox:
                                    k.op("dve", lambda e: e.scalar_tensor_tensor(out=stg[sg_][:, cs], in0=pq[b][:, :], scalar=gain[:, 0:1],
                                                                                 in1=rs[b][:, :], op0=ALU.mult, op1=ALU.mult),
                                         r=[t_pq[b], t_g, t_rs[b]], w=[t_stg[sg_]])
                                else:
                                    k.op("dve", lambda e: e.scalar_tensor_tensor(out=qn[b][:, :], in0=pq[b][:, :], scalar=gain[:, 0:1],
                                                                                 in1=rs[b][:, :], op0=ALU.mult, op1=ALU.mult),
                                         r=[t_pq[b], t_g, t_rs[b]], w=[t_qn[b]])
                                    k.op("act", lambda e: e.activation(out=qnb[b][:, :], in_=qn[b][:, :], func=AF.Identity),
                                         r=[t_qn[b]], w=[t_qnb[b]])
                                    k.op("pe", lambda e: e.matmul(pperm[:, :], lhsT=perm_b, rhs=qnb[b][:, :], start=True, stop=True),
                                         r=[t_qnb[b], t_cf], w=[t_pperm])
                                    k.op("pool", lambda e: e.tensor_tensor(out=t1[b][:, :], in0=qn[b][:, :], in1=ct[:, cs], op=ALU.mult),
                                         r=[t_qn[b], t_cs], w=[t_t1[b]])
                                    k.op("dve", lambda e: e.tensor_tensor(out=t2[b][:, :], in0=pperm[:, :], in1=stt[:, cs], op=ALU.mult),
                                         r=[t_pperm, t_cs], w=[t_t2[b]])
                                    k.op("dve", lambda e: e.tensor_tensor(out=stg[sg_][:, cs], in0=t1[b][:, :], in1=t2[b][:, :], op=ALU.add),
                                         r=[t_t1[b], t_t2[b]], w=[t_stg[sg_]])
                            k.dma("pool", out=qk_s[g][a][cc * 128:(cc + 1) * 128, hf * HL:(hf + 1) * HL], in_=stg[sg_][:, :],
                                  r=[t_stg[sg_]], st=t_stg[sg_])
                        bph = HL // 128
                        for bi in range(bph):
                            if d * 128 <= HL:
                                spans = HL // (128 * d)
                                sp_i = bi // d if False else None
                            sidx = bi // d
                            r = bi % d
                            jb = hf * (HL // (128 * d)) + sidx
                            blk = r * nb + jb
                            start = sidx * 128 * d + r
                            cols = slice(start, start + 127 * d + 1, d) if d > 1 else slice(start, start + 128)
                            for hv in range(2):
                                for kc in range(DC):
                                    k.op("pe", lambda e: e.matmul(pv[hv][:, :], lhsT=hh[:, kc, cols], rhs=wv[:, kc, hv * 512:(hv + 1) * 512],
                                                                  start=(kc == 0), stop=(kc == DC - 1)),
                                         r=[t_hh, t_wv], w=[t_pvp[hv]], inc=(kc == DC - 1))
                            k.op("act", lambda e: e.activation(out=v_all[:, blk, 0:8, 0:HD], in_=pv[0][:, :].rearrange("p (h e) -> p h e", h=8),
                                                               func=AF.Identity), r=[t_pvp[0]], w=[t_v[blk]])
                            k.op("dve", lambda e: e.tensor_copy(out=v_all[:, blk, 8:16, 0:HD], in_=pv[1][:, :].rearrange("p (h e) -> p h e", h=8)),
                                 r=[t_pvp[1]], w=[t_v[blk]])
                            if is_fox:
                                zb = bi % 2
                                for kc in range(DC):
                                    k.op("pe", lambda e: e.matmul(pf[:, 0:NH], lhsT=hh[:, kc, cols], rhs=wf[:, kc, :],
                                                                  start=(kc == 0), stop=(kc == DC - 1)),
                                         r=[t_hh, t_wf], w=[t_pf], inc=(kc == DC - 1))
                                k.op("dve", lambda e: e.tensor_tensor(out=zt[zb][:, :], in0=pf[:, 0:NH], in1=bfb[:, :], op=ALU.add),
                                     r=[t_pf, t_wf], w=[t_zt[zb]])
                                k.op("act", lambda e: e.activation(out=zt[zb][:, :], in_=zt[zb][:, :], func=AF.Exp, scale=-1.0),
                                     r=[t_zt[zb]], w=[t_zt[zb]])
                                k.op("act", lambda e: e.activation(out=sp_all[:, blk, :], in_=zt[zb][:, :], func=AF.Ln, bias=ones_col),
                                     r=[t_zt[zb], t_cf], w=[t_sp[blk]])
                k.barrier()
                stop_at(f"b1_{i}_{g}")
                if is_fox:
                    fox_cum(sp_all)
                    stop_at(f"cum{i}")
                if is_fox:
                    b2_fox(v_all)
                else:
                    b2_dil(g, d, v_all)
                stop_at(f"b2_{i}_{g}")
        merge_outproj(i, NG)

    def fox_cum(sp_all):
        with ExitStack() as st:
            cum = sb(st, "cum", [NH, S], F32)
            t_cum = k.tok()
            pre = sb(st, "pre", [NH, NTB], F32)
            t_pre = k.tok()
            hif = sb(st, "hif", [NH, S], F32)
            t_hif = k.tok()
            rb = [sb(st, "rb", [NH, S], BF16) for _ in range(2)]
            t_rb = k.toks_n(2)
            oneb = sb(st, "oneb", [NH, S], BF16)
            t_one = k.tok()
            pc = [ps(st, "pc") for _ in range(2)]
            t_pc = k.toks_n(2)
            for m4 in range(NTB // 4):
                b = m4 % 2
                for mm in range(4):
                    m = m4 * 4 + mm
                    k.op("pe", lambda e: e.matmul(pc[b][0:NH, mm * 128:(mm + 1) * 128], lhsT=sp_all[:, m, :], rhs=utri_f, start=True, stop=True),
                         r=[t_cf], w=[t_pc[b]], inc=(mm == 3))
                k.op("act", lambda e: e.activation(out=cum[:, m4 * 512:(m4 + 1) * 512], in_=pc[b][0:NH, :], func=AF.Identity),
                     r=[t_pc[b]], w=[t_cum])
            k.op("dve", lambda e: e.memset(pre[:, 0:1], 0.0), w=[t_pre])
            for m in range(1, NTB):
                k.op("dve", lambda e: e.tensor_tensor(out=pre[:, m:m + 1], in0=pre[:, m - 1:m], in1=cum[:, m * 128 - 1:m * 128], op=ALU.add),
                     r=[t_cum, t_pre], w=[t_pre])
            for m in range(1, NTB):
                k.op("dve", lambda e: e.tensor_scalar(out=cum[:, m * 128:(m + 1) * 128], in0=cum[:, m * 128:(m + 1) * 128],
                                                      scalar1=pre[:, m:m + 1], scalar2=None, op0=ALU.add), r=[t_pre, t_cum], w=[t_cum])
            k.op("pool", lambda e: e.memset(oneb[:, :], 1.0), w=[t_one])
            for row in range(3):
                k.dma("sp", out=aug_s[1][:, row, :], in_=oneb[:, :], r=[t_one], st=t_one)
                k.dma("sp", out=aug_s[0][:, 3 + row, :], in_=oneb[:, :], r=[t_one], st=t_one)
            for part in range(3):
                kb_, qb_ = rb[0], rb[1]
                k.op("dve", lambda e: e.tensor_copy(out=kb_[:, :], in_=cum[:, :]), r=[t_cum], w=[t_rb[0]])
                k.op("act", lambda e: e.activation(out=qb_[:, :], in_=kb_[:, :], func=AF.Identity, scale=-1.0), r=[t_rb[0]], w=[t_rb[1]])
                k.dma("sp", out=aug_s[1][:, 3 + part, :], in_=kb_[:, :], r=[t_rb[0]], st=t_rb[0])
                k.dma("sp", out=aug_s[0][:, part, :], in_=qb_[:, :], r=[t_rb[1]], st=t_rb[1])
                if part < 2:
                    k.op("dve", lambda e: e.tensor_copy(out=hif[:, :], in_=kb_[:, :]), r=[t_rb[0]], w=[t_hif])
                    k.op("dve", lambda e: e.tensor_tensor(out=cum[:, :], in0=cum[:, :], in1=hif[:, :], op=ALU.subtract),
                         r=[t_hif, t_cum], w=[t_cum])
        k.barrier()

    def b2_fox(v_all):
        NQT = S // 512
        with ExitStack() as st:
            qa = [sb(st, "qa", [HD + 6, S], BF16) for _ in range(2)]
            ka = [sb(st, "ka", [HD + 6, S], BF16) for _ in range(2)]
            t_qa = k.toks_n(2)
            t_ka = k.toks_n(2)
            NP = 4
            pt = [sb(st, "pt", [128, 512], BF16) for _ in range(NP)]
            t_pt = k.toks_n(NP)
            ost = [sb(st, "ost", [HD + 1, S], F32) for _ in range(2)]
            t_ost = k.toks_n(2)
            NSP = 4
            sps = [ps(st, "sps") for _ in range(NSP)]
            t_sps = k.toks_n(NSP)
            ops_ = [ps(st, "ops") for _ in range(2)]
            t_ops = k.toks_n(2)

            def load_head(h):
                b = h % 2
                k.dma("sp", out=qa[b][0:HD, :], in_=qk_s[0][0][h * HD:(h + 1) * HD, :], w=[t_qa[b]], st=t_qa[b])
                k.dma("sp", out=qa[b][HD:HD + 6, :], in_=aug_s[0][h], w=[t_qa[b]], st=t_qa[b])
                k.dma("sp", out=ka[b][0:HD, :], in_=qk_s[0][1][h * HD:(h + 1) * HD, :], w=[t_ka[b]], st=t_ka[b])
                k.dma("sp", out=ka[b][HD:HD + 6, :], in_=aug_s[1][h], w=[t_ka[b]], st=t_ka[b])

            load_head(0)
            nstep = 0
            for h in range(NH):
                hb = h % 2
                if h + 1 < NH:
                    load_head(h + 1)
                steps = [(jt, kb) for jt in range(NQT) for kb in range(4 * jt + 4)]
                LA = 2

                def emit_qk(idx):
                    jt, kb = steps[idx]
                    sidx = (nstep + idx) % NSP
                    qlo = max(kb, 4 * jt) * 128
                    W = (4 * jt + 4) * 128 - qlo
                    diag = kb >= 4 * jt
                    k.op("pe", lambda e: e.matmul(sps[sidx][:, 0:W], lhsT=ka[hb][:, kb * 128:(kb + 1) * 128], rhs=qa[hb][:, qlo:qlo + W],
                                                  start=True, stop=not diag), r=[t_ka[hb], t_qa[hb]], w=[t_sps[sidx]], inc=not diag)
                    if diag:
                        k.op("pe", lambda e: e.matmul(sps[sidx][:, 0:128], lhsT=ident_b, rhs=mneg_b, start=False, stop=True),
                             r=[t_cf], w=[t_sps[sidx]])

                for idx in range(min(LA, len(steps))):
                    emit_qk(idx)
                for idx, (jt, kb) in enumerate(steps):
                    if idx + LA < len(steps):
                        emit_qk(idx + LA)
                    sidx = (nstep + idx) % NSP
                    pidx = (nstep + idx) % NP
                    qlo = max(kb, 4 * jt) * 128
                    W = (4 * jt + 4) * 128 - qlo
                    off = qlo - 4 * jt * 128
                    ob = jt % 2
                    k.op("act", lambda e: e.activation(out=pt[pidx][:, 0:W], in_=sps[sidx][:, 0:W], func=AF.Exp), r=[t_sps[sidx]], w=[t_pt[pidx]])
                    last = (kb == 4 * jt + 3)
                    k.op("pe", lambda e: e.matmul(ops_[ob][0:HD + 1, off:off + W], lhsT=v_all[:, kb, h, :], rhs=pt[pidx][:, 0:W],
                                                  start=(kb == 0), stop=last), r=[t_pt[pidx]], w=[t_ops[ob]], inc=last)
                    if last:
                        k.op("dve", lambda e: e.tensor_copy(out=ost[hb][:, jt * 512:(jt + 1) * 512], in_=ops_[ob][0:HD + 1, :]),
                             r=[t_ops[ob]], w=[t_ost[hb]])
                nstep += len(steps)
                k.dma("pool", out=oun_o[0][h * HD:(h + 1) * HD, :], in_=ost[hb][0:HD, :], r=[t_ost[hb]], st=t_ost[hb])
                k.dma("pool", out=oun_d[0][h:h + 1, :], in_=ost[hb][HD:HD + 1, :], r=[t_ost[hb]], st=t_ost[hb])
        k.barrier()

    def b2_dil(g, d, v_all):
        nb = S // d // 128
        DBG = int(os.environ.get("DBG_DIL", "0"))
        with ExitStack() as st:
            qa = [sb(st, "dq", [HD, S], BF16) for _ in range(2)]
            ka = [sb(st, "dk", [HD, S], BF16) for _ in range(2)]
            t_qa = k.toks_n(2)
            t_ka = k.toks_n(2)
            NP = 4
            pt = [sb(st, "dpt", [128, 256], BF16) for _ in range(NP)]
            t_pt = k.toks_n(NP)
            ost = [sb(st, "dost", [HD + 1, S], F32) for _ in range(2)]
            t_ost = k.toks_n(2)
            NSP = 4
            sps = [ps(st, "dsps") for _ in range(NSP)]
            t_sps = k.toks_n(NSP)
            NOS = 4
            ops_ = [ps(st, "dops") for _ in range(NOS)]
            t_os = k.toks_n(NOS)

            def oslot(n):
                c = (n // NOS) % 4
                return ops_[n % NOS][0:HD + 1, c * 128:(c + 1) * 128]

            def load_head(h):
                b = h % 2
                k.dma("sp", out=qa[b][:, :], in_=qk_s[g][0][h * HD:(h + 1) * HD, :], w=[t_qa[b]], st=t_qa[b])
                k.dma("sp", out=ka[b][:, :], in_=qk_s[g][1][h * HD:(h + 1) * HD, :], w=[t_ka[b]], st=t_ka[b])

            def sl(start, cnt):
                return slice(start, start + (cnt - 1) * d + 1, d) if d > 1 else slice(start, start + cnt)

            load_head(0)
            nstep = 0
            nos = 0
            for h in range(NH):
                hb = h % 2
                if h + 1 < NH:
                    load_head(h + 1)
                steps = [(r, jb) for r in range(d) for jb in range(nb)]
                LA = 2

                def emit_qk(idx):
                    r, jb = steps[idx]
                    sidx = (nstep + idx) % NSP
                    cnt = 256 if jb + 1 < nb else 128
                    kst = r + d * 128 * jb
                    k.op("pe", lambda e: e.matmul(sps[sidx][:, 0:cnt], lhsT=ka[hb][:, sl(kst, 128)], rhs=qa[hb][:, sl(kst, cnt)],
                                                  start=True, stop=True), r=[t_ka[hb], t_qa[hb]], w=[t_sps[sidx]])

                for idx in range(min(LA, len(steps))):
                    emit_qk(idx)
                for idx, (r, jb) in enumerate(steps):
                    if idx + LA < len(steps):
                        emit_qk(idx + LA)
                    sidx = (nstep + idx) % NSP
                    pidx = (nstep + idx) % NP
                    cnt = 256 if jb + 1 < nb else 128
                    kst = r + d * 128 * jb
                    blk = r * nb + jb
                    k.op("act", lambda e: e.activation(out=pt[pidx][:, 0:cnt], in_=sps[sidx][:, 0:cnt], func=AF.Exp), r=[t_sps[sidx]], w=[t_pt[pidx]])
                    if DBG != 2:
                        k.op("pool", lambda e: e.tensor_tensor(out=pt[pidx][:, 0:128], in0=pt[pidx][:, 0:128], in1=mdiag_b, op=ALU.mult),
                             r=[t_pt[pidx], t_cf], w=[t_pt[pidx]])
                    if cnt == 256 and DBG != 2:
                        k.op("pool", lambda e: e.tensor_tensor(out=pt[pidx][:, 128:256], in0=pt[pidx][:, 128:256], in1=moff_b, op=ALU.mult),
                             r=[t_pt[pidx], t_cf], w=[t_pt[pidx]])
                    s0 = nos + jb
                    k.op("pe", lambda e: e.matmul(oslot(s0), lhsT=v_all[:, blk, h, :], rhs=pt[pidx][:, 0:128], start=(jb == 0) or DBG == 1, stop=True),
                         r=[t_pt[pidx]], w=[t_os[s0 % NOS]])
                    if DBG != 3:
                        k.op("dve", lambda e: e.tensor_copy(out=ost[hb][:, sl(kst, 128)], in_=oslot(s0)), r=[t_os[s0 % NOS]], w=[t_ost[hb]])
                    if cnt == 256:
                        s1 = nos + jb + 1
                        k.op("pe", lambda e: e.matmul(oslot(s1), lhsT=v_all[:, blk, h, :], rhs=pt[pidx][:, 128:256], start=True, stop=(DBG == 1)),
                             r=[t_pt[pidx]], w=[t_os[s1 % NOS]])
                    if jb == nb - 1:
                        nos += nb
                nstep += len(steps)
                k.dma("pool", out=oun_o[g][h * HD:(h + 1) * HD, :], in_=ost[hb][0:HD, :], r=[t_ost[hb]], st=t_ost[hb])
                k.dma("pool", out=oun_d[g][h:h + 1, :], in_=ost[hb][HD:HD + 1, :], r=[t_ost[hb]], st=t_ost[hb])
        k.barrier()

    def merge_outproj(i, NG):
        cv = conv_tok[f"a{i}"]
        T = 512
        woutv = wout_s[i].rearrange("(c p) d -> p c d", p=128)
        with ExitStack() as st:
            wo = sb(st, "wo", [128, DC, D], BF16)
            t_wo = k.tok()
            k.dma("sp", out=wo[:, :, :], in_=woutv[:, :, :], r=[cv], w=[t_wo], st=t_wo)
            xt = [sb(st, "mx", [128, DC, T], F32) for _ in range(2)]
            t_x = k.toks_n(2)
            on = [[sb(st, "on", [128, T], F32) for _ in range(NG)] for _ in range(2)]
            dn = [[sb(st, "dn", [128, T], F32) for _ in range(NG)] for _ in range(2)]
            t_on = k.toks_n(2)
            oT = sb(st, "oT", [128, DC, T], BF16)
            t_oT = k.tok()
            py = [ps(st, "py") for _ in range(2)]
            t_py = k.toks_n(2)
            NT = S // T
            n = 0

            def load_c(t, c, b):
                cs = slice(t * T, (t + 1) * T)
                for g in range(NG):
                    k.dma("sp", out=on[b][g][:, :], in_=oun_o[g][c * 128:(c + 1) * 128, cs], w=[t_on[b]], st=t_on[b])
                    for a in range(2):
                        k.dma("sp", out=dn[b][g][a * 64:(a + 1) * 64, :], in_=oun_d[g][2 * c + a:2 * c + a + 1, cs].partition_broadcast(64),
                              w=[t_on[b]], st=t_on[b])

            seq = [(t, c) for t in range(NT) for c in range(DC)]
            load_c(seq[0][0], seq[0][1], 0)
            ny = 0
            for t in range(NT):
                xb = xt[t % 2]
                tx = t_x[t % 2]
                k.dma("sp", out=xb[:, :, :], in_=xTv_g[:, :, t * T:(t + 1) * T], w=[tx], st=tx)
                for c in range(DC):
                    b = n % 2
                    if n + 1 < len(seq):
                        load_c(seq[n + 1][0], seq[n + 1][1], (n + 1) % 2)
                    n += 1
                    for g in range(1, NG):
                        k.op("dve", lambda e: e.tensor_tensor(out=on[b][0][:, :], in0=on[b][0][:, :], in1=on[b][g][:, :], op=ALU.add),
                             r=[t_on[b]], w=[t_on[b]])
                        k.op("pool", lambda e: e.tensor_tensor(out=dn[b][0][:, :], in0=dn[b][0][:, :], in1=dn[b][g][:, :], op=ALU.add),
                             r=[t_on[b]], w=[t_on[b]])
                    k.op("dve", lambda e: e.reciprocal(out=dn[b][0][:, :], in_=dn[b][0][:, :]), r=[t_on[b]], w=[t_on[b]])
                    k.op("dve", lambda e: e.tensor_tensor(out=oT[:, c, :], in0=on[b][0][:, :], in1=dn[b][0][:, :], op=ALU.mult),
                         r=[t_on[b]], w=[t_oT])
                for dc in range(DC):
                    b = ny % 2
                    ny += 1
                    for c in range(DC):
                        k.op("pe", lambda e: e.matmul(py[b][:, :], lhsT=wo[:, c, dc * 128:(dc + 1) * 128], rhs=oT[:, c, :],
                                                      start=(c == 0), stop=(c == DC - 1)), r=[t_wo, t_oT], w=[t_py[b]], inc=(c == DC - 1))
                    k.op("dve", lambda e: e.scalar_tensor_tensor(out=xb[:, dc, :], in0=py[b][:, :], scalar=g_col(i, 1, dc),
                                                                 in1=xb[:, dc, :], op0=ALU.mult, op1=ALU.add),
                         r=[t_py[b], t_mod, tx], w=[tx])
                k.dma("pool", out=xTv_g[:, :, t * T:(t + 1) * T], in_=xb[:, :, :], r=[tx], st=tx)
        k.barrier()

    def program():
        if debug_stop == "conv":
            k.barrier()
            return
        setup_mods()
        if debug_stop == "mods":
            return
        transpose_in()
        if debug_stop == "tin":
            transpose_out()
            return
        if depth >= 2:
            rope_tables()
            stop_at("rope")
        for i in range(depth):
            ffn_phase(i, 0)
            if debug_stop == f"ffn{i}0":
                break
            mixer_phase(i)
            if debug_stop == f"mix{i}":
                break
            ffn_phase(i, 1)
        transpose_out()

    stopped = False
    try:
        program()
    except StopBuild:
        stopped = True
    for key in sorted(k.bgkeys):
        if k.seen["sp"].get(key, 0) < k.cnt[key]:
            nc.sync.wait_ge(k.sems[key], k.cnt[key])
    if not stopped:
        top.close()
    bad = k.check()
    if bad:
        raise RuntimeError(f"semaphore protocol deadlock: {bad}")
    return nc


_CACHE = {}


def _prep_inputs(inputs, b, S, depth, cfc):
    n_fox = (depth + 1) // 2
    n_dil = depth // 2
    f = lambda a: np.ascontiguousarray(a, dtype=np.float32)
    m = {
        "x": f(inputs["x"][b, :S]),
        "c": f(inputs["c"][b]).reshape(DC, 128),
        "positions": np.ascontiguousarray(inputs["positions"][b, :S], dtype=np.int32).reshape(1, S),
        "mod_w": f(inputs["mod_w"][:depth]),
        "mod_b": f(inputs["mod_b"][:depth]).reshape(depth * 72, 128),
        "norm_g": f(inputs["norm_g"][:depth]).reshape(depth * 24, 128),
        "ffn_w_gate": f(inputs["ffn_w_gate"][:depth]),
        "ffn_w_up": f(inputs["ffn_w_up"][:depth]),
        "ffn_w_down": f(inputs["ffn_w_down"][:depth]),
        "fox_w_in": f(inputs["fox_w_in"][:max(n_fox, 1)]),
        "fox_b_f": f(inputs["fox_b_f"][:max(n_fox, 1)]),
        "fox_q_g": f(inputs["fox_q_g"][:max(n_fox, 1)]),
        "fox_k_g": f(inputs["fox_k_g"][:max(n_fox, 1)]),
        "fox_w_out": f(inputs["fox_w_out"][:max(n_fox, 1)]),
        "dil_w_in": f(inputs["dil_w_in"][:max(n_dil, 1)]),
        "dil_q_g": f(inputs["dil_q_g"][:max(n_dil, 1)]),
        "dil_k_g": f(inputs["dil_k_g"][:max(n_dil, 1)]),
        "dil_w_out": f(inputs["dil_w_out"][:max(n_dil, 1)]),
        "cf": cfc,
    }
    return m


def run(inputs, S=4096, depth=4, n_cores=8, debug_stop=None, trace=False):
    key = (S, depth, debug_stop)
    if key not in _CACHE:
        _CACHE[key] = build(S, depth, debug_stop)
    nc = _CACHE[key]
    cfc = make_consts()
    in_maps = [_prep_inputs(inputs, b, S, depth, cfc) for b in range(n_cores)]
    res = run_bass_kernel_spmd(nc, in_maps, core_ids=list(range(n_cores)), **({"trace": True} if trace else {}))
    out = np.stack([np.asarray(r["y"], dtype=np.float32) for r in res.results], axis=0)
    return out, res


def kernel(**inputs):
    out, _ = run(inputs)
    return out
```

```python
import math
from contextlib import ExitStack

import numpy as np
import concourse.bass as bass
import concourse.mybir as mybir
from concourse.bass_utils import run_bass_kernel_spmd

F32 = mybir.dt.float32
BF16 = mybir.dt.bfloat16
I32 = mybir.dt.int32
AF = mybir.ActivationFunctionType
ALU = mybir.AluOpType

D = 1024
DC = 8
HD = 64
NH = 16
DFF = 2816
FC = 22
EPS = 1e-6
FOX_IN = 3 * D + NH
DIL_IN = 9 * D
DIL_CFG = ((128, 1), (512, 4), (2048, 16))
ROPE_THETA = 500000.0

C_IDENT = 0
C_ONES = 128
C_BLK = 256
C_UTRI = 384
C_MDIAG = 512
C_MOFF = 640
C_PERM = 768
C_ROPE = 896
C_EPS = 898
C_MNEG = 900
C_ESEL = 1028
NCF = 1028 + 1024


def make_consts():
    cf = np.zeros((128, NCF), np.float32)
    cf[:, C_IDENT:C_IDENT + 128] = np.eye(128, dtype=np.float32)
    cf[:, C_ONES:C_ONES + 128] = 1.0
    for a in range(2):
        cf[a * 64:(a + 1) * 64, C_BLK + a * 64:C_BLK + (a + 1) * 64] = 1.0
    k = np.arange(128)[:, None]
    q = np.arange(128)[None, :]
    cf[:, C_UTRI:C_UTRI + 128] = (k <= q)
    cf[:, C_MDIAG:C_MDIAG + 128] = (q >= k)
    cf[:, C_MOFF:C_MOFF + 128] = (q <= k)
    half = 8
    inv_freq = (np.float32(ROPE_THETA) ** (-(np.arange(half, dtype=np.float32) * np.float32(2.0) / np.float32(16.0)))).astype(np.float32)
    for p in range(128):
        d = p % 64
        a = p // 64
        if d < 8:
            cf[a * 64 + d + 8, C_PERM + p] = 1.0
            cf[p, C_ROPE] = inv_freq[d]
            cf[p, C_ROPE + 1] = -1.0
        elif d < 16:
            cf[a * 64 + d - 8, C_PERM + p] = 1.0
            cf[p, C_ROPE] = inv_freq[d - 8]
            cf[p, C_ROPE + 1] = 1.0
    cf[:, C_EPS] = EPS
    cf[:, C_MNEG:C_MNEG + 128] = np.where(q < k, -30000.0, 0.0)
    for c in range(8):
        for m in range(128):
            cf[2 * c + m // 64, C_ESEL + c * 128 + m] = 1.0
    return cf


class StopBuild(Exception):
    pass


class Tok:
    __slots__ = ("w", "r", "dsem", "persist", "name")

    def __init__(self, name="", persist=False):
        self.w = None
        self.r = {}
        self.dsem = None
        self.persist = persist
        self.name = name


class K:
    ENG = ("pe", "act", "dve", "pool", "sp")

    def __init__(self, nc):
        self.nc = nc
        self.e = dict(pe=nc.tensor, act=nc.scalar, dve=nc.vector, pool=nc.gpsimd, sp=nc.sync)
        self.sems = {}
        self.cnt = {}
        self.seen = {e: {} for e in self.ENG}
        self.toks = []
        self.bgkeys = set()
        self.uid = 0
        self.free_dsems = []
        self.log = {e: [] for e in self.ENG}
        self.phase_names = []
        for e in self.ENG:
            self._mk(e)
        self._mk("bar")

    def _mk(self, key):
        self.sems[key] = self.nc.alloc_semaphore(name="s_" + key)
        self.cnt[key] = 0

    def tok(self, name="", persist=False):
        t = Tok(name, persist)
        self.toks.append(t)
        return t

    def toks_n(self, n, name=""):
        return [self.tok(f"{name}{i}") for i in range(n)]

    def name(self, base):
        self.uid += 1
        return f"{base}_{self.uid}"

    def _wait(self, eng, deps):
        for key, val in deps:
            if key == "pe" and eng == "pe":
                continue
            if self.seen[eng].get(key, 0) >= val:
                continue
            self.e[eng].wait_ge(self.sems[key], val)
            self.log[eng].append(("w", key, val))
            self.seen[eng][key] = val

    def check(self):
        val = {key: 0 for key in self.cnt}
        pc = {e: 0 for e in self.ENG}
        progress = True
        while progress:
            progress = False
            for e in self.ENG:
                lg = self.log[e]
                while pc[e] < len(lg):
                    ev = lg[pc[e]]
                    if ev[0] == "w":
                        if val[ev[1]] >= ev[2]:
                            pc[e] += 1
                            progress = True
                        else:
                            break
                    else:
                        val[ev[1]] += ev[2]
                        pc[e] += 1
                        progress = True
        bad = {e: (pc[e], len(self.log[e]), self.log[e][pc[e]], val[self.log[e][pc[e]][1]]) for e in self.ENG if pc[e] < len(self.log[e])}
        return bad

    @staticmethod
    def _deps(r, w):
        d = []
        for b in r:
            if b.w is not None:
                d.append(b.w)
        for b in w:
            if b.w is not None:
                d.append(b.w)
            d.extend(b.r.items())
        return d

    def op(self, eng, fn, r=(), w=(), inc=True):
        self._wait(eng, self._deps(r, w))
        ins = fn(self.e[eng])
        if inc:
            self.cnt[eng] += 1
            ins.then_inc(self.sems[eng], 1)
            self.log[eng].append(("i", eng, 1))
            tag = (eng, self.cnt[eng])
        else:
            tag = (eng, self.cnt[eng] + 1)
        for b in w:
            b.w = tag
            b.r = {}
        for b in r:
            if b.r.get(tag[0], 0) < tag[1]:
                b.r[tag[0]] = tag[1]
        return ins

    def dma(self, q, out, in_, r=(), w=(), st=None, **kw):
        self._wait(q, self._deps(r, w))
        if st.dsem is None:
            if self.free_dsems:
                st.dsem = self.free_dsems.pop()
            else:
                self.uid += 1
                st.dsem = f"d{self.uid}"
                self._mk(st.dsem)
            if st.persist:
                self.bgkeys.add(st.dsem)
        key = st.dsem
        self.cnt[key] += 16
        ins = self.e[q].dma_start(out=out, in_=in_, **kw)
        ins.then_inc(self.sems[key], 16)
        self.log[q].append(("i", key, 16))
        tag = (key, self.cnt[key])
        for b in w:
            b.w = tag
            b.r = {}
        for b in r:
            if b.r.get(key, 0) < tag[1]:
                b.r[key] = tag[1]
        return ins

    def barrier(self, name=""):
        self.phase_names.append(name)
        sp = self.e["sp"]
        for key in list(self.cnt.keys()):
            if key in ("sp", "bar") or key in self.bgkeys:
                continue
            if self.seen["sp"].get(key, 0) < self.cnt[key]:
                sp.wait_ge(self.sems[key], self.cnt[key])
                self.log["sp"].append(("w", key, self.cnt[key]))
                self.seen["sp"][key] = self.cnt[key]
        self.cnt["bar"] += 1
        sp.sem_inc(self.sems["bar"], 1)
        self.log["sp"].append(("i", "bar", 1))
        for e in self.ENG:
            if e != "sp":
                self.e[e].wait_ge(self.sems["bar"], self.cnt["bar"])
                self.log[e].append(("w", "bar", self.cnt["bar"]))
            for key in self.cnt:
                if key in self.bgkeys:
                    continue
                self.seen[e][key] = self.cnt[key]
        keep = []
        for t in self.toks:
            if t.persist:
                keep.append(t)
            else:
                t.w = None
                t.r = {}
                if t.dsem is not None:
                    self.free_dsems.append(t.dsem)
                    t.dsem = None
        self.toks = keep


def build(S=4096, depth=4, debug_stop=None):
    nc = bass.Bass("TRN2", target_bir_lowering=False)
    NTB = S // 128
    n_fox = (depth + 1) // 2
    n_dil = depth // 2

    def din(name, shape, dt=F32):
        return nc.dram_tensor(name, list(shape), dt, kind="ExternalInput")

    x_in = din("x", [S, D]).ap()
    c_in = din("c", [DC, 128]).ap()
    pos_in = din("positions", [1, S], I32).ap()
    mod_w = din("mod_w", [depth, D, 9 * D]).ap()
    mod_b = din("mod_b", [depth * 72, 128]).ap()
    norm_g = din("norm_g", [depth * 24, 128]).ap()
    w_gate = din("ffn_w_gate", [depth, 2, D, DFF]).ap()
    w_up = din("ffn_w_up", [depth, 2, D, DFF]).ap()
    w_down = din("ffn_w_down", [depth, 2, DFF, D]).ap()
    fox_w_in = din("fox_w_in", [max(n_fox, 1), D, FOX_IN]).ap()
    fox_b_f = din("fox_b_f", [max(n_fox, 1), NH]).ap()
    fox_q_g = din("fox_q_g", [max(n_fox, 1), HD]).ap()
    fox_k_g = din("fox_k_g", [max(n_fox, 1), HD]).ap()
    fox_w_out = din("fox_w_out", [max(n_fox, 1), D, D]).ap()
    dil_w_in = din("dil_w_in", [max(n_dil, 1), D, DIL_IN]).ap()
    dil_q_g = din("dil_q_g", [max(n_dil, 1), 3, HD]).ap()
    dil_k_g = din("dil_k_g", [max(n_dil, 1), 3, HD]).ap()
    dil_w_out = din("dil_w_out", [max(n_dil, 1), D, D]).ap()
    cf_in = din("cf", [128, NCF]).ap()
    y_out = nc.dram_tensor("y", [S, D], F32, kind="ExternalOutput").ap()

    def dscr(name, shape, dt):
        if debug_stop is not None and not name.startswith("w"):
            return nc.dram_tensor(name, list(shape), dt, kind="ExternalOutput")
        return nc.dram_tensor(name, list(shape), dt)

    xT_h = dscr("xT_s", [D, S], F32)
    xT = xT_h.ap()
    hT = dscr("hT_s", [D, S], BF16).ap()
    qk_s = [[dscr(f"qk_s{g}_{a}", [D, S], BF16).ap() for a in range(2)] for g in range(3)]
    aug_s = [dscr(f"aug_s{a}", [NH, 6, S], BF16).ap() for a in range(2)]
    oun_o = [dscr(f"oun_o{g}", [D, S], F32).ap() for g in range(3)]
    oun_d_h = [dscr(f"oun_d{g}", [NH, S], F32) for g in range(3)]
    oun_d = [h.ap() for h in oun_d_h]
    cs_s = [dscr(f"cs_s{a}", [128, S], F32).ap() for a in range(2)]
    wg_s = [[dscr(f"wg_s{i}_{s}", [D, DFF], BF16).ap() for s in range(2)] for i in range(depth)]
    wu_s = [[dscr(f"wu_s{i}_{s}", [D, DFF], BF16).ap() for s in range(2)] for i in range(depth)]
    wd_s = [[dscr(f"wd_s{i}_{s}", [DFF, D], BF16).ap() for s in range(2)] for i in range(depth)]
    win_s = [dscr(f"win_s{i}", [D, FOX_IN if i % 2 == 0 else DIL_IN], BF16).ap() for i in range(depth)]
    wout_s = [dscr(f"wout_s{i}", [D, D], BF16).ap() for i in range(depth)]

    k = K(nc)
    top = ExitStack()

    def sb(st, name, shape, dt):
        return st.enter_context(nc.sbuf_tensor(k.name(name), list(shape), dt))

    def ps(st, name, shape=(128, 512), dt=F32):
        return st.enter_context(nc.psum_tensor(k.name(name), list(shape), dt))

    cf = sb(top, "cf", [128, NCF], F32)
    cb = sb(top, "cb", [128, NCF], BF16)
    modT = sb(top, "modT", [128, depth * 72], F32)
    ngT = sb(top, "ngT", [128, depth * 24], F32)
    mA = sb(top, "mA", [128, depth * 24], F32)
    mG = sb(top, "mG", [128, depth * 24], F32)
    t_cf = k.tok("cf", True)
    t_mod = k.tok("mod", True)

    ident = cf[:, C_IDENT:C_IDENT + 128]
    ones_f = cf[:, C_ONES:C_ONES + 128]
    blk_f = cf[:, C_BLK:C_BLK + 128]
    utri_f = cf[:, C_UTRI:C_UTRI + 128]
    eps_col = cf[:, C_EPS:C_EPS + 1]
    ones_b = cb[:, C_ONES:C_ONES + 128]
    blk_b = cb[:, C_BLK:C_BLK + 128]

    conv_tok = {}

    def conv(key, dst, src, rows, cols):
        t = conv_tok.setdefault(key, k.tok("cv" + key, True))
        a = src.rearrange("(p a) c -> p (a c)", p=128)
        b = dst.rearrange("(p a) c -> p (a c)", p=128)
        n = (rows // 128) * cols
        step = 8192
        for o in range(0, n, step):
            e = min(n, o + step)
            k.dma("pool", out=b[:, o:e], in_=a[:, o:e], st=t)
            t.w = (t.dsem, k.cnt[t.dsem])

    k.dma("sp", out=cf[:, :], in_=cf_in[:, :], w=[t_cf], st=t_cf)
    k.dma("pool", out=cb[:, :], in_=cf_in[:, :], w=[t_cf], st=t_cf)
    for i in range(depth):
        j = i // 2
        conv(f"f{i}0", wg_s[i][0], w_gate[i, 0], D, DFF)
        conv(f"f{i}0", wu_s[i][0], w_up[i, 0], D, DFF)
        conv(f"f{i}0", wd_s[i][0], w_down[i, 0], DFF, D)
        if i % 2 == 0:
            conv(f"a{i}", win_s[i], fox_w_in[j], D, FOX_IN)
            conv(f"a{i}", wout_s[i], fox_w_out[j], D, D)
        else:
            conv(f"a{i}", win_s[i], dil_w_in[j], D, DIL_IN)
            conv(f"a{i}", wout_s[i], dil_w_out[j], D, D)
        conv(f"f{i}1", wg_s[i][1], w_gate[i, 1], D, DFF)
        conv(f"f{i}1", wu_s[i][1], w_up[i, 1], D, DFF)
        conv(f"f{i}1", wd_s[i][1], w_down[i, 1], DFF, D)

    def setup_mods():
        with ExitStack() as st:
            craw = sb(st, "craw", [DC, 128], F32)
            cT = sb(st, "cT", [128, DC], F32)
            mb = sb(st, "mb", [72, 128], F32)
            ng = sb(st, "ng", [24, 128], F32)
            wbuf = [sb(st, "mw", [128, DC, 512], F32) for _ in range(2)]
            pmod = ps(st, "pmod")
            pmisc = ps(st, "pmisc")
            prow = [ps(st, "prow") for _ in range(2)]
            t_prow = k.toks_n(2)
            row = sb(st, "mrow", [1, 9 * D], F32)
            t_row = k.tok()
            t_c, t_cT, t_mb, t_ng, t_pm, t_pmisc = (k.tok() for _ in range(6))
            t_w = k.toks_n(2)
            k.dma("sp", out=craw[:, :], in_=c_in[:, :], w=[t_c], st=t_c)
            k.op("pe", lambda e: e.matmul(pmisc[:, 0:DC], lhsT=craw[:, :], rhs=ident[0:DC, 0:DC], start=True, stop=True),
                 r=[t_c, t_cf], w=[t_pmisc])
            k.op("act", lambda e: e.activation(out=cT[:, :], in_=pmisc[:, 0:DC], func=AF.Silu), r=[t_pmisc], w=[t_cT])
            for i in range(depth):
                k.dma("sp", out=mb[:, :], in_=mod_b[i * 72:(i + 1) * 72, :], w=[t_mb], st=t_mb)
                k.dma("sp", out=ng[:, :], in_=norm_g[i * 24:(i + 1) * 24, :], w=[t_ng], st=t_ng)
                for jg in range(18):
                    n = i * 18 + jg
                    wb = wbuf[n % 2]
                    k.dma("sp", out=wb[:, :, :],
                          in_=mod_w[i].rearrange("(kc p) f -> p kc f", p=128)[:, :, jg * 512:(jg + 1) * 512],
                          w=[t_w[n % 2]], st=t_w[n % 2])
                    pr = prow[n % 2]
                    for kc in range(DC):
                        k.op("pe", lambda e: e.matmul(pr[0:1, :], lhsT=cT[:, kc:kc + 1], rhs=wb[:, kc, :],
                                                      start=(kc == 0), stop=(kc == DC - 1)),
                             r=[t_w[n % 2], t_cT], w=[t_prow[n % 2]], inc=(kc == DC - 1))
                    k.op("act", lambda e: e.activation(out=row[0:1, jg * 512:(jg + 1) * 512], in_=pr[0:1, :], func=AF.Identity),
                         r=[t_prow[n % 2]], w=[t_row])
                for col in range(72):
                    k.op("pe", lambda e: e.matmul(pmod[:, col:col + 1], lhsT=row[0:1, col * 128:(col + 1) * 128], rhs=ones_f[0:1, 0:1],
                                                  start=True, stop=True), r=[t_row, t_cf], w=[t_pm], inc=(col == 71))
                k.op("pe", lambda e: e.matmul(pmisc[:, 0:72], lhsT=mb[:, :], rhs=ident[0:72, 0:72], start=True, stop=True),
                     r=[t_mb, t_cf], w=[t_pmisc])
                k.op("act", lambda e: e.activation(out=modT[:, i * 72:(i + 1) * 72], in_=pmisc[:, 0:72], func=AF.Identity),
                     r=[t_pmisc], w=[t_mod])
                k.op("dve", lambda e: e.tensor_tensor(out=modT[:, i * 72:(i + 1) * 72], in0=modT[:, i * 72:(i + 1) * 72],
                                                      in1=pmod[:, 0:72], op=ALU.add), r=[t_pm, t_mod], w=[t_mod])
                k.op("pe", lambda e: e.matmul(pmisc[:, 0:24], lhsT=ng[:, :], rhs=ident[0:24, 0:24], start=True, stop=True),
                     r=[t_ng, t_cf], w=[t_pmisc])
                k.op("act", lambda e: e.activation(out=ngT[:, i * 24:(i + 1) * 24], in_=pmisc[:, 0:24], func=AF.Identity),
                     r=[t_pmisc], w=[t_mod])
                for s in range(3):
                    base = i * 72 + s * 24
                    o = i * 24 + s * 8
                    k.op("dve", lambda e: e.scalar_tensor_tensor(out=mA[:, o:o + 8], in0=modT[:, base + 8:base + 16], scalar=1.0,
                                                                 in1=ngT[:, o:o + 8], op0=ALU.add, op1=ALU.mult),
                         r=[t_mod], w=[t_mod])
                    gsc = 1.0 if s == 1 else 0.5
                    k.op("dve", lambda e: e.tensor_scalar(out=mG[:, o:o + 8], in0=modT[:, base + 16:base + 24], scalar1=gsc, scalar2=None,
                                                          op0=ALU.mult), r=[t_mod], w=[t_mod])
        k.barrier("setup_mods")

    def shift_col(i, s, dc):
        c = i * 72 + s * 24 + dc
        return modT[:, c:c + 1]

    def a_col(i, s, dc):
        c = i * 24 + s * 8 + dc
        return mA[:, c:c + 1]

    def g_col(i, s, dc):
        c = i * 24 + s * 8 + dc
        return mG[:, c:c + 1]

    def transpose_in():
        with ExitStack() as st:
            xin = [sb(st, "xin", [128, D], F32) for _ in range(2)]
            xst = [sb(st, "xst", [128, DC, 512], F32) for _ in range(2)]
            pt = [ps(st, "ptr") for _ in range(4)]
            t_xin = k.toks_n(2)
            t_xst = k.toks_n(2)
            t_pt = k.toks_n(4)
            n = 0
            for tb in range(NTB):
                xb = xin[tb % 2]
                k.dma("sp", out=xb[:, :], in_=x_in[tb * 128:(tb + 1) * 128, :], w=[t_xin[tb % 2]], st=t_xin[tb % 2])
                g4 = tb // 4
                sbuf = xst[g4 % 2]
                for dg in range(2):
                    p = pt[n % 4]
                    tp = t_pt[n % 4]
                    for j in range(4):
                        dc = dg * 4 + j
                        k.op("pe", lambda e: e.matmul(p[:, j * 128:(j + 1) * 128], lhsT=xb[:, dc * 128:(dc + 1) * 128], rhs=ident,
                                                      start=True, stop=True), r=[t_xin[tb % 2], t_cf], w=[tp], inc=(j == 3))
                    eng = "act" if n % 2 == 0 else "dve"
                    dst = sbuf[:, dg * 4:(dg + 1) * 4, (tb % 4) * 128:(tb % 4 + 1) * 128]
                    src = p[:, :].rearrange("p (j t) -> p j t", j=4)
                    if eng == "act":
                        k.op("act", lambda e: e.activation(out=dst, in_=src, func=AF.Identity), r=[tp], w=[t_xst[g4 % 2]])
                    else:
                        k.op("dve", lambda e: e.tensor_copy(out=dst, in_=src), r=[tp], w=[t_xst[g4 % 2]])
                    n += 1
                if tb % 4 == 3:
                    k.dma("pool", out=xT.rearrange("(dc p) s -> p dc s", p=128)[:, :, g4 * 512:(g4 + 1) * 512], in_=sbuf[:, :, :],
                          r=[t_xst[g4 % 2]], st=t_xst[g4 % 2])
        k.barrier("transpose_in")

    def transpose_out():
        with ExitStack() as st:
            xt = [sb(st, "xo", [128, DC, 512], F32) for _ in range(2)]
            yst = [sb(st, "yst", [128, D], F32) for _ in range(2)]
            pt = [ps(st, "pto") for _ in range(4)]
            t_xt = k.toks_n(2)
            t_y = k.toks_n(2)
            t_pt = k.toks_n(4)
            n = 0
            for g4 in range(S // 512):
                xb = xt[g4 % 2]
                k.dma("sp", out=xb[:, :, :], in_=xT.rearrange("(dc p) s -> p dc s", p=128)[:, :, g4 * 512:(g4 + 1) * 512],
                      w=[t_xt[g4 % 2]], st=t_xt[g4 % 2])
                for b4 in range(4):
                    tb = g4 * 4 + b4
                    yb = yst[tb % 2]
                    for dg in range(2):
                        p = pt[n % 4]
                        tp = t_pt[n % 4]
                        for j in range(4):
                            dc = dg * 4 + j
                            k.op("pe", lambda e: e.matmul(p[:, j * 128:(j + 1) * 128], lhsT=xb[:, dc, b4 * 128:(b4 + 1) * 128], rhs=ident,
                                                          start=True, stop=True), r=[t_xt[g4 % 2], t_cf], w=[tp], inc=(j == 3))
                        dst = yb[:, dg * 512:(dg + 1) * 512]
                        if n % 2 == 0:
                            k.op("act", lambda e: e.activation(out=dst, in_=p[:, :], func=AF.Identity), r=[tp], w=[t_y[tb % 2]])
                        else:
                            k.op("dve", lambda e: e.tensor_copy(out=dst, in_=p[:, :]), r=[tp], w=[t_y[tb % 2]])
                        n += 1
                    k.dma("pool", out=y_out[tb * 128:(tb + 1) * 128, :], in_=yb[:, :], r=[t_y[tb % 2]], st=t_y[tb % 2])
        k.barrier("transpose_out")

    class NormRes:
        def __init__(self, st, T):
            self.T = T
            self.sq = [sb(st, "sq", [128, 512], BF16) for _ in range(3)]
            self.t_sq = k.toks_n(3)
            self.rstd = sb(st, "rstd", [128, T], F32)
            self.t_rstd = k.tok()
            self.tmp = [sb(st, "ntmp", [128, T], F32) for _ in range(2)]
            self.t_tmp = k.toks_n(2)
            self.pss = ps(st, "pss")
            self.t_pss = k.tok()
            self.n = 0

    def norm_tile(nr, xb, t_x, i, s, hdst, t_h):
        T = nr.T
        for hf in range(T // 512):
            for dc in range(DC):
                q = nr.n % 3
                nr.n += 1
                k.op("act", lambda e: e.activation(out=nr.sq[q][:, :], in_=xb[:, dc, hf * 512:(hf + 1) * 512], func=AF.Square),
                     r=[t_x], w=[nr.t_sq[q]])
                k.op("pe", lambda e: e.matmul(nr.pss[:, :], lhsT=ones_b, rhs=nr.sq[q][:, :], start=(dc == 0), stop=(dc == DC - 1)),
                     r=[nr.t_sq[q], t_cf], w=[nr.t_pss])
            rs_ = nr.rstd[:, hf * 512:(hf + 1) * 512]
            k.op("act", lambda e: e.activation(out=rs_, in_=nr.pss[:, :], func=AF.Sqrt, scale=1.0 / D, bias=eps_col),
                 r=[nr.t_pss, t_cf], w=[nr.t_rstd])
            k.op("dve", lambda e: e.reciprocal(out=rs_, in_=rs_), r=[nr.t_rstd], w=[nr.t_rstd])
        for dc in range(DC):
            q = dc % 2
            k.op("dve", lambda e: e.tensor_tensor(out=nr.tmp[q][:, :], in0=xb[:, dc, :], in1=nr.rstd[:, :], op=ALU.mult),
                 r=[t_x, nr.t_rstd], w=[nr.t_tmp[q]])
            k.op("act", lambda e: e.activation(out=hdst(dc), in_=nr.tmp[q][:, :], func=AF.Identity, scale=a_col(i, s, dc),
                                               bias=shift_col(i, s, dc)), r=[nr.t_tmp[q], t_mod], w=[t_h])

    def ffn_phase(i, s):
        sl = 0 if s == 0 else 2
        T = 1024 if S % 1024 == 0 else 512
        NH2 = T // 512
        cv = conv_tok[f"f{i}{s}"]
        wgv = wg_s[i][s].rearrange("(kc p) f -> p kc f", p=128)
        wuv = wu_s[i][s].rearrange("(kc p) f -> p kc f", p=128)
        wdv = wd_s[i][s].rearrange("(fc p) d -> p fc d", p=128)
        xTv = xT.rearrange("(dc p) s -> p dc s", p=128)
        with ExitStack() as st:
            xt = [sb(st, "fx", [128, DC, T], F32) for _ in range(2)]
            t_x = k.toks_n(2)
            nr = NormRes(st, T)
            hT_sb = sb(st, "fh", [128, DC, T], BF16)
            t_h = k.tok()
            aT = sb(st, "fa", [128, FC, T], BF16)
            t_a = k.tok()
            NWB = 3
            wg = [sb(st, "fwg", [128, DC, 256], BF16) for _ in range(NWB)]
            wu = [sb(st, "fwu", [128, DC, 256], BF16) for _ in range(NWB)]
            t_wg = k.toks_n(NWB)
            t_wu = k.toks_n(NWB)
            wd = [sb(st, "fwd", [128, FC, 128], BF16) for _ in range(2)]
            t_wd = k.toks_n(2)
            sg = [sb(st, "fsg", [128, 512], F32) for _ in range(2)]
            t_sg = k.toks_n(2)
            pg = [ps(st, "pg") for _ in range(2)]
            pu = [ps(st, "pu") for _ in range(2)]
            t_pg = k.toks_n(2)
            t_pu = k.toks_n(2)
            po = [ps(st, "po") for _ in range(2)]
            t_po = k.toks_n(2)
            NT = S // T
            NFG = FC // 2
            wcount = [0]

            def load_w(fg):
                q = wcount[0] % NWB
                wcount[0] += 1
                k.dma("sp", out=wg[q][:, :, :], in_=wgv[:, :, fg * 256:(fg + 1) * 256], r=[cv], w=[t_wg[q]], st=t_wg[q])
                k.dma("sp", out=wu[q][:, :, :], in_=wuv[:, :, fg * 256:(fg + 1) * 256], r=[cv], w=[t_wu[q]], st=t_wu[q])
                return q

            dcount = [0]

            def load_wd(dc):
                q = dcount[0] % 2
                dcount[0] += 1
                k.dma("sp", out=wd[q][:, :, :], in_=wdv[:, :, dc * 128:(dc + 1) * 128], r=[cv], w=[t_wd[q]], st=t_wd[q])
                return q

            k.dma("sp", out=xt[0][:, :, :], in_=xTv[:, :, 0:T], w=[t_x[0]], st=t_x[0])
            n1 = 0
            n2 = 0
            for t in range(NT):
                xb = xt[t % 2]
                tx = t_x[t % 2]
                wq = [load_w(0), load_w(1)]
                if t + 1 < NT:
                    k.dma("sp", out=xt[(t + 1) % 2][:, :, :], in_=xTv[:, :, (t + 1) * T:(t + 2) * T], w=[t_x[(t + 1) % 2]],
                          st=t_x[(t + 1) % 2])
                norm_tile(nr, xb, tx, i, sl, lambda dc: hT_sb[:, dc, :], t_h)
                for fg in range(NFG):
                    if fg + 2 < NFG:
                        wq.append(load_w(fg + 2))
                    q = wq[fg]
                    for fl in range(2):
                        fc = fg * 2 + fl
                        for hf in range(NH2):
                            b = n1 % 2
                            n1 += 1
                            cs = slice(hf * 512, (hf + 1) * 512)
                            for kc in range(DC):
                                k.op("pe", lambda e: e.matmul(pg[b][:, :], lhsT=wg[q][:, kc, fl * 128:(fl + 1) * 128], rhs=hT_sb[:, kc, cs],
                                                              start=(kc == 0), stop=(kc == DC - 1)),
                                     r=[t_wg[q], t_h], w=[t_pg[b]], inc=(kc == DC - 1))
                            for kc in range(DC):
                                k.op("pe", lambda e: e.matmul(pu[b][:, :], lhsT=wu[q][:, kc, fl * 128:(fl + 1) * 128], rhs=hT_sb[:, kc, cs],
                                                              start=(kc == 0), stop=(kc == DC - 1)),
                                     r=[t_wu[q], t_h], w=[t_pu[b]], inc=(kc == DC - 1))
                            k.op("act", lambda e: e.activation(out=sg[b][:, :], in_=pg[b][:, :], func=AF.Silu), r=[t_pg[b]], w=[t_sg[b]])
                            k.op("dve", lambda e: e.tensor_tensor(out=aT[:, fc, cs], in0=sg[b][:, :], in1=pu[b][:, :], op=ALU.mult),
                                 r=[t_sg[b], t_pu[b]], w=[t_a])
                dq = [load_wd(0), load_wd(1)]
                for dc in range(DC):
                    q = dq[dc]
                    for hf in range(NH2):
                        b = n2 % 2
                        n2 += 1
                        cs = slice(hf * 512, (hf + 1) * 512)
                        for fc in range(FC):
                            k.op("pe", lambda e: e.matmul(po[b][:, :], lhsT=wd[q][:, fc, :], rhs=aT[:, fc, cs],
                                                          start=(fc == 0), stop=(fc == FC - 1)),
                                 r=[t_wd[q], t_a], w=[t_po[b]], inc=(fc == FC - 1))
                        k.op("dve", lambda e: e.scalar_tensor_tensor(out=xb[:, dc, cs], in0=po[b][:, :], scalar=g_col(i, sl, dc),
                                                                     in1=xb[:, dc, cs], op0=ALU.mult, op1=ALU.add),
                             r=[t_po[b], t_mod, tx], w=[tx])
                    if dc + 2 < DC:
                        dq.append(load_wd(dc + 2))
                k.dma("pool", out=xTv[:, :, t * T:(t + 1) * T], in_=xb[:, :, :], r=[tx], st=tx)
        k.barrier("load_wd")


    hTv = hT.rearrange("(kc p) s -> p kc s", p=128)
    xTv_g = xT.rearrange("(dc p) s -> p dc s", p=128)
    mdiag_b = cb[:, C_MDIAG:C_MDIAG + 128]
    mneg_b = cb[:, C_MNEG:C_MNEG + 128]
    ident_b = cb[:, C_IDENT:C_IDENT + 128]
    moff_b = cb[:, C_MOFF:C_MOFF + 128]
    perm_b = cb[:, C_PERM:C_PERM + 128]
    ones_col = cf[:, C_ONES:C_ONES + 1]

    def stop_at(name):
        if debug_stop == name:
            raise StopBuild()

    def h_phase(i):
        T = 1024 if S % 1024 == 0 else 512
        with ExitStack() as st:
            xt = [sb(st, "hx", [128, DC, T], F32) for _ in range(2)]
            t_x = k.toks_n(2)
            nr = NormRes(st, T)
            hs = [sb(st, "hh", [128, DC, T], BF16) for _ in range(2)]
            t_hs = k.toks_n(2)
            NT = S // T
            k.dma("sp", out=xt[0][:, :, :], in_=xTv_g[:, :, 0:T], w=[t_x[0]], st=t_x[0])
            for t in range(NT):
                if t + 1 < NT:
                    k.dma("sp", out=xt[(t + 1) % 2][:, :, :], in_=xTv_g[:, :, (t + 1) * T:(t + 2) * T], w=[t_x[(t + 1) % 2]],
                          st=t_x[(t + 1) % 2])
                hb = hs[t % 2]
                norm_tile(nr, xt[t % 2], t_x[t % 2], i, 1, lambda dc: hb[:, dc, :], t_hs[t % 2])
                k.dma("pool", out=hTv[:, :, t * T:(t + 1) * T], in_=hb[:, :, :], r=[t_hs[t % 2]], st=t_hs[t % 2])
        k.barrier("h_phase")

    def rope_tables():
        TWO_PI = 2.0 * math.pi
        C1 = 6.28125
        C2 = TWO_PI - C1
        W = 2048 if S >= 2048 else S
        with ExitStack() as st:
            pi_ = sb(st, "rp_i", [128, W], I32)
            ang = sb(st, "rp_f", [128, W], F32)
            kf = sb(st, "rp_kf", [128, W], F32)
            m = sb(st, "rp_m", [128, W], F32)
            m2 = sb(st, "rp_m2", [128, W], F32)
            gt = sb(st, "rp_gt", [128, W], F32)
            t_pi, t_ang, t_kf, t_m, t_m2, t_gt = (k.tok() for _ in range(6))

            def wrap(buf, tb):
                k.op("dve", lambda e: e.tensor_scalar(out=gt[:, :], in0=buf[:, :], scalar1=math.pi, scalar2=None, op0=ALU.is_gt),
                     r=[tb], w=[t_gt])
                k.op("dve", lambda e: e.scalar_tensor_tensor(out=buf[:, :], in0=gt[:, :], scalar=-TWO_PI, in1=buf[:, :],
                                                             op0=ALU.mult, op1=ALU.add), r=[t_gt, tb], w=[tb])
                k.op("dve", lambda e: e.tensor_scalar(out=gt[:, :], in0=buf[:, :], scalar1=-math.pi, scalar2=None, op0=ALU.is_lt),
                     r=[tb], w=[t_gt])
                k.op("dve", lambda e: e.scalar_tensor_tensor(out=buf[:, :], in0=gt[:, :], scalar=TWO_PI, in1=buf[:, :],
                                                             op0=ALU.mult, op1=ALU.add), r=[t_gt, tb], w=[tb])
                k.op("dve", lambda e: e.tensor_scalar(out=buf[:, :], in0=buf[:, :], scalar1=-math.pi, scalar2=math.pi,
                                                      op0=ALU.max, op1=ALU.min), r=[tb], w=[tb])

            for hf in range(S // W):
                cs = slice(hf * W, (hf + 1) * W)
                k.dma("sp", out=pi_[:, :], in_=pos_in[0:1, cs].partition_broadcast(128), w=[t_pi], st=t_pi)
                k.op("dve", lambda e: e.tensor_copy(out=ang[:, :], in_=pi_[:, :]), r=[t_pi], w=[t_ang])
                k.op("dve", lambda e: e.tensor_scalar(out=ang[:, :], in0=ang[:, :], scalar1=cf[:, C_ROPE:C_ROPE + 1], scalar2=None,
                                                      op0=ALU.mult), r=[t_ang, t_cf], w=[t_ang])
                k.op("dve", lambda e: e.tensor_scalar(out=kf[:, :], in0=ang[:, :], scalar1=1.0 / TWO_PI, scalar2=None, op0=ALU.mult),
                     r=[t_ang], w=[t_kf])
                k.op("dve", lambda e: e.tensor_copy(out=pi_[:, :], in_=kf[:, :]), r=[t_kf, t_pi], w=[t_pi])
                k.op("dve", lambda e: e.tensor_copy(out=kf[:, :], in_=pi_[:, :]), r=[t_pi], w=[t_kf])
                k.op("dve", lambda e: e.scalar_tensor_tensor(out=m[:, :], in0=kf[:, :], scalar=-C1, in1=ang[:, :], op0=ALU.mult, op1=ALU.add),
                     r=[t_kf, t_ang], w=[t_m])
                k.op("dve", lambda e: e.scalar_tensor_tensor(out=m[:, :], in0=kf[:, :], scalar=-C2, in1=m[:, :], op0=ALU.mult, op1=ALU.add),
                     r=[t_kf, t_m], w=[t_m])
                wrap(m, t_m)
                k.op("dve", lambda e: e.tensor_scalar(out=m2[:, :], in0=m[:, :], scalar1=0.5 * math.pi, scalar2=None, op0=ALU.add),
                     r=[t_m], w=[t_m2])
                wrap(m2, t_m2)
                k.op("act", lambda e: e.activation(out=m[:, :], in_=m[:, :], func=AF.Sin), r=[t_m], w=[t_m])
                k.op("dve", lambda e: e.tensor_scalar(out=m[:, :], in0=m[:, :], scalar1=cf[:, C_ROPE + 1:C_ROPE + 2], scalar2=None,
                                                      op0=ALU.mult), r=[t_m, t_cf], w=[t_m])
                k.dma("pool", out=cs_s[1][:, cs], in_=m[:, :], r=[t_m], st=t_m)
                k.op("act", lambda e: e.activation(out=m2[:, :], in_=m2[:, :], func=AF.Sin), r=[t_m2], w=[t_m2])
                k.dma("pool", out=cs_s[0][:, cs], in_=m2[:, :], r=[t_m2], st=t_m2)
        k.barrier("wrap")

    def mixer_phase(i):
        is_fox = (i % 2 == 0)
        j = i // 2
        cv = conv_tok[f"a{i}"]
        winv = win_s[i].rearrange("(kc p) f -> p kc f", p=128)
        dils = [1] if is_fox else [c[1] for c in DIL_CFG]
        NG = len(dils)
        NHALF = S // 2048 if S >= 2048 else 1
        HL = S // NHALF
        h_phase(i)
        stop_at(f"hph{i}")
        for g, d in enumerate(dils):
            nb = S // d // 128
            with ExitStack() as stg_:
                v_all = sb(stg_, "v_all", [128, NTB, NH, HD + 1], BF16)
                t_v = k.toks_n(NTB, "v")
                t_vone = k.tok()
                k.op("pool", lambda e: e.memset(v_all[:, :, :, HD:HD + 1], 1.0), w=[t_vone])
                sp_all = sb(stg_, "sp_all", [128, NTB, NH], F32) if is_fox else None
                t_sp = k.toks_n(NTB, "sp") if is_fox else None
                with ExitStack() as st:
                    qc0 = 0 if is_fox else g * 3 * D
                    wv = sb(st, "wv", [128, DC, D], BF16)
                    t_wv = k.tok()
                    k.dma("sp", out=wv[:, :, :], in_=winv[:, :, qc0 + 2 * D:qc0 + 3 * D], r=[cv], w=[t_wv], st=t_wv)
                    gq = sb(st, "gq", [128, 1], F32)
                    gk = sb(st, "gk", [128, 1], F32)
                    t_g = k.tok()
                    if is_fox:
                        srcq = fox_q_g[j:j + 1, :].rearrange("o d -> d o")
                        srck = fox_k_g[j:j + 1, :].rearrange("o d -> d o")
                    else:
                        srcq = dil_q_g[j, g:g + 1, :].rearrange("o d -> d o")
                        srck = dil_k_g[j, g:g + 1, :].rearrange("o d -> d o")
                    for a in range(2):
                        k.dma("sp", out=gq[a * 64:(a + 1) * 64, :], in_=srcq, w=[t_g], st=t_g)
                        k.dma("sp", out=gk[a * 64:(a + 1) * 64, :], in_=srck, w=[t_g], st=t_g)
                    k.op("dve", lambda e: e.tensor_scalar(out=gq[:, :], in0=gq[:, :], scalar1=0.125, scalar2=None, op0=ALU.mult),
                         r=[t_g], w=[t_g])
                    if is_fox:
                        wf = sb(st, "wf", [128, DC, NH], BF16)
                        bfb = sb(st, "bfb", [128, NH], F32)
                        t_wf = k.tok()
                        k.dma("sp", out=wf[:, :, :], in_=winv[:, :, 3 * D:3 * D + NH], r=[cv], w=[t_wf], st=t_wf)
                        k.dma("sp", out=bfb[:, :], in_=fox_b_f[j:j + 1, :].partition_broadcast(128), w=[t_wf], st=t_wf)
                        zt = [sb(st, "zt", [128, NH], F32) for _ in range(2)]
                        t_zt = k.toks_n(2)
                        pf = ps(st, "pf")
                        t_pf = k.tok()
                    else:
                        ct = sb(st, "ct", [128, HL], F32)
                        stt = sb(st, "stt", [128, HL], F32)
                        t_cs = k.tok()
                        t1 = [sb(st, "t1", [128, 512], F32) for _ in range(2)]
                        t2 = [sb(st, "t2", [128, 512], F32) for _ in range(2)]
                        t_t1 = k.toks_n(2)
                        t_t2 = k.toks_n(2)
                        pperm = ps(st, "pperm")
                        t_pperm = k.tok()
                    hh = sb(st, "hhalf", [128, DC, HL], BF16)
                    t_hh = k.tok()
                    wqk = [sb(st, "wqk", [128, DC, 128], BF16) for _ in range(3)]
                    t_wqk = k.toks_n(3)
                    stg = [sb(st, "stg", [128, HL], BF16) for _ in range(2)]
                    t_stg = k.toks_n(2)
                    NB = 3
                    sq = [sb(st, "bsq", [128, 512], BF16) for _ in range(NB)]
                    t_sq = k.toks_n(NB)
                    rs = [sb(st, "brs", [128, 512], F32) for _ in range(NB)]
                    t_rs = k.toks_n(NB)
                    pq = [ps(st, "pq") for _ in range(NB)]
                    t_pq = k.toks_n(NB)
                    pssq = [ps(st, "pssq") for _ in range(2)]
                    t_pssq = k.toks_n(2)
                    pv = pq[0:2]
                    t_pvp = t_pq[0:2]
                    if not is_fox:
                        qn = [sb(st, "qn", [128, 512], F32) for _ in range(NB)]
                        qnb = [sb(st, "qnb", [128, 512], BF16) for _ in range(NB)]
                        t_qn = k.toks_n(NB)
                        t_qnb = k.toks_n(NB)
                        pperm2 = [pperm, ps(st, "pperm2")]
                        t_pperm2 = [t_pperm, k.tok()]
                    NTT = HL // 512
                    for hf in range(NHALF):
                        k.dma("sp", out=hh[:, :, :], in_=hTv[:, :, hf * HL:(hf + 1) * HL], w=[t_hh], st=t_hh)
                        if not is_fox:
                            k.dma("sp", out=ct[:, :], in_=cs_s[0][:, hf * HL:(hf + 1) * HL], w=[t_cs], st=t_cs)
                            k.dma("sp", out=stt[:, :], in_=cs_s[1][:, hf * HL:(hf + 1) * HL], w=[t_cs], st=t_cs)
                        items = [(c, tt) for c in range(16) for tt in range(NTT)]
                        NI = len(items)

                        def wload(c):
                            a = c // 8
                            cc = c % 8
                            col0 = qc0 + a * D + cc * 128
                            k.dma("sp", out=wqk[c % 3][:, :, :], in_=winv[:, :, col0:col0 + 128], r=[cv], w=[t_wqk[c % 3]], st=t_wqk[c % 3])

                        def stage_a(n):
                            c, tt = items[n]
                            if tt == 0 and c + 1 < 16:
                                wload(c + 1)
                            b = n % NB
                            cs = slice(tt * 512, (tt + 1) * 512)
                            for kc in range(DC):
                                k.op("pe", lambda e: e.matmul(pq[b][:, :], lhsT=wqk[c % 3][:, kc, :], rhs=hh[:, kc, cs],
                                                              start=(kc == 0), stop=(kc == DC - 1)),
                                     r=[t_wqk[c % 3], t_hh], w=[t_pq[b]], inc=(kc == DC - 1))
                            k.op("act", lambda e: e.activation(out=sq[b][:, :], in_=pq[b][:, :], func=AF.Square), r=[t_pq[b]], w=[t_sq[b]])

                        def stage_b(n):
                            c, tt = items[n]
                            b = n % NB
                            b2 = n % 2
                            sg_ = c % 2
                            cs = slice(tt * 512, (tt + 1) * 512)
                            gain = gq if c < 8 else gk
                            k.op("pe", lambda e: e.matmul(pssq[b2][:, :], lhsT=blk_b, rhs=sq[b][:, :], start=True, stop=True),
                                 r=[t_sq[b], t_cf], w=[t_pssq[b2]])
                            k.op("act", lambda e: e.activation(out=rs[b][:, :], in_=pssq[b2][:, :], func=AF.Sqrt, scale=1.0 / HD, bias=eps_col),
                                 r=[t_pssq[b2], t_cf], w=[t_rs[b]])
                            k.op("dve", lambda e: e.reciprocal(out=rs[b][:, :], in_=rs[b][:, :]), r=[t_rs[b]], w=[t_rs[b]])
                            if is_fox:
                                k.op("dve", lambda e: e.scalar_tensor_tensor(out=stg[sg_][:, cs], in0=pq[b][:, :], scalar=gain[:, 0:1],
                                                                             in1=rs[b][:, :], op0=ALU.mult, op1=ALU.mult),
                                     r=[t_pq[b], t_g, t_rs[b]], w=[t_stg[sg_]])
                                if tt == NTT - 1:
                                    store(c)
                            else:
                                k.op("dve", lambda e: e.scalar_tensor_tensor(out=qn[b][:, :], in0=pq[b][:, :], scalar=gain[:, 0:1],
                                                                             in1=rs[b][:, :], op0=ALU.mult, op1=ALU.mult),
                                     r=[t_pq[b], t_g, t_rs[b]], w=[t_qn[b]])
                                k.op("act", lambda e: e.activation(out=qnb[b][:, :], in_=qn[b][:, :], func=AF.Identity),
                                     r=[t_qn[b]], w=[t_qnb[b]])

                        def stage_c(n):
                            c, tt = items[n]
                            b = n % NB
                            b2 = n % 2
                            sg_ = c % 2
                            cs = slice(tt * 512, (tt + 1) * 512)
                            k.op("pe", lambda e: e.matmul(pperm2[b2][:, :], lhsT=perm_b, rhs=qnb[b][:, :], start=True, stop=True),
                                 r=[t_qnb[b], t_cf], w=[t_pperm2[b2]])
                            k.op("pool", lambda e: e.tensor_tensor(out=t1[b2][:, :], in0=qn[b][:, :], in1=ct[:, cs], op=ALU.mult),
                                 r=[t_qn[b], t_cs], w=[t_t1[b2]])
                            k.op("dve", lambda e: e.tensor_tensor(out=t2[b2][:, :], in0=pperm2[b2][:, :], in1=stt[:, cs], op=ALU.mult),
                                 r=[t_pperm2[b2], t_cs], w=[t_t2[b2]])
                            k.op("dve", lambda e: e.tensor_tensor(out=stg[sg_][:, cs], in0=t1[b2][:, :], in1=t2[b2][:, :], op=ALU.add),
                                 r=[t_t1[b2], t_t2[b2]], w=[t_stg[sg_]])
                            if tt == NTT - 1:
                                store(c)

                        def store(c):
                            a = c // 8
                            cc = c % 8
                            sg_ = c % 2
                            k.dma("pool", out=qk_s[g][a][cc * 128:(cc + 1) * 128, hf * HL:(hf + 1) * HL], in_=stg[sg_][:, :],
                                  r=[t_stg[sg_]], st=t_stg[sg_])

                        wload(0)
                        for n in range(NI + 2):
                            if n < NI:
                                stage_a(n)
                            if 0 <= n - 1 < NI:
                                stage_b(n - 1)
                            if (not is_fox) and 0 <= n - 2 < NI:
                                stage_c(n - 2)
                        bph = HL // 128
                        for bi in range(bph):
                            if d * 128 <= HL:
                                spans = HL // (128 * d)
                                sp_i = bi // d if False else None
                            sidx = bi // d
                            r = bi % d
                            jb = hf * (HL // (128 * d)) + sidx
                            blk = r * nb + jb
                            start = sidx * 128 * d + r
                            cols = slice(start, start + 127 * d + 1, d) if d > 1 else slice(start, start + 128)
                            for hv in range(2):
                                for kc in range(DC):
                                    k.op("pe", lambda e: e.matmul(pv[hv][:, :], lhsT=hh[:, kc, cols], rhs=wv[:, kc, hv * 512:(hv + 1) * 512],
                                                                  start=(kc == 0), stop=(kc == DC - 1)),
                                         r=[t_hh, t_wv], w=[t_pvp[hv]], inc=(kc == DC - 1))
                            k.op("act", lambda e: e.activation(out=v_all[:, blk, 0:8, 0:HD], in_=pv[0][:, :].rearrange("p (h e) -> p h e", h=8),
                                                               func=AF.Identity), r=[t_pvp[0]], w=[t_v[blk]])
                            k.op("dve", lambda e: e.tensor_copy(out=v_all[:, blk, 8:16, 0:HD], in_=pv[1][:, :].rearrange("p (h e) -> p h e", h=8)),
                                 r=[t_pvp[1]], w=[t_v[blk]])
                            if is_fox:
                                zb = bi % 2
                                for kc in range(DC):
                                    k.op("pe", lambda e: e.matmul(pf[:, 0:NH], lhsT=hh[:, kc, cols], rhs=wf[:, kc, :],
                                                                  start=(kc == 0), stop=(kc == DC - 1)),
                                         r=[t_hh, t_wf], w=[t_pf], inc=(kc == DC - 1))
                                k.op("dve", lambda e: e.tensor_tensor(out=zt[zb][:, :], in0=pf[:, 0:NH], in1=bfb[:, :], op=ALU.add),
                                     r=[t_pf, t_wf], w=[t_zt[zb]])
                                k.op("act", lambda e: e.activation(out=zt[zb][:, :], in_=zt[zb][:, :], func=AF.Exp, scale=-1.0),
                                     r=[t_zt[zb]], w=[t_zt[zb]])
                                k.op("act", lambda e: e.activation(out=sp_all[:, blk, :], in_=zt[zb][:, :], func=AF.Ln, bias=ones_col),
                                     r=[t_zt[zb], t_cf], w=[t_sp[blk]])
                k.barrier("mixer_phase")
                stop_at(f"b1_{i}_{g}")
                if is_fox:
                    fox_cum(sp_all)
                    stop_at(f"cum{i}")
                if is_fox:
                    b2_fox(v_all)
                else:
                    b2_dil(g, d, v_all)
                stop_at(f"b2_{i}_{g}")
        merge_outproj(i, NG)

    def fox_cum(sp_all):
        with ExitStack() as st:
            cum = sb(st, "cum", [NH, S], F32)
            t_cum = k.tok()
            pre = sb(st, "pre", [NH, NTB], F32)
            t_pre = k.tok()
            hif = sb(st, "hif", [NH, S], F32)
            t_hif = k.tok()
            rb = [sb(st, "rb", [NH, S], BF16) for _ in range(2)]
            t_rb = k.toks_n(2)
            oneb = sb(st, "oneb", [NH, S], BF16)
            t_one = k.tok()
            pc = [ps(st, "pc") for _ in range(2)]
            t_pc = k.toks_n(2)
            for m4 in range(NTB // 4):
                b = m4 % 2
                for mm in range(4):
                    m = m4 * 4 + mm
                    k.op("pe", lambda e: e.matmul(pc[b][0:NH, mm * 128:(mm + 1) * 128], lhsT=sp_all[:, m, :], rhs=utri_f, start=True, stop=True),
                         r=[t_cf], w=[t_pc[b]], inc=(mm == 3))
                k.op("act", lambda e: e.activation(out=cum[:, m4 * 512:(m4 + 1) * 512], in_=pc[b][0:NH, :], func=AF.Identity),
                     r=[t_pc[b]], w=[t_cum])
            k.op("dve", lambda e: e.memset(pre[:, 0:1], 0.0), w=[t_pre])
            for m in range(1, NTB):
                k.op("dve", lambda e: e.tensor_tensor(out=pre[:, m:m + 1], in0=pre[:, m - 1:m], in1=cum[:, m * 128 - 1:m * 128], op=ALU.add),
                     r=[t_cum, t_pre], w=[t_pre])
            for m in range(1, NTB):
                k.op("dve", lambda e: e.tensor_scalar(out=cum[:, m * 128:(m + 1) * 128], in0=cum[:, m * 128:(m + 1) * 128],
                                                      scalar1=pre[:, m:m + 1], scalar2=None, op0=ALU.add), r=[t_pre, t_cum], w=[t_cum])
            k.op("pool", lambda e: e.memset(oneb[:, :], 1.0), w=[t_one])
            for row in range(3):
                k.dma("sp", out=aug_s[1][:, row, :], in_=oneb[:, :], r=[t_one], st=t_one)
                k.dma("sp", out=aug_s[0][:, 3 + row, :], in_=oneb[:, :], r=[t_one], st=t_one)
            for part in range(3):
                kb_, qb_ = rb[0], rb[1]
                k.op("dve", lambda e: e.tensor_copy(out=kb_[:, :], in_=cum[:, :]), r=[t_cum], w=[t_rb[0]])
                k.op("act", lambda e: e.activation(out=qb_[:, :], in_=kb_[:, :], func=AF.Identity, scale=-1.0), r=[t_rb[0]], w=[t_rb[1]])
                k.dma("sp", out=aug_s[1][:, 3 + part, :], in_=kb_[:, :], r=[t_rb[0]], st=t_rb[0])
                k.dma("sp", out=aug_s[0][:, part, :], in_=qb_[:, :], r=[t_rb[1]], st=t_rb[1])
                if part < 2:
                    k.op("dve", lambda e: e.tensor_copy(out=hif[:, :], in_=kb_[:, :]), r=[t_rb[0]], w=[t_hif])
                    k.op("dve", lambda e: e.tensor_tensor(out=cum[:, :], in0=cum[:, :], in1=hif[:, :], op=ALU.subtract),
                         r=[t_hif, t_cum], w=[t_cum])
        k.barrier("fox_cum")

    def b2_fox(v_all):
        NQT = S // 512
        with ExitStack() as st:
            qa = [sb(st, "qa", [HD + 6, S], BF16) for _ in range(2)]
            ka = [sb(st, "ka", [HD + 6, S], BF16) for _ in range(2)]
            t_qa = k.toks_n(2)
            t_ka = k.toks_n(2)
            NP = 4
            pt = [sb(st, "pt", [128, 512], BF16) for _ in range(NP)]
            t_pt = k.toks_n(NP)
            ost = [sb(st, "ost", [HD + 1, S], F32) for _ in range(2)]
            t_ost = k.toks_n(2)
            NSP = 4
            sps = [ps(st, "sps") for _ in range(NSP)]
            t_sps = k.toks_n(NSP)
            ops_ = [ps(st, "ops") for _ in range(2)]
            t_ops = k.toks_n(2)

            def load_head(h):
                b = h % 2
                k.dma("sp", out=qa[b][0:HD, :], in_=qk_s[0][0][h * HD:(h + 1) * HD, :], w=[t_qa[b]], st=t_qa[b])
                k.dma("sp", out=qa[b][HD:HD + 6, :], in_=aug_s[0][h], w=[t_qa[b]], st=t_qa[b])
                k.dma("sp", out=ka[b][0:HD, :], in_=qk_s[0][1][h * HD:(h + 1) * HD, :], w=[t_ka[b]], st=t_ka[b])
                k.dma("sp", out=ka[b][HD:HD + 6, :], in_=aug_s[1][h], w=[t_ka[b]], st=t_ka[b])

            load_head(0)
            nstep = 0
            for h in range(NH):
                hb = h % 2
                if h + 1 < NH:
                    load_head(h + 1)
                steps = [(jt, kb) for jt in range(NQT) for kb in range(4 * jt + 4)]
                LA = 2

                def emit_qk(idx):
                    jt, kb = steps[idx]
                    sidx = (nstep + idx) % NSP
                    qlo = max(kb, 4 * jt) * 128
                    W = (4 * jt + 4) * 128 - qlo
                    diag = kb >= 4 * jt
                    k.op("pe", lambda e: e.matmul(sps[sidx][:, 0:W], lhsT=ka[hb][:, kb * 128:(kb + 1) * 128], rhs=qa[hb][:, qlo:qlo + W],
                                                  start=True, stop=not diag), r=[t_ka[hb], t_qa[hb]], w=[t_sps[sidx]], inc=not diag)
                    if diag:
                        k.op("pe", lambda e: e.matmul(sps[sidx][:, 0:128], lhsT=ident_b, rhs=mneg_b, start=False, stop=True),
                             r=[t_cf], w=[t_sps[sidx]])

                for idx in range(min(LA, len(steps))):
                    emit_qk(idx)
                for idx, (jt, kb) in enumerate(steps):
                    if idx + LA < len(steps):
                        emit_qk(idx + LA)
                    sidx = (nstep + idx) % NSP
                    pidx = (nstep + idx) % NP
                    qlo = max(kb, 4 * jt) * 128
                    W = (4 * jt + 4) * 128 - qlo
                    off = qlo - 4 * jt * 128
                    ob = jt % 2
                    k.op("act", lambda e: e.activation(out=pt[pidx][:, 0:W], in_=sps[sidx][:, 0:W], func=AF.Exp), r=[t_sps[sidx]], w=[t_pt[pidx]])
                    last = (kb == 4 * jt + 3)
                    k.op("pe", lambda e: e.matmul(ops_[ob][0:HD + 1, off:off + W], lhsT=v_all[:, kb, h, :], rhs=pt[pidx][:, 0:W],
                                                  start=(kb == 0), stop=last), r=[t_pt[pidx]], w=[t_ops[ob]], inc=last)
                    if last:
                        k.op("dve", lambda e: e.tensor_copy(out=ost[hb][:, jt * 512:(jt + 1) * 512], in_=ops_[ob][0:HD + 1, :]),
                             r=[t_ops[ob]], w=[t_ost[hb]])
                nstep += len(steps)
                k.dma("pool", out=oun_o[0][h * HD:(h + 1) * HD, :], in_=ost[hb][0:HD, :], r=[t_ost[hb]], st=t_ost[hb])
                k.dma("pool", out=oun_d[0][h:h + 1, :], in_=ost[hb][HD:HD + 1, :], r=[t_ost[hb]], st=t_ost[hb])
        k.barrier("emit_qk")

    def b2_dil(g, d, v_all):
        nb = S // d // 128
        DBG = 0
        with ExitStack() as st:
            qa = [sb(st, "dq", [HD, S], BF16) for _ in range(2)]
            ka = [sb(st, "dk", [HD, S], BF16) for _ in range(2)]
            t_qa = k.toks_n(2)
            t_ka = k.toks_n(2)
            NP = 4
            pt = [sb(st, "dpt", [128, 256], BF16) for _ in range(NP)]
            t_pt = k.toks_n(NP)
            ost = [sb(st, "dost", [HD + 1, S], F32) for _ in range(2)]
            t_ost = k.toks_n(2)
            NSP = 4
            sps = [ps(st, "dsps") for _ in range(NSP)]
            t_sps = k.toks_n(NSP)
            NOS = 4
            ops_ = [ps(st, "dops") for _ in range(NOS)]
            t_os = k.toks_n(NOS)

            def oslot(n):
                c = (n // NOS) % 4
                return ops_[n % NOS][0:HD + 1, c * 128:(c + 1) * 128]

            def load_head(h):
                b = h % 2
                k.dma("sp", out=qa[b][:, :], in_=qk_s[g][0][h * HD:(h + 1) * HD, :], w=[t_qa[b]], st=t_qa[b])
                k.dma("sp", out=ka[b][:, :], in_=qk_s[g][1][h * HD:(h + 1) * HD, :], w=[t_ka[b]], st=t_ka[b])

            def sl(start, cnt):
                return slice(start, start + (cnt - 1) * d + 1, d) if d > 1 else slice(start, start + cnt)

            load_head(0)
            nstep = 0
            nos = 0
            for h in range(NH):
                hb = h % 2
                if h + 1 < NH:
                    load_head(h + 1)
                steps = [(r, jb) for r in range(d) for jb in range(nb)]
                LA = 2

                def emit_qk(idx):
                    r, jb = steps[idx]
                    sidx = (nstep + idx) % NSP
                    cnt = 256 if jb + 1 < nb else 128
                    kst = r + d * 128 * jb
                    k.op("pe", lambda e: e.matmul(sps[sidx][:, 0:cnt], lhsT=ka[hb][:, sl(kst, 128)], rhs=qa[hb][:, sl(kst, cnt)],
                                                  start=True, stop=True), r=[t_ka[hb], t_qa[hb]], w=[t_sps[sidx]])

                for idx in range(min(LA, len(steps))):
                    emit_qk(idx)
                for idx, (r, jb) in enumerate(steps):
                    if idx + LA < len(steps):
                        emit_qk(idx + LA)
                    sidx = (nstep + idx) % NSP
                    pidx = (nstep + idx) % NP
                    cnt = 256 if jb + 1 < nb else 128
                    kst = r + d * 128 * jb
                    blk = r * nb + jb
                    k.op("act", lambda e: e.activation(out=pt[pidx][:, 0:cnt], in_=sps[sidx][:, 0:cnt], func=AF.Exp), r=[t_sps[sidx]], w=[t_pt[pidx]])
                    k.op("dve", lambda e: e.tensor_tensor(out=pt[pidx][:, 0:cnt], in0=pt[pidx][:, 0:cnt], in1=cb[:, C_MDIAG:C_MDIAG + cnt], op=ALU.mult),
                         r=[t_pt[pidx], t_cf], w=[t_pt[pidx]])
                    s0 = nos + jb
                    k.op("pe", lambda e: e.matmul(oslot(s0), lhsT=v_all[:, blk, h, :], rhs=pt[pidx][:, 0:128], start=(jb == 0) or DBG == 1, stop=True),
                         r=[t_pt[pidx]], w=[t_os[s0 % NOS]])
                    if DBG != 3:
                        k.op("dve", lambda e: e.tensor_copy(out=ost[hb][:, sl(kst, 128)], in_=oslot(s0)), r=[t_os[s0 % NOS]], w=[t_ost[hb]])
                    if cnt == 256:
                        s1 = nos + jb + 1
                        k.op("pe", lambda e: e.matmul(oslot(s1), lhsT=v_all[:, blk, h, :], rhs=pt[pidx][:, 128:256], start=True, stop=(DBG == 1)),
                             r=[t_pt[pidx]], w=[t_os[s1 % NOS]])
                    if jb == nb - 1:
                        nos += nb
                nstep += len(steps)
                k.dma("pool", out=oun_o[g][h * HD:(h + 1) * HD, :], in_=ost[hb][0:HD, :], r=[t_ost[hb]], st=t_ost[hb])
                k.dma("pool", out=oun_d[g][h:h + 1, :], in_=ost[hb][HD:HD + 1, :], r=[t_ost[hb]], st=t_ost[hb])
        k.barrier("emit_qk")

    def merge_outproj(i, NG):
        cv = conv_tok[f"a{i}"]
        T = 512
        woutv = wout_s[i].rearrange("(c p) d -> p c d", p=128)
        with ExitStack() as st:
            wo = sb(st, "wo", [128, DC, D], BF16)
            t_wo = k.tok()
            k.dma("sp", out=wo[:, :, :], in_=woutv[:, :, :], r=[cv], w=[t_wo], st=t_wo)
            xt = [sb(st, "mx", [128, DC, T], F32) for _ in range(2)]
            t_x = k.toks_n(2)
            on = [[sb(st, "on", [128, DC, T], F32) for _ in range(NG)] for _ in range(2)]
            t_on = [k.toks_n(NG) for _ in range(2)]
            dn = [sb(st, "dn", [NH, NG, T], F32) for _ in range(2)]
            t_dn = k.toks_n(2)
            rec = sb(st, "rec", [NH, T], F32)
            t_rec = k.tok()
            osum = [sb(st, "osum", [128, T], F32) for _ in range(2)]
            t_osum = k.toks_n(2)
            oT = sb(st, "oT", [128, DC, T], BF16)
            t_oT = k.tok()
            py = [ps(st, "py") for _ in range(2)]
            t_py = k.toks_n(2)
            pbc = [ps(st, "pbc") for _ in range(2)]
            t_pbc = k.toks_n(2)
            NT = S // T

            def load_t(t):
                b = t % 2
                cs = slice(t * T, (t + 1) * T)
                k.dma("sp", out=xt[b][:, :, :], in_=xTv_g[:, :, cs], w=[t_x[b]], st=t_x[b])
                for g in range(NG):
                    k.dma("sp", out=on[b][g][:, :, :], in_=oun_o[g].rearrange("(c p) s -> p c s", p=128)[:, :, cs],
                          w=[t_on[b][g]], st=t_on[b][g])
                    k.dma("sp", out=dn[b][:, g, :], in_=oun_d[g][:, cs], w=[t_dn[b]], st=t_dn[b])

            load_t(0)
            ny = 0
            nb_ = 0
            for t in range(NT):
                b = t % 2
                xb = xt[b]
                tx = t_x[b]
                if t + 1 < NT:
                    load_t(t + 1)
                for g in range(1, NG):
                    k.op("dve", lambda e: e.tensor_tensor(out=dn[b][:, 0, :], in0=dn[b][:, 0, :], in1=dn[b][:, g, :], op=ALU.add),
                         r=[t_dn[b]], w=[t_dn[b]])
                k.op("dve", lambda e: e.reciprocal(out=rec[:, :], in_=dn[b][:, 0, :]), r=[t_dn[b]], w=[t_rec])
                for c in range(DC):
                    bb = nb_ % 2
                    nb_ += 1
                    k.op("pe", lambda e: e.matmul(pbc[bb][:, :], lhsT=cf[0:NH, C_ESEL + c * 128:C_ESEL + (c + 1) * 128], rhs=rec[:, :],
                                                  start=True, stop=True), r=[t_rec, t_cf], w=[t_pbc[bb]])
                    src = on[b][0][:, c, :]
                    rd = [t_on[b][0]]
                    if NG == 3:
                        k.op("pool", lambda e: e.tensor_tensor(out=osum[bb][:, :], in0=on[b][1][:, c, :], in1=on[b][2][:, c, :], op=ALU.add),
                             r=[t_on[b][1], t_on[b][2]], w=[t_osum[bb]])
                        k.op("dve", lambda e: e.tensor_tensor(out=osum[bb][:, :], in0=osum[bb][:, :], in1=on[b][0][:, c, :], op=ALU.add),
                             r=[t_on[b][0], t_osum[bb]], w=[t_osum[bb]])
                        src = osum[bb][:, :]
                        rd = [t_osum[bb]]
                    k.op("dve", lambda e: e.tensor_tensor(out=oT[:, c, :], in0=src, in1=pbc[bb][:, :], op=ALU.mult),
                         r=rd + [t_pbc[bb]], w=[t_oT])
                for dc in range(DC):
                    yb = ny % 2
                    ny += 1
                    for c in range(DC):
                        k.op("pe", lambda e: e.matmul(py[yb][:, :], lhsT=wo[:, c, dc * 128:(dc + 1) * 128], rhs=oT[:, c, :],
                                                      start=(c == 0), stop=(c == DC - 1)), r=[t_wo, t_oT], w=[t_py[yb]], inc=(c == DC - 1))
                    k.op("dve", lambda e: e.scalar_tensor_tensor(out=xb[:, dc, :], in0=py[yb][:, :], scalar=g_col(i, 1, dc),
                                                                 in1=xb[:, dc, :], op0=ALU.mult, op1=ALU.add),
                         r=[t_py[yb], t_mod, tx], w=[tx])
                k.dma("pool", out=xTv_g[:, :, t * T:(t + 1) * T], in_=xb[:, :, :], r=[tx], st=tx)
        k.barrier("merge_outproj")

    def program():
        if debug_stop == "conv":
            k.barrier("program")
            return
        setup_mods()
        if debug_stop == "mods":
            return
        transpose_in()
        if debug_stop == "tin":
            transpose_out()
            return
        if depth >= 2:
            rope_tables()
            stop_at("rope")
        for i in range(depth):
            ffn_phase(i, 0)
            if debug_stop == f"ffn{i}0":
                break
            mixer_phase(i)
            if debug_stop == f"mix{i}":
                break
            ffn_phase(i, 1)
        transpose_out()

    stopped = False
    try:
        program()
    except StopBuild:
        stopped = True
    for key in sorted(k.bgkeys):
        if k.seen["sp"].get(key, 0) < k.cnt[key]:
            nc.sync.wait_ge(k.sems[key], k.cnt[key])
    if not stopped:
        top.close()
    nc._phase_names = k.phase_names
    bad = k.check()
    if bad:
        raise RuntimeError(f"semaphore protocol deadlock: {bad}")
    return nc


_CACHE = {}


def _prep_inputs(inputs, b, S, depth, cfc):
    n_fox = (depth + 1) // 2
    n_dil = depth // 2
    f = lambda a: np.ascontiguousarray(a, dtype=np.float32)
    m = {
        "x": f(inputs["x"][b, :S]),
        "c": f(inputs["c"][b]).reshape(DC, 128),
        "positions": np.ascontiguousarray(inputs["positions"][b, :S], dtype=np.int32).reshape(1, S),
        "mod_w": f(inputs["mod_w"][:depth]),
        "mod_b": f(inputs["mod_b"][:depth]).reshape(depth * 72, 128),
        "norm_g": f(inputs["norm_g"][:depth]).reshape(depth * 24, 128),
        "ffn_w_gate": f(inputs["ffn_w_gate"][:depth]),
        "ffn_w_up": f(inputs["ffn_w_up"][:depth]),
        "ffn_w_down": f(inputs["ffn_w_down"][:depth]),
        "fox_w_in": f(inputs["fox_w_in"][:max(n_fox, 1)]),
        "fox_b_f": f(inputs["fox_b_f"][:max(n_fox, 1)]),
        "fox_q_g": f(inputs["fox_q_g"][:max(n_fox, 1)]),
        "fox_k_g": f(inputs["fox_k_g"][:max(n_fox, 1)]),
        "fox_w_out": f(inputs["fox_w_out"][:max(n_fox, 1)]),
        "dil_w_in": f(inputs["dil_w_in"][:max(n_dil, 1)]),
        "dil_q_g": f(inputs["dil_q_g"][:max(n_dil, 1)]),
        "dil_k_g": f(inputs["dil_k_g"][:max(n_dil, 1)]),
        "dil_w_out": f(inputs["dil_w_out"][:max(n_dil, 1)]),
        "cf": cfc,
    }
    return m


def run(inputs, S=4096, depth=4, n_cores=8, debug_stop=None, trace=False):
    key = (S, depth, debug_stop)
    if key not in _CACHE:
        _CACHE[key] = build(S, depth, debug_stop)
    nc = _CACHE[key]
    cfc = make_consts()
    in_maps = [_prep_inputs(inputs, b, S, depth, cfc) for b in range(n_cores)]
    res = run_bass_kernel_spmd(nc, in_maps, core_ids=list(range(n_cores)), **({"trace": True} if trace else {}))
    out = np.stack([np.asarray(r["y"], dtype=np.float32) for r in res.results], axis=0)
    return out, res


def kernel(**inputs):
    out, _ = run(inputs)
    return out
```

```python
import math
from contextlib import ExitStack

import numpy as np
import concourse.bass as bass
import concourse.mybir as mybir
from concourse.bass_utils import run_bass_kernel_spmd

F32 = mybir.dt.float32
BF16 = mybir.dt.bfloat16
I32 = mybir.dt.int32
AF = mybir.ActivationFunctionType
ALU = mybir.AluOpType

D = 1024
DC = 8
HD = 64
NH = 16
DFF = 2816
FC = 22
EPS = 1e-6
FOX_IN = 3 * D + NH
DIL_IN = 9 * D
DIL_CFG = ((128, 1), (512, 4), (2048, 16))
ROPE_THETA = 500000.0

C_IDENT = 0
C_ONES = 128
C_BLK = 256
C_UTRI = 384
C_MDIAG = 512
C_MOFF = 640
C_PERM = 768
C_ROPE = 896
C_EPS = 898
C_MNEG = 900
C_ESEL = 1028
NCF = 1028 + 1024


def make_consts():
    cf = np.zeros((128, NCF), np.float32)
    cf[:, C_IDENT:C_IDENT + 128] = np.eye(128, dtype=np.float32)
    cf[:, C_ONES:C_ONES + 128] = 1.0
    for a in range(2):
        cf[a * 64:(a + 1) * 64, C_BLK + a * 64:C_BLK + (a + 1) * 64] = 1.0
    k = np.arange(128)[:, None]
    q = np.arange(128)[None, :]
    cf[:, C_UTRI:C_UTRI + 128] = (k <= q)
    cf[:, C_MDIAG:C_MDIAG + 128] = (q >= k)
    cf[:, C_MOFF:C_MOFF + 128] = (q <= k)
    half = 8
    inv_freq = (np.float32(ROPE_THETA) ** (-(np.arange(half, dtype=np.float32) * np.float32(2.0) / np.float32(16.0)))).astype(np.float32)
    for p in range(128):
        d = p % 64
        a = p // 64
        if d < 8:
            cf[a * 64 + d + 8, C_PERM + p] = 1.0
            cf[p, C_ROPE] = inv_freq[d]
            cf[p, C_ROPE + 1] = -1.0
        elif d < 16:
            cf[a * 64 + d - 8, C_PERM + p] = 1.0
            cf[p, C_ROPE] = inv_freq[d - 8]
            cf[p, C_ROPE + 1] = 1.0
    cf[:, C_EPS] = EPS
    cf[:, C_MNEG:C_MNEG + 128] = np.where(q < k, -30000.0, 0.0)
    for c in range(8):
        for m in range(128):
            cf[2 * c + m // 64, C_ESEL + c * 128 + m] = 1.0
    return cf


class StopBuild(Exception):
    pass


class Tok:
    __slots__ = ("w", "r", "dsem", "persist", "name")

    def __init__(self, name="", persist=False):
        self.w = None
        self.r = {}
        self.dsem = None
        self.persist = persist
        self.name = name


class K:
    ENG = ("pe", "act", "dve", "pool", "sp")

    def __init__(self, nc):
        self.nc = nc
        self.e = dict(pe=nc.tensor, act=nc.scalar, dve=nc.vector, pool=nc.gpsimd, sp=nc.sync)
        self.sems = {}
        self.cnt = {}
        self.seen = {e: {} for e in self.ENG}
        self.toks = []
        self.bgkeys = set()
        self.uid = 0
        self.free_dsems = []
        self.log = {e: [] for e in self.ENG}
        self.phase_names = []
        for e in self.ENG:
            self._mk(e)
        self._mk("bar")

    def _mk(self, key):
        self.sems[key] = self.nc.alloc_semaphore(name="s_" + key)
        self.cnt[key] = 0

    def tok(self, name="", persist=False):
        t = Tok(name, persist)
        self.toks.append(t)
        return t

    def toks_n(self, n, name=""):
        return [self.tok(f"{name}{i}") for i in range(n)]

    def name(self, base):
        self.uid += 1
        return f"{base}_{self.uid}"

    def _wait(self, eng, deps):
        for key, val in deps:
            if key == "pe" and eng == "pe":
                continue
            if self.seen[eng].get(key, 0) >= val:
                continue
            self.e[eng].wait_ge(self.sems[key], val)
            self.log[eng].append(("w", key, val))
            self.seen[eng][key] = val

    def check(self):
        val = {key: 0 for key in self.cnt}
        pc = {e: 0 for e in self.ENG}
        progress = True
        while progress:
            progress = False
            for e in self.ENG:
                lg = self.log[e]
                while pc[e] < len(lg):
                    ev = lg[pc[e]]
                    if ev[0] == "w":
                        if val[ev[1]] >= ev[2]:
                            pc[e] += 1
                            progress = True
                        else:
                            break
                    else:
                        val[ev[1]] += ev[2]
                        pc[e] += 1
                        progress = True
        bad = {e: (pc[e], len(self.log[e]), self.log[e][pc[e]], val[self.log[e][pc[e]][1]]) for e in self.ENG if pc[e] < len(self.log[e])}
        return bad

    @staticmethod
    def _deps(r, w):
        d = []
        for b in r:
            if b.w is not None:
                d.append(b.w)
        for b in w:
            if b.w is not None:
                d.append(b.w)
            d.extend(b.r.items())
        return d

    def op(self, eng, fn, r=(), w=(), inc=True):
        self._wait(eng, self._deps(r, w))
        ins = fn(self.e[eng])
        if inc:
            self.cnt[eng] += 1
            ins.then_inc(self.sems[eng], 1)
            self.log[eng].append(("i", eng, 1))
            tag = (eng, self.cnt[eng])
        else:
            tag = (eng, self.cnt[eng] + 1)
        for b in w:
            b.w = tag
            b.r = {}
        for b in r:
            if b.r.get(tag[0], 0) < tag[1]:
                b.r[tag[0]] = tag[1]
        return ins

    def dma(self, q, out, in_, r=(), w=(), st=None, **kw):
        self._wait(q, self._deps(r, w))
        if st.dsem is None:
            if self.free_dsems:
                st.dsem = self.free_dsems.pop()
            else:
                self.uid += 1
                st.dsem = f"d{self.uid}"
                self._mk(st.dsem)
            if st.persist:
                self.bgkeys.add(st.dsem)
        key = st.dsem
        self.cnt[key] += 16
        ins = self.e[q].dma_start(out=out, in_=in_, **kw)
        ins.then_inc(self.sems[key], 16)
        self.log[q].append(("i", key, 16))
        tag = (key, self.cnt[key])
        for b in w:
            b.w = tag
            b.r = {}
        for b in r:
            if b.r.get(key, 0) < tag[1]:
                b.r[key] = tag[1]
        return ins

    def barrier(self, name=""):
        self.phase_names.append(name)
        sp = self.e["sp"]
        for key in list(self.cnt.keys()):
            if key in ("sp", "bar") or key in self.bgkeys:
                continue
            if self.seen["sp"].get(key, 0) < self.cnt[key]:
                sp.wait_ge(self.sems[key], self.cnt[key])
                self.log["sp"].append(("w", key, self.cnt[key]))
                self.seen["sp"][key] = self.cnt[key]
        self.cnt["bar"] += 1
        sp.sem_inc(self.sems["bar"], 1)
        self.log["sp"].append(("i", "bar", 1))
        for e in self.ENG:
            if e != "sp":
                self.e[e].wait_ge(self.sems["bar"], self.cnt["bar"])
                self.log[e].append(("w", "bar", self.cnt["bar"]))
            for key in self.cnt:
                if key in self.bgkeys:
                    continue
                self.seen[e][key] = self.cnt[key]
        keep = []
        for t in self.toks:
            if t.persist:
                keep.append(t)
            else:
                t.w = None
                t.r = {}
                if t.dsem is not None:
                    self.free_dsems.append(t.dsem)
                    t.dsem = None
        self.toks = keep


def build(S=4096, depth=4, debug_stop=None):
    nc = bass.Bass("TRN2", target_bir_lowering=False)
    NTB = S // 128
    n_fox = (depth + 1) // 2
    n_dil = depth // 2

    def din(name, shape, dt=F32):
        return nc.dram_tensor(name, list(shape), dt, kind="ExternalInput")

    x_in = din("x", [S, D]).ap()
    c_in = din("c", [DC, 128]).ap()
    pos_in = din("positions", [1, S], I32).ap()
    mod_w = din("mod_w", [depth, D, 9 * D]).ap()
    mod_b = din("mod_b", [depth * 72, 128]).ap()
    norm_g = din("norm_g", [depth * 24, 128]).ap()
    w_gate = din("ffn_w_gate", [depth, 2, D, DFF]).ap()
    w_up = din("ffn_w_up", [depth, 2, D, DFF]).ap()
    w_down = din("ffn_w_down", [depth, 2, DFF, D]).ap()
    fox_w_in = din("fox_w_in", [max(n_fox, 1), D, FOX_IN]).ap()
    fox_b_f = din("fox_b_f", [max(n_fox, 1), NH]).ap()
    fox_q_g = din("fox_q_g", [max(n_fox, 1), HD]).ap()
    fox_k_g = din("fox_k_g", [max(n_fox, 1), HD]).ap()
    fox_w_out = din("fox_w_out", [max(n_fox, 1), D, D]).ap()
    dil_w_in = din("dil_w_in", [max(n_dil, 1), D, DIL_IN]).ap()
    dil_q_g = din("dil_q_g", [max(n_dil, 1), 3, HD]).ap()
    dil_k_g = din("dil_k_g", [max(n_dil, 1), 3, HD]).ap()
    dil_w_out = din("dil_w_out", [max(n_dil, 1), D, D]).ap()
    cf_in = din("cf", [128, NCF]).ap()
    y_out = nc.dram_tensor("y", [S, D], F32, kind="ExternalOutput").ap()

    def dscr(name, shape, dt):
        if debug_stop is not None and not name.startswith("w"):
            return nc.dram_tensor(name, list(shape), dt, kind="ExternalOutput")
        return nc.dram_tensor(name, list(shape), dt)

    xT_h = dscr("xT_s", [D, S], F32)
    xT = xT_h.ap()
    hT = dscr("hT_s", [D, S], BF16).ap()
    qk_s = [[dscr(f"qk_s{g}_{a}", [D, S], BF16).ap() for a in range(2)] for g in range(3)]
    aug_s = [dscr(f"aug_s{a}", [NH, 6, S], BF16).ap() for a in range(2)]
    oun_o = [dscr(f"oun_o{g}", [D, S], F32).ap() for g in range(3)]
    oun_d_h = [dscr(f"oun_d{g}", [NH, S], F32) for g in range(3)]
    oun_d = [h.ap() for h in oun_d_h]
    cs_s = [dscr(f"cs_s{a}", [128, S], F32).ap() for a in range(2)]
    wg_s = [[dscr(f"wg_s{i}_{s}", [D, DFF], BF16).ap() for s in range(2)] for i in range(depth)]
    wu_s = [[dscr(f"wu_s{i}_{s}", [D, DFF], BF16).ap() for s in range(2)] for i in range(depth)]
    wd_s = [[dscr(f"wd_s{i}_{s}", [DFF, D], BF16).ap() for s in range(2)] for i in range(depth)]
    win_s = [dscr(f"win_s{i}", [D, FOX_IN if i % 2 == 0 else DIL_IN], BF16).ap() for i in range(depth)]
    wout_s = [dscr(f"wout_s{i}", [D, D], BF16).ap() for i in range(depth)]

    k = K(nc)
    top = ExitStack()

    def sb(st, name, shape, dt):
        return st.enter_context(nc.sbuf_tensor(k.name(name), list(shape), dt))

    def ps(st, name, shape=(128, 512), dt=F32):
        return st.enter_context(nc.psum_tensor(k.name(name), list(shape), dt))

    cf = sb(top, "cf", [128, NCF], F32)
    cb = sb(top, "cb", [128, NCF], BF16)
    modT = sb(top, "modT", [128, depth * 72], F32)
    ngT = sb(top, "ngT", [128, depth * 24], F32)
    mA = sb(top, "mA", [128, depth * 24], F32)
    mG = sb(top, "mG", [128, depth * 24], F32)
    t_cf = k.tok("cf", True)
    t_mod = k.tok("mod", True)

    ident = cf[:, C_IDENT:C_IDENT + 128]
    ones_f = cf[:, C_ONES:C_ONES + 128]
    blk_f = cf[:, C_BLK:C_BLK + 128]
    utri_f = cf[:, C_UTRI:C_UTRI + 128]
    eps_col = cf[:, C_EPS:C_EPS + 1]
    ones_b = cb[:, C_ONES:C_ONES + 128]
    blk_b = cb[:, C_BLK:C_BLK + 128]

    conv_tok = {}

    def conv(key, dst, src, rows, cols):
        t = conv_tok.setdefault(key, k.tok("cv" + key, True))
        a = src.rearrange("(p a) c -> p (a c)", p=128)
        b = dst.rearrange("(p a) c -> p (a c)", p=128)
        n = (rows // 128) * cols
        step = 8192
        for o in range(0, n, step):
            e = min(n, o + step)
            k.dma("pool", out=b[:, o:e], in_=a[:, o:e], st=t)
            t.w = (t.dsem, k.cnt[t.dsem])

    k.dma("sp", out=cf[:, :], in_=cf_in[:, :], w=[t_cf], st=t_cf)
    k.dma("pool", out=cb[:, :], in_=cf_in[:, :], w=[t_cf], st=t_cf)
    for i in range(depth):
        j = i // 2
        conv(f"f{i}0", wg_s[i][0], w_gate[i, 0], D, DFF)
        conv(f"f{i}0", wu_s[i][0], w_up[i, 0], D, DFF)
        conv(f"f{i}0", wd_s[i][0], w_down[i, 0], DFF, D)
        if i % 2 == 0:
            conv(f"a{i}", win_s[i], fox_w_in[j], D, FOX_IN)
            conv(f"a{i}", wout_s[i], fox_w_out[j], D, D)
        else:
            conv(f"a{i}", win_s[i], dil_w_in[j], D, DIL_IN)
            conv(f"a{i}", wout_s[i], dil_w_out[j], D, D)
        conv(f"f{i}1", wg_s[i][1], w_gate[i, 1], D, DFF)
        conv(f"f{i}1", wu_s[i][1], w_up[i, 1], D, DFF)
        conv(f"f{i}1", wd_s[i][1], w_down[i, 1], DFF, D)

    def setup_mods():
        with ExitStack() as st:
            craw = sb(st, "craw", [DC, 128], F32)
            cT = sb(st, "cT", [128, DC], F32)
            mb = sb(st, "mb", [72, 128], F32)
            ng = sb(st, "ng", [24, 128], F32)
            NMW = 4
            wbuf = [sb(st, "mw", [128, DC, 512], F32) for _ in range(NMW)]
            pmod = ps(st, "pmod")
            pmisc = ps(st, "pmisc")
            prow = [ps(st, "prow") for _ in range(2)]
            t_prow = k.toks_n(2)
            row = sb(st, "mrow", [1, 9 * D], F32)
            t_row = k.tok()
            t_c, t_cT, t_mb, t_ng, t_pm, t_pmisc = (k.tok() for _ in range(6))
            t_w = k.toks_n(NMW)
            k.dma("sp", out=craw[:, :], in_=c_in[:, :], w=[t_c], st=t_c)
            k.op("pe", lambda e: e.matmul(pmisc[:, 0:DC], lhsT=craw[:, :], rhs=ident[0:DC, 0:DC], start=True, stop=True),
                 r=[t_c, t_cf], w=[t_pmisc])
            k.op("act", lambda e: e.activation(out=cT[:, :], in_=pmisc[:, 0:DC], func=AF.Silu), r=[t_pmisc], w=[t_cT])
            for i in range(depth):
                k.dma("sp", out=mb[:, :], in_=mod_b[i * 72:(i + 1) * 72, :], w=[t_mb], st=t_mb)
                k.dma("sp", out=ng[:, :], in_=norm_g[i * 24:(i + 1) * 24, :], w=[t_ng], st=t_ng)
                for jg in range(18):
                    n = i * 18 + jg
                    wb = wbuf[n % NMW]
                    k.dma("sp" if n % 2 == 0 else "act", out=wb[:, :, :],
                          in_=mod_w[i].rearrange("(kc p) f -> p kc f", p=128)[:, :, jg * 512:(jg + 1) * 512],
                          w=[t_w[n % NMW]], st=t_w[n % NMW])
                    pr = prow[n % 2]
                    for kc in range(DC):
                        k.op("pe", lambda e: e.matmul(pr[0:1, :], lhsT=cT[:, kc:kc + 1], rhs=wb[:, kc, :],
                                                      start=(kc == 0), stop=(kc == DC - 1)),
                             r=[t_w[n % NMW], t_cT], w=[t_prow[n % 2]], inc=(kc == DC - 1))
                    k.op("act", lambda e: e.activation(out=row[0:1, jg * 512:(jg + 1) * 512], in_=pr[0:1, :], func=AF.Identity),
                         r=[t_prow[n % 2]], w=[t_row])
                for col in range(72):
                    k.op("pe", lambda e: e.matmul(pmod[:, col:col + 1], lhsT=row[0:1, col * 128:(col + 1) * 128], rhs=ones_f[0:1, 0:1],
                                                  start=True, stop=True), r=[t_row, t_cf], w=[t_pm], inc=(col == 71))
                k.op("pe", lambda e: e.matmul(pmisc[:, 0:72], lhsT=mb[:, :], rhs=ident[0:72, 0:72], start=True, stop=True),
                     r=[t_mb, t_cf], w=[t_pmisc])
                k.op("act", lambda e: e.activation(out=modT[:, i * 72:(i + 1) * 72], in_=pmisc[:, 0:72], func=AF.Identity),
                     r=[t_pmisc], w=[t_mod])
                k.op("dve", lambda e: e.tensor_tensor(out=modT[:, i * 72:(i + 1) * 72], in0=modT[:, i * 72:(i + 1) * 72],
                                                      in1=pmod[:, 0:72], op=ALU.add), r=[t_pm, t_mod], w=[t_mod])
                k.op("pe", lambda e: e.matmul(pmisc[:, 0:24], lhsT=ng[:, :], rhs=ident[0:24, 0:24], start=True, stop=True),
                     r=[t_ng, t_cf], w=[t_pmisc])
                k.op("act", lambda e: e.activation(out=ngT[:, i * 24:(i + 1) * 24], in_=pmisc[:, 0:24], func=AF.Identity),
                     r=[t_pmisc], w=[t_mod])
                for s in range(3):
                    base = i * 72 + s * 24
                    o = i * 24 + s * 8
                    k.op("dve", lambda e: e.scalar_tensor_tensor(out=mA[:, o:o + 8], in0=modT[:, base + 8:base + 16], scalar=1.0,
                                                                 in1=ngT[:, o:o + 8], op0=ALU.add, op1=ALU.mult),
                         r=[t_mod], w=[t_mod])
                    gsc = 1.0 if s == 1 else 0.5
                    k.op("dve", lambda e: e.tensor_scalar(out=mG[:, o:o + 8], in0=modT[:, base + 16:base + 24], scalar1=gsc, scalar2=None,
                                                          op0=ALU.mult), r=[t_mod], w=[t_mod])
        k.barrier("setup_mods")

    def shift_col(i, s, dc):
        c = i * 72 + s * 24 + dc
        return modT[:, c:c + 1]

    def a_col(i, s, dc):
        c = i * 24 + s * 8 + dc
        return mA[:, c:c + 1]

    def g_col(i, s, dc):
        c = i * 24 + s * 8 + dc
        return mG[:, c:c + 1]

    def transpose_in():
        with ExitStack() as st:
            xin = [sb(st, "xin", [128, D], F32) for _ in range(2)]
            xst = [sb(st, "xst", [128, DC, 512], F32) for _ in range(2)]
            pt = [ps(st, "ptr") for _ in range(4)]
            t_xin = k.toks_n(2)
            t_xst = k.toks_n(2)
            t_pt = k.toks_n(4)
            n = 0
            for tb in range(NTB):
                xb = xin[tb % 2]
                k.dma("sp", out=xb[:, :], in_=x_in[tb * 128:(tb + 1) * 128, :], w=[t_xin[tb % 2]], st=t_xin[tb % 2])
                g4 = tb // 4
                sbuf = xst[g4 % 2]
                for dg in range(2):
                    p = pt[n % 4]
                    tp = t_pt[n % 4]
                    for j in range(4):
                        dc = dg * 4 + j
                        k.op("pe", lambda e: e.matmul(p[:, j * 128:(j + 1) * 128], lhsT=xb[:, dc * 128:(dc + 1) * 128], rhs=ident,
                                                      start=True, stop=True), r=[t_xin[tb % 2], t_cf], w=[tp], inc=(j == 3))
                    eng = "act" if n % 2 == 0 else "dve"
                    dst = sbuf[:, dg * 4:(dg + 1) * 4, (tb % 4) * 128:(tb % 4 + 1) * 128]
                    src = p[:, :].rearrange("p (j t) -> p j t", j=4)
                    if eng == "act":
                        k.op("act", lambda e: e.activation(out=dst, in_=src, func=AF.Identity), r=[tp], w=[t_xst[g4 % 2]])
                    else:
                        k.op("dve", lambda e: e.tensor_copy(out=dst, in_=src), r=[tp], w=[t_xst[g4 % 2]])
                    n += 1
                if tb % 4 == 3:
                    k.dma("pool", out=xT.rearrange("(dc p) s -> p dc s", p=128)[:, :, g4 * 512:(g4 + 1) * 512], in_=sbuf[:, :, :],
                          r=[t_xst[g4 % 2]], st=t_xst[g4 % 2])
        k.barrier("transpose_in")

    def transpose_out():
        with ExitStack() as st:
            xt = [sb(st, "xo", [128, DC, 512], F32) for _ in range(2)]
            yst = [sb(st, "yst", [128, D], F32) for _ in range(2)]
            pt = [ps(st, "pto") for _ in range(4)]
            t_xt = k.toks_n(2)
            t_y = k.toks_n(2)
            t_pt = k.toks_n(4)
            n = 0
            for g4 in range(S // 512):
                xb = xt[g4 % 2]
                k.dma("sp", out=xb[:, :, :], in_=xT.rearrange("(dc p) s -> p dc s", p=128)[:, :, g4 * 512:(g4 + 1) * 512],
                      w=[t_xt[g4 % 2]], st=t_xt[g4 % 2])
                for b4 in range(4):
                    tb = g4 * 4 + b4
                    yb = yst[tb % 2]
                    for dg in range(2):
                        p = pt[n % 4]
                        tp = t_pt[n % 4]
                        for j in range(4):
                            dc = dg * 4 + j
                            k.op("pe", lambda e: e.matmul(p[:, j * 128:(j + 1) * 128], lhsT=xb[:, dc, b4 * 128:(b4 + 1) * 128], rhs=ident,
                                                          start=True, stop=True), r=[t_xt[g4 % 2], t_cf], w=[tp], inc=(j == 3))
                        dst = yb[:, dg * 512:(dg + 1) * 512]
                        if n % 2 == 0:
                            k.op("act", lambda e: e.activation(out=dst, in_=p[:, :], func=AF.Identity), r=[tp], w=[t_y[tb % 2]])
                        else:
                            k.op("dve", lambda e: e.tensor_copy(out=dst, in_=p[:, :]), r=[tp], w=[t_y[tb % 2]])
                        n += 1
                    k.dma("pool", out=y_out[tb * 128:(tb + 1) * 128, :], in_=yb[:, :], r=[t_y[tb % 2]], st=t_y[tb % 2])
        k.barrier("transpose_out")

    class NormRes:
        def __init__(self, st, T):
            self.T = T
            self.sq = [sb(st, "sq", [128, 512], BF16) for _ in range(3)]
            self.t_sq = k.toks_n(3)
            self.rstd = sb(st, "rstd", [128, T], F32)
            self.t_rstd = k.tok()
            self.tmp = [sb(st, "ntmp", [128, T], F32) for _ in range(2)]
            self.t_tmp = k.toks_n(2)
            self.pss = ps(st, "pss")
            self.t_pss = k.tok()
            self.n = 0

    def norm_tile(nr, xb, t_x, i, s, hdst, t_h):
        T = nr.T
        for hf in range(T // 512):
            for dc in range(DC):
                q = nr.n % 3
                nr.n += 1
                k.op("act", lambda e: e.activation(out=nr.sq[q][:, :], in_=xb[:, dc, hf * 512:(hf + 1) * 512], func=AF.Square),
                     r=[t_x], w=[nr.t_sq[q]])
                k.op("pe", lambda e: e.matmul(nr.pss[:, :], lhsT=ones_b, rhs=nr.sq[q][:, :], start=(dc == 0), stop=(dc == DC - 1)),
                     r=[nr.t_sq[q], t_cf], w=[nr.t_pss])
            rs_ = nr.rstd[:, hf * 512:(hf + 1) * 512]
            k.op("act", lambda e: e.activation(out=rs_, in_=nr.pss[:, :], func=AF.Ln, scale=1.0 / D, bias=eps_col),
                 r=[nr.t_pss, t_cf], w=[nr.t_rstd])
            k.op("act", lambda e: e.activation(out=rs_, in_=rs_, func=AF.Exp, scale=-0.5), r=[nr.t_rstd], w=[nr.t_rstd])
        for dc in range(DC):
            q = dc % 2
            k.op("dve", lambda e: e.tensor_tensor(out=nr.tmp[q][:, :], in0=xb[:, dc, :], in1=nr.rstd[:, :], op=ALU.mult),
                 r=[t_x, nr.t_rstd], w=[nr.t_tmp[q]])
            k.op("act", lambda e: e.activation(out=hdst(dc), in_=nr.tmp[q][:, :], func=AF.Identity, scale=a_col(i, s, dc),
                                               bias=shift_col(i, s, dc)), r=[nr.t_tmp[q], t_mod], w=[t_h])

    def ffn_phase(i, s):
        sl = 0 if s == 0 else 2
        T = 1024 if S % 1024 == 0 else 512
        NH2 = T // 512
        cv = conv_tok[f"f{i}{s}"]
        wgv = wg_s[i][s].rearrange("(kc p) f -> p kc f", p=128)
        wuv = wu_s[i][s].rearrange("(kc p) f -> p kc f", p=128)
        wdv = wd_s[i][s].rearrange("(fc p) d -> p fc d", p=128)
        xTv = xT.rearrange("(dc p) s -> p dc s", p=128)
        with ExitStack() as st:
            xt = [sb(st, "fx", [128, DC, T], F32) for _ in range(2)]
            t_x = k.toks_n(2)
            nr = NormRes(st, T)
            hT_sb = sb(st, "fh", [128, DC, T], BF16)
            t_h = k.tok()
            aT = sb(st, "fa", [128, FC, T], BF16)
            t_a = k.tok()
            NWB = 3
            wg = [sb(st, "fwg", [128, DC, 256], BF16) for _ in range(NWB)]
            wu = [sb(st, "fwu", [128, DC, 256], BF16) for _ in range(NWB)]
            t_wg = k.toks_n(NWB)
            t_wu = k.toks_n(NWB)
            wd = [sb(st, "fwd", [128, FC, 128], BF16) for _ in range(2)]
            t_wd = k.toks_n(2)
            sg = [sb(st, "fsg", [128, 512], F32) for _ in range(2)]
            t_sg = k.toks_n(2)
            pg = [ps(st, "pg") for _ in range(2)]
            pu = [ps(st, "pu") for _ in range(2)]
            t_pg = k.toks_n(2)
            t_pu = k.toks_n(2)
            po = [ps(st, "po") for _ in range(2)]
            t_po = k.toks_n(2)
            NT = S // T
            NFG = FC // 2
            wcount = [0]

            def load_w(fg):
                q = wcount[0] % NWB
                wcount[0] += 1
                k.dma("sp", out=wg[q][:, :, :], in_=wgv[:, :, fg * 256:(fg + 1) * 256], r=[cv], w=[t_wg[q]], st=t_wg[q])
                k.dma("sp", out=wu[q][:, :, :], in_=wuv[:, :, fg * 256:(fg + 1) * 256], r=[cv], w=[t_wu[q]], st=t_wu[q])
                return q

            dcount = [0]

            def load_wd(dc):
                q = dcount[0] % 2
                dcount[0] += 1
                k.dma("sp", out=wd[q][:, :, :], in_=wdv[:, :, dc * 128:(dc + 1) * 128], r=[cv], w=[t_wd[q]], st=t_wd[q])
                return q

            k.dma("sp", out=xt[0][:, :, :], in_=xTv[:, :, 0:T], w=[t_x[0]], st=t_x[0])
            n1 = 0
            n2 = 0
            for t in range(NT):
                xb = xt[t % 2]
                tx = t_x[t % 2]
                wq = [load_w(0), load_w(1)]
                if t + 1 < NT:
                    k.dma("sp", out=xt[(t + 1) % 2][:, :, :], in_=xTv[:, :, (t + 1) * T:(t + 2) * T], w=[t_x[(t + 1) % 2]],
                          st=t_x[(t + 1) % 2])
                norm_tile(nr, xb, tx, i, sl, lambda dc: hT_sb[:, dc, :], t_h)
                for fg in range(NFG):
                    if fg + 2 < NFG:
                        wq.append(load_w(fg + 2))
                    q = wq[fg]
                    for fl in range(2):
                        fc = fg * 2 + fl
                        for hf in range(NH2):
                            b = n1 % 2
                            n1 += 1
                            cs = slice(hf * 512, (hf + 1) * 512)
                            for kc in range(DC):
                                k.op("pe", lambda e: e.matmul(pg[b][:, :], lhsT=wg[q][:, kc, fl * 128:(fl + 1) * 128], rhs=hT_sb[:, kc, cs],
                                                              start=(kc == 0), stop=(kc == DC - 1)),
                                     r=[t_wg[q], t_h], w=[t_pg[b]], inc=(kc == DC - 1))
                            for kc in range(DC):
                                k.op("pe", lambda e: e.matmul(pu[b][:, :], lhsT=wu[q][:, kc, fl * 128:(fl + 1) * 128], rhs=hT_sb[:, kc, cs],
                                                              start=(kc == 0), stop=(kc == DC - 1)),
                                     r=[t_wu[q], t_h], w=[t_pu[b]], inc=(kc == DC - 1))
                            k.op("act", lambda e: e.activation(out=sg[b][:, :], in_=pg[b][:, :], func=AF.Silu), r=[t_pg[b]], w=[t_sg[b]])
                            k.op("dve", lambda e: e.tensor_tensor(out=aT[:, fc, cs], in0=sg[b][:, :], in1=pu[b][:, :], op=ALU.mult),
                                 r=[t_sg[b], t_pu[b]], w=[t_a])
                dq = [load_wd(0), load_wd(1)]
                for dc in range(DC):
                    q = dq[dc]
                    for hf in range(NH2):
                        b = n2 % 2
                        n2 += 1
                        cs = slice(hf * 512, (hf + 1) * 512)
                        for fc in range(FC):
                            k.op("pe", lambda e: e.matmul(po[b][:, :], lhsT=wd[q][:, fc, :], rhs=aT[:, fc, cs],
                                                          start=(fc == 0), stop=(fc == FC - 1)),
                                 r=[t_wd[q], t_a], w=[t_po[b]], inc=(fc == FC - 1))
                        k.op("dve", lambda e: e.scalar_tensor_tensor(out=xb[:, dc, cs], in0=po[b][:, :], scalar=g_col(i, sl, dc),
                                                                     in1=xb[:, dc, cs], op0=ALU.mult, op1=ALU.add),
                             r=[t_po[b], t_mod, tx], w=[tx])
                    if dc + 2 < DC:
                        dq.append(load_wd(dc + 2))
                k.dma("pool", out=xTv[:, :, t * T:(t + 1) * T], in_=xb[:, :, :], r=[tx], st=tx)
        k.barrier("load_wd")


    hTv = hT.rearrange("(kc p) s -> p kc s", p=128)
    xTv_g = xT.rearrange("(dc p) s -> p dc s", p=128)
    mdiag_b = cb[:, C_MDIAG:C_MDIAG + 128]
    mneg_b = cb[:, C_MNEG:C_MNEG + 128]
    ident_b = cb[:, C_IDENT:C_IDENT + 128]
    moff_b = cb[:, C_MOFF:C_MOFF + 128]
    perm_b = cb[:, C_PERM:C_PERM + 128]
    ones_col = cf[:, C_ONES:C_ONES + 1]

    def stop_at(name):
        if debug_stop == name:
            raise StopBuild()

    def h_phase(i):
        T = 1024 if S % 1024 == 0 else 512
        with ExitStack() as st:
            xt = [sb(st, "hx", [128, DC, T], F32) for _ in range(2)]
            t_x = k.toks_n(2)
            nr = NormRes(st, T)
            hs = [sb(st, "hh", [128, DC, T], BF16) for _ in range(2)]
            t_hs = k.toks_n(2)
            NT = S // T
            k.dma("sp", out=xt[0][:, :, :], in_=xTv_g[:, :, 0:T], w=[t_x[0]], st=t_x[0])
            for t in range(NT):
                if t + 1 < NT:
                    k.dma("sp", out=xt[(t + 1) % 2][:, :, :], in_=xTv_g[:, :, (t + 1) * T:(t + 2) * T], w=[t_x[(t + 1) % 2]],
                          st=t_x[(t + 1) % 2])
                hb = hs[t % 2]
                norm_tile(nr, xt[t % 2], t_x[t % 2], i, 1, lambda dc: hb[:, dc, :], t_hs[t % 2])
                k.dma("pool", out=hTv[:, :, t * T:(t + 1) * T], in_=hb[:, :, :], r=[t_hs[t % 2]], st=t_hs[t % 2])
        k.barrier("h_phase")

    def rope_tables():
        TWO_PI = 2.0 * math.pi
        C1 = 6.28125
        C2 = TWO_PI - C1
        W = 2048 if S >= 2048 else S
        with ExitStack() as st:
            pi_ = sb(st, "rp_i", [128, W], I32)
            ang = sb(st, "rp_f", [128, W], F32)
            kf = sb(st, "rp_kf", [128, W], F32)
            m = sb(st, "rp_m", [128, W], F32)
            m2 = sb(st, "rp_m2", [128, W], F32)
            gt = sb(st, "rp_gt", [128, W], F32)
            t_pi, t_ang, t_kf, t_m, t_m2, t_gt = (k.tok() for _ in range(6))

            def wrap(buf, tb):
                k.op("dve", lambda e: e.tensor_scalar(out=gt[:, :], in0=buf[:, :], scalar1=math.pi, scalar2=None, op0=ALU.is_gt),
                     r=[tb], w=[t_gt])
                k.op("dve", lambda e: e.scalar_tensor_tensor(out=buf[:, :], in0=gt[:, :], scalar=-TWO_PI, in1=buf[:, :],
                                                             op0=ALU.mult, op1=ALU.add), r=[t_gt, tb], w=[tb])
                k.op("dve", lambda e: e.tensor_scalar(out=gt[:, :], in0=buf[:, :], scalar1=-math.pi, scalar2=None, op0=ALU.is_lt),
                     r=[tb], w=[t_gt])
                k.op("dve", lambda e: e.scalar_tensor_tensor(out=buf[:, :], in0=gt[:, :], scalar=TWO_PI, in1=buf[:, :],
                                                             op0=ALU.mult, op1=ALU.add), r=[t_gt, tb], w=[tb])
                k.op("dve", lambda e: e.tensor_scalar(out=buf[:, :], in0=buf[:, :], scalar1=-math.pi, scalar2=math.pi,
                                                      op0=ALU.max, op1=ALU.min), r=[tb], w=[tb])

            for hf in range(S // W):
                cs = slice(hf * W, (hf + 1) * W)
                k.dma("sp", out=pi_[:, :], in_=pos_in[0:1, cs].partition_broadcast(128), w=[t_pi], st=t_pi)
                k.op("dve", lambda e: e.tensor_copy(out=ang[:, :], in_=pi_[:, :]), r=[t_pi], w=[t_ang])
                k.op("dve", lambda e: e.tensor_scalar(out=ang[:, :], in0=ang[:, :], scalar1=cf[:, C_ROPE:C_ROPE + 1], scalar2=None,
                                                      op0=ALU.mult), r=[t_ang, t_cf], w=[t_ang])
                k.op("dve", lambda e: e.tensor_scalar(out=kf[:, :], in0=ang[:, :], scalar1=1.0 / TWO_PI, scalar2=None, op0=ALU.mult),
                     r=[t_ang], w=[t_kf])
                k.op("dve", lambda e: e.tensor_copy(out=pi_[:, :], in_=kf[:, :]), r=[t_kf, t_pi], w=[t_pi])
                k.op("dve", lambda e: e.tensor_copy(out=kf[:, :], in_=pi_[:, :]), r=[t_pi], w=[t_kf])
                k.op("dve", lambda e: e.scalar_tensor_tensor(out=m[:, :], in0=kf[:, :], scalar=-C1, in1=ang[:, :], op0=ALU.mult, op1=ALU.add),
                     r=[t_kf, t_ang], w=[t_m])
                k.op("dve", lambda e: e.scalar_tensor_tensor(out=m[:, :], in0=kf[:, :], scalar=-C2, in1=m[:, :], op0=ALU.mult, op1=ALU.add),
                     r=[t_kf, t_m], w=[t_m])
                wrap(m, t_m)
                k.op("dve", lambda e: e.tensor_scalar(out=m2[:, :], in0=m[:, :], scalar1=0.5 * math.pi, scalar2=None, op0=ALU.add),
                     r=[t_m], w=[t_m2])
                wrap(m2, t_m2)
                k.op("act", lambda e: e.activation(out=m[:, :], in_=m[:, :], func=AF.Sin), r=[t_m], w=[t_m])
                k.op("dve", lambda e: e.tensor_scalar(out=m[:, :], in0=m[:, :], scalar1=cf[:, C_ROPE + 1:C_ROPE + 2], scalar2=None,
                                                      op0=ALU.mult), r=[t_m, t_cf], w=[t_m])
                k.dma("pool", out=cs_s[1][:, cs], in_=m[:, :], r=[t_m], st=t_m)
                k.op("act", lambda e: e.activation(out=m2[:, :], in_=m2[:, :], func=AF.Sin), r=[t_m2], w=[t_m2])
                k.dma("pool", out=cs_s[0][:, cs], in_=m2[:, :], r=[t_m2], st=t_m2)
        k.barrier("wrap")

    def mixer_phase(i):
        is_fox = (i % 2 == 0)
        j = i // 2
        cv = conv_tok[f"a{i}"]
        winv = win_s[i].rearrange("(kc p) f -> p kc f", p=128)
        dils = [1] if is_fox else [c[1] for c in DIL_CFG]
        NG = len(dils)
        NHALF = S // 2048 if S >= 2048 else 1
        HL = S // NHALF
        h_phase(i)
        stop_at(f"hph{i}")
        for g, d in enumerate(dils):
            nb = S // d // 128
            with ExitStack() as stg_:
                v_all = sb(stg_, "v_all", [128, NTB, NH, HD + 1], BF16)
                t_v = k.toks_n(NTB, "v")
                t_vone = k.tok()
                k.op("pool", lambda e: e.memset(v_all[:, :, :, HD:HD + 1], 1.0), w=[t_vone])
                sp_all = sb(stg_, "sp_all", [128, NTB, NH], F32) if is_fox else None
                t_sp = k.toks_n(NTB, "sp") if is_fox else None
                with ExitStack() as st:
                    qc0 = 0 if is_fox else g * 3 * D
                    wv = sb(st, "wv", [128, DC, D], BF16)
                    t_wv = k.tok()
                    k.dma("sp", out=wv[:, :, :], in_=winv[:, :, qc0 + 2 * D:qc0 + 3 * D], r=[cv], w=[t_wv], st=t_wv)
                    gq = sb(st, "gq", [128, 1], F32)
                    gk = sb(st, "gk", [128, 1], F32)
                    t_g = k.tok()
                    if is_fox:
                        srcq = fox_q_g[j:j + 1, :].rearrange("o d -> d o")
                        srck = fox_k_g[j:j + 1, :].rearrange("o d -> d o")
                    else:
                        srcq = dil_q_g[j, g:g + 1, :].rearrange("o d -> d o")
                        srck = dil_k_g[j, g:g + 1, :].rearrange("o d -> d o")
                    for a in range(2):
                        k.dma("sp", out=gq[a * 64:(a + 1) * 64, :], in_=srcq, w=[t_g], st=t_g)
                        k.dma("sp", out=gk[a * 64:(a + 1) * 64, :], in_=srck, w=[t_g], st=t_g)
                    k.op("dve", lambda e: e.tensor_scalar(out=gq[:, :], in0=gq[:, :], scalar1=0.125, scalar2=None, op0=ALU.mult),
                         r=[t_g], w=[t_g])
                    if is_fox:
                        wf = sb(st, "wf", [128, DC, NH], BF16)
                        bfb = sb(st, "bfb", [128, NH], F32)
                        t_wf = k.tok()
                        k.dma("sp", out=wf[:, :, :], in_=winv[:, :, 3 * D:3 * D + NH], r=[cv], w=[t_wf], st=t_wf)
                        k.dma("sp", out=bfb[:, :], in_=fox_b_f[j:j + 1, :].partition_broadcast(128), w=[t_wf], st=t_wf)
                        zt = [sb(st, "zt", [128, NH], F32) for _ in range(2)]
                        t_zt = k.toks_n(2)
                        pf = ps(st, "pf")
                        t_pf = k.tok()
                    else:
                        ct = sb(st, "ct", [128, HL], F32)
                        stt = sb(st, "stt", [128, HL], F32)
                        t_cs = k.tok()
                        t1 = [sb(st, "t1", [128, 512], F32) for _ in range(2)]
                        t2 = [sb(st, "t2", [128, 512], F32) for _ in range(2)]
                        t_t1 = k.toks_n(2)
                        t_t2 = k.toks_n(2)
                        pperm = ps(st, "pperm")
                        t_pperm = k.tok()
                    hh = sb(st, "hhalf", [128, DC, HL], BF16)
                    t_hh = k.tok()
                    wqk = [sb(st, "wqk", [128, DC, 128], BF16) for _ in range(3)]
                    t_wqk = k.toks_n(3)
                    stg = [sb(st, "stg", [128, HL], BF16) for _ in range(2)]
                    t_stg = k.toks_n(2)
                    NB = 3
                    sq = [sb(st, "bsq", [128, 512], BF16) for _ in range(NB)]
                    t_sq = k.toks_n(NB)
                    rs = [sb(st, "brs", [128, 512], F32) for _ in range(NB)]
                    t_rs = k.toks_n(NB)
                    pq = [ps(st, "pq") for _ in range(NB)]
                    t_pq = k.toks_n(NB)
                    pssq = [ps(st, "pssq") for _ in range(2)]
                    t_pssq = k.toks_n(2)
                    pv = pq[0:2]
                    t_pvp = t_pq[0:2]
                    if not is_fox:
                        qn = [sb(st, "qn", [128, 512], F32) for _ in range(NB)]
                        qnb = [sb(st, "qnb", [128, 512], BF16) for _ in range(NB)]
                        t_qn = k.toks_n(NB)
                        t_qnb = k.toks_n(NB)
                        pperm2 = [pperm, ps(st, "pperm2")]
                        t_pperm2 = [t_pperm, k.tok()]
                    NTT = HL // 512
                    for hf in range(NHALF):
                        k.dma("sp", out=hh[:, :, :], in_=hTv[:, :, hf * HL:(hf + 1) * HL], w=[t_hh], st=t_hh)
                        if not is_fox:
                            k.dma("sp", out=ct[:, :], in_=cs_s[0][:, hf * HL:(hf + 1) * HL], w=[t_cs], st=t_cs)
                            k.dma("sp", out=stt[:, :], in_=cs_s[1][:, hf * HL:(hf + 1) * HL], w=[t_cs], st=t_cs)
                        items = [(c, tt) for c in range(16) for tt in range(NTT)]
                        NI = len(items)

                        def wload(c):
                            a = c // 8
                            cc = c % 8
                            col0 = qc0 + a * D + cc * 128
                            k.dma("sp", out=wqk[c % 3][:, :, :], in_=winv[:, :, col0:col0 + 128], r=[cv], w=[t_wqk[c % 3]], st=t_wqk[c % 3])

                        def stage_a(n):
                            c, tt = items[n]
                            if tt == 0 and c + 1 < 16:
                                wload(c + 1)
                            b = n % NB
                            cs = slice(tt * 512, (tt + 1) * 512)
                            for kc in range(DC):
                                k.op("pe", lambda e: e.matmul(pq[b][:, :], lhsT=wqk[c % 3][:, kc, :], rhs=hh[:, kc, cs],
                                                              start=(kc == 0), stop=(kc == DC - 1)),
                                     r=[t_wqk[c % 3], t_hh], w=[t_pq[b]], inc=(kc == DC - 1))
                            k.op("act", lambda e: e.activation(out=sq[b][:, :], in_=pq[b][:, :], func=AF.Square), r=[t_pq[b]], w=[t_sq[b]])

                        def stage_b(n):
                            c, tt = items[n]
                            b = n % NB
                            b2 = n % 2
                            sg_ = c % 2
                            cs = slice(tt * 512, (tt + 1) * 512)
                            gain = gq if c < 8 else gk
                            k.op("pe", lambda e: e.matmul(pssq[b2][:, :], lhsT=blk_b, rhs=sq[b][:, :], start=True, stop=True),
                                 r=[t_sq[b], t_cf], w=[t_pssq[b2]])
                            k.op("act", lambda e: e.activation(out=rs[b][:, :], in_=pssq[b2][:, :], func=AF.Ln, scale=1.0 / HD, bias=eps_col),
                                 r=[t_pssq[b2], t_cf], w=[t_rs[b]])
                            k.op("act", lambda e: e.activation(out=rs[b][:, :], in_=rs[b][:, :], func=AF.Exp, scale=-0.5),
                                 r=[t_rs[b]], w=[t_rs[b]])
                            if is_fox:
                                k.op("dve", lambda e: e.scalar_tensor_tensor(out=stg[sg_][:, cs], in0=pq[b][:, :], scalar=gain[:, 0:1],
                                                                             in1=rs[b][:, :], op0=ALU.mult, op1=ALU.mult),
                                     r=[t_pq[b], t_g, t_rs[b]], w=[t_stg[sg_]])
                                if tt == NTT - 1:
                                    store(c)
                            else:
                                k.op("dve", lambda e: e.scalar_tensor_tensor(out=qn[b][:, :], in0=pq[b][:, :], scalar=gain[:, 0:1],
                                                                             in1=rs[b][:, :], op0=ALU.mult, op1=ALU.mult),
                                     r=[t_pq[b], t_g, t_rs[b]], w=[t_qn[b]])
                                k.op("act", lambda e: e.activation(out=qnb[b][:, :], in_=qn[b][:, :], func=AF.Identity),
                                     r=[t_qn[b]], w=[t_qnb[b]])

                        def stage_c(n):
                            c, tt = items[n]
                            b = n % NB
                            b2 = n % 2
                            sg_ = c % 2
                            cs = slice(tt * 512, (tt + 1) * 512)
                            k.op("pe", lambda e: e.matmul(pperm2[b2][:, :], lhsT=perm_b, rhs=qnb[b][:, :], start=True, stop=True),
                                 r=[t_qnb[b], t_cf], w=[t_pperm2[b2]])
                            k.op("pool", lambda e: e.tensor_tensor(out=t1[b2][:, :], in0=qn[b][:, :], in1=ct[:, cs], op=ALU.mult),
                                 r=[t_qn[b], t_cs], w=[t_t1[b2]])
                            k.op("dve", lambda e: e.tensor_tensor(out=t2[b2][:, :], in0=pperm2[b2][:, :], in1=stt[:, cs], op=ALU.mult),
                                 r=[t_pperm2[b2], t_cs], w=[t_t2[b2]])
                            k.op("dve", lambda e: e.tensor_tensor(out=stg[sg_][:, cs], in0=t1[b2][:, :], in1=t2[b2][:, :], op=ALU.add),
                                 r=[t_t1[b2], t_t2[b2]], w=[t_stg[sg_]])
                            if tt == NTT - 1:
                                store(c)

                        def store(c):
                            a = c // 8
                            cc = c % 8
                            sg_ = c % 2
                            k.dma("pool", out=qk_s[g][a][cc * 128:(cc + 1) * 128, hf * HL:(hf + 1) * HL], in_=stg[sg_][:, :],
                                  r=[t_stg[sg_]], st=t_stg[sg_])

                        wload(0)
                        for n in range(NI + 2):
                            if n < NI:
                                stage_a(n)
                            if 0 <= n - 1 < NI:
                                stage_b(n - 1)
                            if (not is_fox) and 0 <= n - 2 < NI:
                                stage_c(n - 2)
                        bph = HL // 128
                        for bi in range(bph):
                            if d * 128 <= HL:
                                spans = HL // (128 * d)
                                sp_i = bi // d if False else None
                            sidx = bi // d
                            r = bi % d
                            jb = hf * (HL // (128 * d)) + sidx
                            blk = r * nb + jb
                            start = sidx * 128 * d + r
                            cols = slice(start, start + 127 * d + 1, d) if d > 1 else slice(start, start + 128)
                            for hv in range(2):
                                for kc in range(DC):
                                    k.op("pe", lambda e: e.matmul(pv[hv][:, :], lhsT=hh[:, kc, cols], rhs=wv[:, kc, hv * 512:(hv + 1) * 512],
                                                                  start=(kc == 0), stop=(kc == DC - 1)),
                                         r=[t_hh, t_wv], w=[t_pvp[hv]], inc=(kc == DC - 1))
                            k.op("act", lambda e: e.activation(out=v_all[:, blk, 0:8, 0:HD], in_=pv[0][:, :].rearrange("p (h e) -> p h e", h=8),
                                                               func=AF.Identity), r=[t_pvp[0]], w=[t_v[blk]])
                            k.op("dve", lambda e: e.tensor_copy(out=v_all[:, blk, 8:16, 0:HD], in_=pv[1][:, :].rearrange("p (h e) -> p h e", h=8)),
                                 r=[t_pvp[1]], w=[t_v[blk]])
                            if is_fox:
                                zb = bi % 2
                                for kc in range(DC):
                                    k.op("pe", lambda e: e.matmul(pf[:, 0:NH], lhsT=hh[:, kc, cols], rhs=wf[:, kc, :],
                                                                  start=(kc == 0), stop=(kc == DC - 1)),
                                         r=[t_hh, t_wf], w=[t_pf], inc=(kc == DC - 1))
                                k.op("dve", lambda e: e.tensor_tensor(out=zt[zb][:, :], in0=pf[:, 0:NH], in1=bfb[:, :], op=ALU.add),
                                     r=[t_pf, t_wf], w=[t_zt[zb]])
                                k.op("act", lambda e: e.activation(out=zt[zb][:, :], in_=zt[zb][:, :], func=AF.Exp, scale=-1.0),
                                     r=[t_zt[zb]], w=[t_zt[zb]])
                                k.op("act", lambda e: e.activation(out=sp_all[:, blk, :], in_=zt[zb][:, :], func=AF.Ln, bias=ones_col),
                                     r=[t_zt[zb], t_cf], w=[t_sp[blk]])
                k.barrier("mixer_phase")
                stop_at(f"b1_{i}_{g}")
                if is_fox:
                    fox_cum(sp_all)
                    stop_at(f"cum{i}")
                if is_fox:
                    b2_fox(v_all)
                else:
                    b2_dil(g, d, v_all)
                stop_at(f"b2_{i}_{g}")
        merge_outproj(i, NG)

    def fox_cum(sp_all):
        with ExitStack() as st:
            cum = sb(st, "cum", [NH, S], F32)
            t_cum = k.tok()
            pre = sb(st, "pre", [NH, NTB], F32)
            t_pre = k.tok()
            hif = sb(st, "hif", [NH, S], F32)
            t_hif = k.tok()
            rb = [sb(st, "rb", [NH, S], BF16) for _ in range(2)]
            t_rb = k.toks_n(2)
            oneb = sb(st, "oneb", [NH, S], BF16)
            t_one = k.tok()
            pc = [ps(st, "pc") for _ in range(2)]
            t_pc = k.toks_n(2)
            for m4 in range(NTB // 4):
                b = m4 % 2
                for mm in range(4):
                    m = m4 * 4 + mm
                    k.op("pe", lambda e: e.matmul(pc[b][0:NH, mm * 128:(mm + 1) * 128], lhsT=sp_all[:, m, :], rhs=utri_f, start=True, stop=True),
                         r=[t_cf], w=[t_pc[b]], inc=(mm == 3))
                k.op("act", lambda e: e.activation(out=cum[:, m4 * 512:(m4 + 1) * 512], in_=pc[b][0:NH, :], func=AF.Identity),
                     r=[t_pc[b]], w=[t_cum])
            k.op("dve", lambda e: e.memset(pre[:, 0:1], 0.0), w=[t_pre])
            for m in range(1, NTB):
                k.op("dve", lambda e: e.tensor_tensor(out=pre[:, m:m + 1], in0=pre[:, m - 1:m], in1=cum[:, m * 128 - 1:m * 128], op=ALU.add),
                     r=[t_cum, t_pre], w=[t_pre])
            for m in range(1, NTB):
                k.op("dve", lambda e: e.tensor_scalar(out=cum[:, m * 128:(m + 1) * 128], in0=cum[:, m * 128:(m + 1) * 128],
                                                      scalar1=pre[:, m:m + 1], scalar2=None, op0=ALU.add), r=[t_pre, t_cum], w=[t_cum])
            k.op("pool", lambda e: e.memset(oneb[:, :], 1.0), w=[t_one])
            for row in range(3):
                k.dma("sp", out=aug_s[1][:, row, :], in_=oneb[:, :], r=[t_one], st=t_one)
                k.dma("sp", out=aug_s[0][:, 3 + row, :], in_=oneb[:, :], r=[t_one], st=t_one)
            for part in range(3):
                kb_, qb_ = rb[0], rb[1]
                k.op("dve", lambda e: e.tensor_copy(out=kb_[:, :], in_=cum[:, :]), r=[t_cum], w=[t_rb[0]])
                k.op("act", lambda e: e.activation(out=qb_[:, :], in_=kb_[:, :], func=AF.Identity, scale=-1.0), r=[t_rb[0]], w=[t_rb[1]])
                k.dma("sp", out=aug_s[1][:, 3 + part, :], in_=kb_[:, :], r=[t_rb[0]], st=t_rb[0])
                k.dma("sp", out=aug_s[0][:, part, :], in_=qb_[:, :], r=[t_rb[1]], st=t_rb[1])
                if part < 2:
                    k.op("dve", lambda e: e.tensor_copy(out=hif[:, :], in_=kb_[:, :]), r=[t_rb[0]], w=[t_hif])
                    k.op("dve", lambda e: e.tensor_tensor(out=cum[:, :], in0=cum[:, :], in1=hif[:, :], op=ALU.subtract),
                         r=[t_hif, t_cum], w=[t_cum])
        k.barrier("fox_cum")

    def b2_fox(v_all):
        NQT = S // 512
        with ExitStack() as st:
            qa = [sb(st, "qa", [HD + 6, S], BF16) for _ in range(2)]
            ka = [sb(st, "ka", [HD + 6, S], BF16) for _ in range(2)]
            t_qa = k.toks_n(2)
            t_ka = k.toks_n(2)
            NP = 4
            pt = [sb(st, "pt", [128, 512], BF16) for _ in range(NP)]
            t_pt = k.toks_n(NP)
            ost = [sb(st, "ost", [HD + 1, S], F32) for _ in range(2)]
            t_ost = k.toks_n(2)
            NSP = 4
            sps = [ps(st, "sps") for _ in range(NSP)]
            t_sps = k.toks_n(NSP)
            ops_ = [ps(st, "ops") for _ in range(2)]
            t_ops = k.toks_n(2)

            def load_head(h):
                b = h % 2
                k.dma("sp", out=qa[b][0:HD, :], in_=qk_s[0][0][h * HD:(h + 1) * HD, :], w=[t_qa[b]], st=t_qa[b])
                k.dma("sp", out=qa[b][HD:HD + 6, :], in_=aug_s[0][h], w=[t_qa[b]], st=t_qa[b])
                k.dma("sp", out=ka[b][0:HD, :], in_=qk_s[0][1][h * HD:(h + 1) * HD, :], w=[t_ka[b]], st=t_ka[b])
                k.dma("sp", out=ka[b][HD:HD + 6, :], in_=aug_s[1][h], w=[t_ka[b]], st=t_ka[b])

            load_head(0)
            nstep = 0
            for h in range(NH):
                hb = h % 2
                if h + 1 < NH:
                    load_head(h + 1)
                steps = [(jt, kb) for jt in range(NQT) for kb in range(4 * jt + 4)]
                LA = 2

                def emit_qk(idx):
                    jt, kb = steps[idx]
                    sidx = (nstep + idx) % NSP
                    qlo = max(kb, 4 * jt) * 128
                    W = (4 * jt + 4) * 128 - qlo
                    diag = kb >= 4 * jt
                    k.op("pe", lambda e: e.matmul(sps[sidx][:, 0:W], lhsT=ka[hb][:, kb * 128:(kb + 1) * 128], rhs=qa[hb][:, qlo:qlo + W],
                                                  start=True, stop=not diag), r=[t_ka[hb], t_qa[hb]], w=[t_sps[sidx]], inc=not diag)
                    if diag:
                        k.op("pe", lambda e: e.matmul(sps[sidx][:, 0:128], lhsT=ident_b, rhs=mneg_b, start=False, stop=True),
                             r=[t_cf], w=[t_sps[sidx]])

                for idx in range(min(LA, len(steps))):
                    emit_qk(idx)
                for idx, (jt, kb) in enumerate(steps):
                    if idx + LA < len(steps):
                        emit_qk(idx + LA)
                    sidx = (nstep + idx) % NSP
                    pidx = (nstep + idx) % NP
                    qlo = max(kb, 4 * jt) * 128
                    W = (4 * jt + 4) * 128 - qlo
                    off = qlo - 4 * jt * 128
                    ob = jt % 2
                    k.op("act", lambda e: e.activation(out=pt[pidx][:, 0:W], in_=sps[sidx][:, 0:W], func=AF.Exp), r=[t_sps[sidx]], w=[t_pt[pidx]])
                    last = (kb == 4 * jt + 3)
                    k.op("pe", lambda e: e.matmul(ops_[ob][0:HD + 1, off:off + W], lhsT=v_all[:, kb, h, :], rhs=pt[pidx][:, 0:W],
                                                  start=(kb == 0), stop=last), r=[t_pt[pidx]], w=[t_ops[ob]], inc=last)
                    if last:
                        k.op("dve", lambda e: e.tensor_copy(out=ost[hb][:, jt * 512:(jt + 1) * 512], in_=ops_[ob][0:HD + 1, :]),
                             r=[t_ops[ob]], w=[t_ost[hb]])
                nstep += len(steps)
                k.dma("pool", out=oun_o[0][h * HD:(h + 1) * HD, :], in_=ost[hb][0:HD, :], r=[t_ost[hb]], st=t_ost[hb])
                k.dma("pool", out=oun_d[0][h:h + 1, :], in_=ost[hb][HD:HD + 1, :], r=[t_ost[hb]], st=t_ost[hb])
        k.barrier("emit_qk")

    def b2_dil(g, d, v_all):
        nb = S // d // 128
        DBG = 0
        with ExitStack() as st:
            qa = [sb(st, "dq", [HD, S], BF16) for _ in range(2)]
            ka = [sb(st, "dk", [HD, S], BF16) for _ in range(2)]
            t_qa = k.toks_n(2)
            t_ka = k.toks_n(2)
            NP = 4
            pt = [sb(st, "dpt", [128, 256], BF16) for _ in range(NP)]
            t_pt = k.toks_n(NP)
            ost = [sb(st, "dost", [HD + 1, S], F32) for _ in range(2)]
            t_ost = k.toks_n(2)
            NSP = 4
            sps = [ps(st, "dsps") for _ in range(NSP)]
            t_sps = k.toks_n(NSP)
            NOS = 4
            ops_ = [ps(st, "dops") for _ in range(NOS)]
            t_os = k.toks_n(NOS)

            def oslot(n):
                c = (n // NOS) % 4
                return ops_[n % NOS][0:HD + 1, c * 128:(c + 1) * 128]

            def load_head(h):
                b = h % 2
                k.dma("sp", out=qa[b][:, :], in_=qk_s[g][0][h * HD:(h + 1) * HD, :], w=[t_qa[b]], st=t_qa[b])
                k.dma("sp", out=ka[b][:, :], in_=qk_s[g][1][h * HD:(h + 1) * HD, :], w=[t_ka[b]], st=t_ka[b])

            def sl(start, cnt):
                return slice(start, start + (cnt - 1) * d + 1, d) if d > 1 else slice(start, start + cnt)

            load_head(0)
            nstep = 0
            nos = 0
            for h in range(NH):
                hb = h % 2
                if h + 1 < NH:
                    load_head(h + 1)
                steps = [(r, jb) for r in range(d) for jb in range(nb)]
                LA = 2
                k.op("pool", lambda e: e.memset(ost[hb][:, :], 0.0), w=[t_ost[hb]])

                def emit_qk(idx):
                    r, jb = steps[idx]
                    sidx = (nstep + idx) % NSP
                    cnt = 256 if jb + 1 < nb else 128
                    kst = r + d * 128 * jb
                    k.op("pe", lambda e: e.matmul(sps[sidx][:, 0:cnt], lhsT=ka[hb][:, sl(kst, 128)], rhs=qa[hb][:, sl(kst, cnt)],
                                                  start=True, stop=True), r=[t_ka[hb], t_qa[hb]], w=[t_sps[sidx]])

                for idx in range(min(LA, len(steps))):
                    emit_qk(idx)
                for idx, (r, jb) in enumerate(steps):
                    if idx + LA < len(steps):
                        emit_qk(idx + LA)
                    sidx = (nstep + idx) % NSP
                    pidx = (nstep + idx) % NP
                    cnt = 256 if jb + 1 < nb else 128
                    kst = r + d * 128 * jb
                    blk = r * nb + jb
                    k.op("act", lambda e: e.activation(out=pt[pidx][:, 0:cnt], in_=sps[sidx][:, 0:cnt], func=AF.Exp), r=[t_sps[sidx]], w=[t_pt[pidx]])
                    k.op("dve", lambda e: e.tensor_tensor(out=pt[pidx][:, 0:cnt], in0=pt[pidx][:, 0:cnt], in1=cb[:, C_MDIAG:C_MDIAG + cnt], op=ALU.mult),
                         r=[t_pt[pidx], t_cf], w=[t_pt[pidx]])
                    ob = (nstep + idx) % NOS
                    k.op("pe", lambda e: e.matmul(ops_[ob][0:HD + 1, 0:cnt], lhsT=v_all[:, blk, h, :], rhs=pt[pidx][:, 0:cnt], start=True, stop=True),
                         r=[t_pt[pidx]], w=[t_os[ob]])
                    k.op("dve", lambda e: e.tensor_tensor(out=ost[hb][:, sl(kst, cnt)], in0=ops_[ob][0:HD + 1, 0:cnt], in1=ost[hb][:, sl(kst, cnt)],
                                                          op=ALU.add), r=[t_os[ob], t_ost[hb]], w=[t_ost[hb]])
                    if jb == nb - 1:
                        nos += nb
                nstep += len(steps)
                k.dma("pool", out=oun_o[g][h * HD:(h + 1) * HD, :], in_=ost[hb][0:HD, :], r=[t_ost[hb]], st=t_ost[hb])
                k.dma("pool", out=oun_d[g][h:h + 1, :], in_=ost[hb][HD:HD + 1, :], r=[t_ost[hb]], st=t_ost[hb])
        k.barrier("emit_qk")

    def merge_outproj(i, NG):
        cv = conv_tok[f"a{i}"]
        T = 512
        woutv = wout_s[i].rearrange("(c p) d -> p c d", p=128)
        with ExitStack() as st:
            wo = sb(st, "wo", [128, DC, D], BF16)
            t_wo = k.tok()
            k.dma("sp", out=wo[:, :, :], in_=woutv[:, :, :], r=[cv], w=[t_wo], st=t_wo)
            xt = [sb(st, "mx", [128, DC, T], F32) for _ in range(2)]
            t_x = k.toks_n(2)
            on = [[sb(st, "on", [128, DC, T], F32) for _ in range(NG)] for _ in range(2)]
            t_on = [k.toks_n(NG) for _ in range(2)]
            dn = [sb(st, "dn", [NH, NG, T], F32) for _ in range(2)]
            t_dn = k.toks_n(2)
            rec = sb(st, "rec", [NH, T], F32)
            t_rec = k.tok()
            osum = [sb(st, "osum", [128, T], F32) for _ in range(2)]
            t_osum = k.toks_n(2)
            oT = sb(st, "oT", [128, DC, T], BF16)
            t_oT = k.tok()
            py = [ps(st, "py") for _ in range(2)]
            t_py = k.toks_n(2)
            pbc = [ps(st, "pbc") for _ in range(2)]
            t_pbc = k.toks_n(2)
            NT = S // T

            def load_t(t):
                b = t % 2
                cs = slice(t * T, (t + 1) * T)
                k.dma("sp", out=xt[b][:, :, :], in_=xTv_g[:, :, cs], w=[t_x[b]], st=t_x[b])
                for g in range(NG):
                    k.dma("sp", out=on[b][g][:, :, :], in_=oun_o[g].rearrange("(c p) s -> p c s", p=128)[:, :, cs],
                          w=[t_on[b][g]], st=t_on[b][g])
                    k.dma("sp", out=dn[b][:, g, :], in_=oun_d[g][:, cs], w=[t_dn[b]], st=t_dn[b])

            load_t(0)
            ny = 0
            nb_ = 0
            for t in range(NT):
                b = t % 2
                xb = xt[b]
                tx = t_x[b]
                if t + 1 < NT:
                    load_t(t + 1)
                for g in range(1, NG):
                    k.op("dve", lambda e: e.tensor_tensor(out=dn[b][:, 0, :], in0=dn[b][:, 0, :], in1=dn[b][:, g, :], op=ALU.add),
                         r=[t_dn[b]], w=[t_dn[b]])
                k.op("dve", lambda e: e.reciprocal(out=rec[:, :], in_=dn[b][:, 0, :]), r=[t_dn[b]], w=[t_rec])
                for c in range(DC):
                    bb = nb_ % 2
                    nb_ += 1
                    k.op("pe", lambda e: e.matmul(pbc[bb][:, :], lhsT=cf[0:NH, C_ESEL + c * 128:C_ESEL + (c + 1) * 128], rhs=rec[:, :],
                                                  start=True, stop=True), r=[t_rec, t_cf], w=[t_pbc[bb]])
                    src = on[b][0][:, c, :]
                    rd = [t_on[b][0]]
                    if NG == 3:
                        k.op("pool", lambda e: e.tensor_tensor(out=osum[bb][:, :], in0=on[b][1][:, c, :], in1=on[b][2][:, c, :], op=ALU.add),
                             r=[t_on[b][1], t_on[b][2]], w=[t_osum[bb]])
                        k.op("dve", lambda e: e.tensor_tensor(out=osum[bb][:, :], in0=osum[bb][:, :], in1=on[b][0][:, c, :], op=ALU.add),
                             r=[t_on[b][0], t_osum[bb]], w=[t_osum[bb]])
                        src = osum[bb][:, :]
                        rd = [t_osum[bb]]
                    k.op("dve", lambda e: e.tensor_tensor(out=oT[:, c, :], in0=src, in1=pbc[bb][:, :], op=ALU.mult),
                         r=rd + [t_pbc[bb]], w=[t_oT])
                for dc in range(DC):
                    yb = ny % 2
                    ny += 1
                    for c in range(DC):
                        k.op("pe", lambda e: e.matmul(py[yb][:, :], lhsT=wo[:, c, dc * 128:(dc + 1) * 128], rhs=oT[:, c, :],
                                                      start=(c == 0), stop=(c == DC - 1)), r=[t_wo, t_oT], w=[t_py[yb]], inc=(c == DC - 1))
                    k.op("dve", lambda e: e.scalar_tensor_tensor(out=xb[:, dc, :], in0=py[yb][:, :], scalar=g_col(i, 1, dc),
                                                                 in1=xb[:, dc, :], op0=ALU.mult, op1=ALU.add),
                         r=[t_py[yb], t_mod, tx], w=[tx])
                k.dma("pool", out=xTv_g[:, :, t * T:(t + 1) * T], in_=xb[:, :, :], r=[tx], st=tx)
        k.barrier("merge_outproj")

    def program():
        if debug_stop == "conv":
            k.barrier("program")
            return
        setup_mods()
        if debug_stop == "mods":
            return
        transpose_in()
        if debug_stop == "tin":
            transpose_out()
            return
        if depth >= 2:
            rope_tables()
            stop_at("rope")
        for i in range(depth):
            ffn_phase(i, 0)
            if debug_stop == f"ffn{i}0":
                break
            mixer_phase(i)
            if debug_stop == f"mix{i}":
                break
            ffn_phase(i, 1)
        transpose_out()

    stopped = False
    try:
        program()
    except StopBuild:
        stopped = True
    for key in sorted(k.bgkeys):
        if k.seen["sp"].get(key, 0) < k.cnt[key]:
            nc.sync.wait_ge(k.sems[key], k.cnt[key])
    if not stopped:
        top.close()
    nc._phase_names = k.phase_names
    bad = k.check()
    if bad:
        raise RuntimeError(f"semaphore protocol deadlock: {bad}")
    return nc


_CACHE = {}


def _prep_inputs(inputs, b, S, depth, cfc):
    n_fox = (depth + 1) // 2
    n_dil = depth // 2
    f = lambda a: np.ascontiguousarray(a, dtype=np.float32)
    m = {
        "x": f(inputs["x"][b, :S]),
        "c": f(inputs["c"][b]).reshape(DC, 128),
        "positions": np.ascontiguousarray(inputs["positions"][b, :S], dtype=np.int32).reshape(1, S),
        "mod_w": f(inputs["mod_w"][:depth]),
        "mod_b": f(inputs["mod_b"][:depth]).reshape(depth * 72, 128),
        "norm_g": f(inputs["norm_g"][:depth]).reshape(depth * 24, 128),
        "ffn_w_gate": f(inputs["ffn_w_gate"][:depth]),
        "ffn_w_up": f(inputs["ffn_w_up"][:depth]),
        "ffn_w_down": f(inputs["ffn_w_down"][:depth]),
        "fox_w_in": f(inputs["fox_w_in"][:max(n_fox, 1)]),
        "fox_b_f": f(inputs["fox_b_f"][:max(n_fox, 1)]),
        "fox_q_g": f(inputs["fox_q_g"][:max(n_fox, 1)]),
        "fox_k_g": f(inputs["fox_k_g"][:max(n_fox, 1)]),
        "fox_w_out": f(inputs["fox_w_out"][:max(n_fox, 1)]),
        "dil_w_in": f(inputs["dil_w_in"][:max(n_dil, 1)]),
        "dil_q_g": f(inputs["dil_q_g"][:max(n_dil, 1)]),
        "dil_k_g": f(inputs["dil_k_g"][:max(n_dil, 1)]),
        "dil_w_out": f(inputs["dil_w_out"][:max(n_dil, 1)]),
        "cf": cfc,
    }
    return m


def run(inputs, S=4096, depth=4, n_cores=8, debug_stop=None, trace=False):
    key = (S, depth, debug_stop)
    if key not in _CACHE:
        _CACHE[key] = build(S, depth, debug_stop)
    nc = _CACHE[key]
    cfc = make_consts()
    in_maps = [_prep_inputs(inputs, b, S, depth, cfc) for b in range(n_cores)]
    res = run_bass_kernel_spmd(nc, in_maps, core_ids=list(range(n_cores)), **({"trace": True} if trace else {}))
    out = np.stack([np.asarray(r["y"], dtype=np.float32) for r in res.results], axis=0)
    return out, res


def kernel(**inputs):
    out, _ = run(inputs)
    return out
```

```python
import math
from contextlib import ExitStack

import numpy as np
import concourse.bass as bass
import concourse.mybir as mybir
from concourse.bass_utils import run_bass_kernel_spmd

F32 = mybir.dt.float32
BF16 = mybir.dt.bfloat16
I32 = mybir.dt.int32
AF = mybir.ActivationFunctionType
ALU = mybir.AluOpType

D = 1024
DC = 8
HD = 64
NH = 16
DFF = 2816
FC = 22
EPS = 1e-6
FOX_IN = 3 * D + NH
DIL_IN = 9 * D
DIL_CFG = ((128, 1), (512, 4), (2048, 16))
ROPE_THETA = 500000.0

C_IDENT = 0
C_ONES = 128
C_BLK = 256
C_UTRI = 384
C_MDIAG = 512
C_MOFF = 640
C_PERM = 768
C_ROPE = 896
C_EPS = 898
C_MNEG = 900
C_ESEL = 1028
NCF = 1028 + 1024


def make_consts():
    cf = np.zeros((128, NCF), np.float32)
    cf[:, C_IDENT:C_IDENT + 128] = np.eye(128, dtype=np.float32)
    cf[:, C_ONES:C_ONES + 128] = 1.0
    for a in range(2):
        cf[a * 64:(a + 1) * 64, C_BLK + a * 64:C_BLK + (a + 1) * 64] = 1.0
    k = np.arange(128)[:, None]
    q = np.arange(128)[None, :]
    cf[:, C_UTRI:C_UTRI + 128] = (k <= q)
    cf[:, C_MDIAG:C_MDIAG + 128] = (q >= k)
    cf[:, C_MOFF:C_MOFF + 128] = (q <= k)
    half = 8
    inv_freq = (np.float32(ROPE_THETA) ** (-(np.arange(half, dtype=np.float32) * np.float32(2.0) / np.float32(16.0)))).astype(np.float32)
    for p in range(128):
        d = p % 64
        a = p // 64
        if d < 8:
            cf[a * 64 + d + 8, C_PERM + p] = 1.0
            cf[p, C_ROPE] = inv_freq[d]
            cf[p, C_ROPE + 1] = -1.0
        elif d < 16:
            cf[a * 64 + d - 8, C_PERM + p] = 1.0
            cf[p, C_ROPE] = inv_freq[d - 8]
            cf[p, C_ROPE + 1] = 1.0
    cf[:, C_EPS] = EPS
    cf[:, C_MNEG:C_MNEG + 128] = np.where(q < k, -30000.0, 0.0)
    for c in range(8):
        for m in range(128):
            cf[2 * c + m // 64, C_ESEL + c * 128 + m] = 1.0
    return cf


class StopBuild(Exception):
    pass


class Tok:
    __slots__ = ("w", "r", "dsem", "persist", "name")

    def __init__(self, name="", persist=False):
        self.w = None
        self.r = {}
        self.dsem = None
        self.persist = persist
        self.name = name


class K:
    ENG = ("pe", "act", "dve", "pool", "sp")

    def __init__(self, nc):
        self.nc = nc
        self.e = dict(pe=nc.tensor, act=nc.scalar, dve=nc.vector, pool=nc.gpsimd, sp=nc.sync)
        self.sems = {}
        self.cnt = {}
        self.seen = {e: {} for e in self.ENG}
        self.toks = []
        self.bgkeys = set()
        self.uid = 0
        self.free_dsems = []
        self.log = {e: [] for e in self.ENG}
        self.phase_names = []
        for e in self.ENG:
            self._mk(e)
        self._mk("bar")

    def _mk(self, key):
        self.sems[key] = self.nc.alloc_semaphore(name="s_" + key)
        self.cnt[key] = 0

    def tok(self, name="", persist=False):
        t = Tok(name, persist)
        self.toks.append(t)
        return t

    def toks_n(self, n, name=""):
        return [self.tok(f"{name}{i}") for i in range(n)]

    def name(self, base):
        self.uid += 1
        return f"{base}_{self.uid}"

    def _wait(self, eng, deps):
        for key, val in deps:
            if key == "pe" and eng == "pe":
                continue
            if self.seen[eng].get(key, 0) >= val:
                continue
            self.e[eng].wait_ge(self.sems[key], val)
            self.log[eng].append(("w", key, val))
            self.seen[eng][key] = val

    def check(self):
        val = {key: 0 for key in self.cnt}
        pc = {e: 0 for e in self.ENG}
        progress = True
        while progress:
            progress = False
            for e in self.ENG:
                lg = self.log[e]
                while pc[e] < len(lg):
                    ev = lg[pc[e]]
                    if ev[0] == "w":
                        if val[ev[1]] >= ev[2]:
                            pc[e] += 1
                            progress = True
                        else:
                            break
                    else:
                        val[ev[1]] += ev[2]
                        pc[e] += 1
                        progress = True
        bad = {e: (pc[e], len(self.log[e]), self.log[e][pc[e]], val[self.log[e][pc[e]][1]]) for e in self.ENG if pc[e] < len(self.log[e])}
        return bad

    @staticmethod
    def _deps(r, w):
        d = []
        for b in r:
            if b.w is not None:
                d.append(b.w)
        for b in w:
            if b.w is not None:
                d.append(b.w)
            d.extend(b.r.items())
        return d

    def op(self, eng, fn, r=(), w=(), inc=True):
        self._wait(eng, self._deps(r, w))
        ins = fn(self.e[eng])
        if inc:
            self.cnt[eng] += 1
            ins.then_inc(self.sems[eng], 1)
            self.log[eng].append(("i", eng, 1))
            tag = (eng, self.cnt[eng])
        else:
            tag = (eng, self.cnt[eng] + 1)
        for b in w:
            b.w = tag
            b.r = {}
        for b in r:
            if b.r.get(tag[0], 0) < tag[1]:
                b.r[tag[0]] = tag[1]
        return ins

    def dma(self, q, out, in_, r=(), w=(), st=None, **kw):
        self._wait(q, self._deps(r, w))
        if st.dsem is None:
            if self.free_dsems:
                st.dsem = self.free_dsems.pop()
            else:
                self.uid += 1
                st.dsem = f"d{self.uid}"
                self._mk(st.dsem)
            if st.persist:
                self.bgkeys.add(st.dsem)
        key = st.dsem
        self.cnt[key] += 16
        ins = self.e[q].dma_start(out=out, in_=in_, **kw)
        ins.then_inc(self.sems[key], 16)
        self.log[q].append(("i", key, 16))
        tag = (key, self.cnt[key])
        for b in w:
            b.w = tag
            b.r = {}
        for b in r:
            if b.r.get(key, 0) < tag[1]:
                b.r[key] = tag[1]
        return ins

    def barrier(self, name=""):
        self.phase_names.append(name)
        sp = self.e["sp"]
        for key in list(self.cnt.keys()):
            if key in ("sp", "bar") or key in self.bgkeys:
                continue
            if self.seen["sp"].get(key, 0) < self.cnt[key]:
                sp.wait_ge(self.sems[key], self.cnt[key])
                self.log["sp"].append(("w", key, self.cnt[key]))
                self.seen["sp"][key] = self.cnt[key]
        self.cnt["bar"] += 1
        sp.sem_inc(self.sems["bar"], 1)
        self.log["sp"].append(("i", "bar", 1))
        for e in self.ENG:
            if e != "sp":
                self.e[e].wait_ge(self.sems["bar"], self.cnt["bar"])
                self.log[e].append(("w", "bar", self.cnt["bar"]))
            for key in self.cnt:
                if key in self.bgkeys:
                    continue
                self.seen[e][key] = self.cnt[key]
        keep = []
        for t in self.toks:
            if t.persist:
                keep.append(t)
            else:
                t.w = None
                t.r = {}
                if t.dsem is not None:
                    self.free_dsems.append(t.dsem)
                    t.dsem = None
        self.toks = keep


def build(S=4096, depth=4, debug_stop=None):
    nc = bass.Bass("TRN2", target_bir_lowering=False)
    NTB = S // 128
    n_fox = (depth + 1) // 2
    n_dil = depth // 2

    def din(name, shape, dt=F32):
        return nc.dram_tensor(name, list(shape), dt, kind="ExternalInput")

    x_in = din("x", [S, D]).ap()
    c_in = din("c", [DC, 128]).ap()
    pos_in = din("positions", [1, S], I32).ap()
    mod_w = din("mod_w", [depth, D, 9 * D]).ap()
    mod_b = din("mod_b", [depth * 72, 128]).ap()
    norm_g = din("norm_g", [depth * 24, 128]).ap()
    w_gate = din("ffn_w_gate", [depth, 2, D, DFF]).ap()
    w_up = din("ffn_w_up", [depth, 2, D, DFF]).ap()
    w_down = din("ffn_w_down", [depth, 2, DFF, D]).ap()
    fox_w_in = din("fox_w_in", [max(n_fox, 1), D, FOX_IN]).ap()
    fox_b_f = din("fox_b_f", [max(n_fox, 1), NH]).ap()
    fox_q_g = din("fox_q_g", [max(n_fox, 1), HD]).ap()
    fox_k_g = din("fox_k_g", [max(n_fox, 1), HD]).ap()
    fox_w_out = din("fox_w_out", [max(n_fox, 1), D, D]).ap()
    dil_w_in = din("dil_w_in", [max(n_dil, 1), D, DIL_IN]).ap()
    dil_q_g = din("dil_q_g", [max(n_dil, 1), 3, HD]).ap()
    dil_k_g = din("dil_k_g", [max(n_dil, 1), 3, HD]).ap()
    dil_w_out = din("dil_w_out", [max(n_dil, 1), D, D]).ap()
    cf_in = din("cf", [128, NCF]).ap()
    y_out = nc.dram_tensor("y", [S, D], F32, kind="ExternalOutput").ap()

    def dscr(name, shape, dt):
        if debug_stop is not None and not name.startswith("w"):
            return nc.dram_tensor(name, list(shape), dt, kind="ExternalOutput")
        return nc.dram_tensor(name, list(shape), dt)

    xT_h = dscr("xT_s", [D, S], F32)
    xT = xT_h.ap()
    hT = dscr("hT_s", [D, S], BF16).ap()
    qk_s = [[dscr(f"qk_s{g}_{a}", [D, S], BF16).ap() for a in range(2)] for g in range(3)]
    aug_s = [dscr(f"aug_s{a}", [NH, 6, S], BF16).ap() for a in range(2)]
    oun_o = [dscr(f"oun_o{g}", [D, S], F32).ap() for g in range(3)]
    oun_d_h = [dscr(f"oun_d{g}", [NH, S], F32) for g in range(3)]
    oun_d = [h.ap() for h in oun_d_h]
    cs_s = [dscr(f"cs_s{a}", [128, S], F32).ap() for a in range(2)]
    wg_s = [[dscr(f"wg_s{i}_{s}", [D, DFF], BF16).ap() for s in range(2)] for i in range(depth)]
    wu_s = [[dscr(f"wu_s{i}_{s}", [D, DFF], BF16).ap() for s in range(2)] for i in range(depth)]
    wd_s = [[dscr(f"wd_s{i}_{s}", [DFF, D], BF16).ap() for s in range(2)] for i in range(depth)]
    win_s = [dscr(f"win_s{i}", [D, FOX_IN if i % 2 == 0 else DIL_IN], BF16).ap() for i in range(depth)]
    wout_s = [dscr(f"wout_s{i}", [D, D], BF16).ap() for i in range(depth)]

    k = K(nc)
    top = ExitStack()

    def sb(st, name, shape, dt):
        return st.enter_context(nc.sbuf_tensor(k.name(name), list(shape), dt))

    def ps(st, name, shape=(128, 512), dt=F32):
        return st.enter_context(nc.psum_tensor(k.name(name), list(shape), dt))

    cf = sb(top, "cf", [128, NCF], F32)
    cb = sb(top, "cb", [128, NCF], BF16)
    modT = sb(top, "modT", [128, depth * 72], F32)
    ngT = sb(top, "ngT", [128, depth * 24], F32)
    mA = sb(top, "mA", [128, depth * 24], F32)
    mG = sb(top, "mG", [128, depth * 24], F32)
    t_cf = k.tok("cf", True)
    t_mod = k.tok("mod", True)

    ident = cf[:, C_IDENT:C_IDENT + 128]
    ones_f = cf[:, C_ONES:C_ONES + 128]
    blk_f = cf[:, C_BLK:C_BLK + 128]
    utri_f = cf[:, C_UTRI:C_UTRI + 128]
    eps_col = cf[:, C_EPS:C_EPS + 1]
    ones_b = cb[:, C_ONES:C_ONES + 128]
    blk_b = cb[:, C_BLK:C_BLK + 128]

    conv_tok = {}

    def conv(key, dst, src, rows, cols):
        t = conv_tok.setdefault(key, k.tok("cv" + key, True))
        a = src.rearrange("(p a) c -> p (a c)", p=128)
        b = dst.rearrange("(p a) c -> p (a c)", p=128)
        n = (rows // 128) * cols
        step = 8192
        for o in range(0, n, step):
            e = min(n, o + step)
            k.dma("pool", out=b[:, o:e], in_=a[:, o:e], st=t)
            t.w = (t.dsem, k.cnt[t.dsem])

    k.dma("sp", out=cf[:, :], in_=cf_in[:, :], w=[t_cf], st=t_cf)
    k.dma("pool", out=cb[:, :], in_=cf_in[:, :], w=[t_cf], st=t_cf)
    conv_jobs = []
    for i in range(depth):
        j = i // 2
        conv_jobs.append([(f"f{i}0", wg_s[i][0], w_gate[i, 0], D, DFF), (f"f{i}0", wu_s[i][0], w_up[i, 0], D, DFF),
                          (f"f{i}0", wd_s[i][0], w_down[i, 0], DFF, D)])
        if i % 2 == 0:
            conv_jobs.append([(f"a{i}", win_s[i], fox_w_in[j], D, FOX_IN), (f"a{i}", wout_s[i], fox_w_out[j], D, D)])
        else:
            conv_jobs.append([(f"a{i}", win_s[i], dil_w_in[j], D, DIL_IN), (f"a{i}", wout_s[i], dil_w_out[j], D, D)])
        conv_jobs.append([(f"f{i}1", wg_s[i][1], w_gate[i, 1], D, DFF), (f"f{i}1", wu_s[i][1], w_up[i, 1], D, DFF),
                          (f"f{i}1", wd_s[i][1], w_down[i, 1], DFF, D)])
    conv_next = [0]

    def conv_upto(n):
        while conv_next[0] < min(n, len(conv_jobs)):
            for job in conv_jobs[conv_next[0]]:
                conv(*job)
            conv_next[0] += 1

    conv_upto(1)

    def setup_mods():
        with ExitStack() as st:
            craw = sb(st, "craw", [DC, 128], F32)
            cT = sb(st, "cT", [128, DC], F32)
            mb = sb(st, "mb", [72, 128], F32)
            ng = sb(st, "ng", [24, 128], F32)
            NMW = 4
            wbuf = [sb(st, "mw", [128, DC, 512], F32) for _ in range(NMW)]
            pmod = ps(st, "pmod")
            pmisc = ps(st, "pmisc")
            prow = [ps(st, "prow") for _ in range(2)]
            t_prow = k.toks_n(2)
            row = sb(st, "mrow", [1, 9 * D], F32)
            t_row = k.tok()
            t_c, t_cT, t_mb, t_ng, t_pm, t_pmisc = (k.tok() for _ in range(6))
            t_w = k.toks_n(NMW)
            k.dma("sp", out=craw[:, :], in_=c_in[:, :], w=[t_c], st=t_c)
            k.op("pe", lambda e: e.matmul(pmisc[:, 0:DC], lhsT=craw[:, :], rhs=ident[0:DC, 0:DC], start=True, stop=True),
                 r=[t_c, t_cf], w=[t_pmisc])
            k.op("act", lambda e: e.activation(out=cT[:, :], in_=pmisc[:, 0:DC], func=AF.Silu), r=[t_pmisc], w=[t_cT])
            for i in range(depth):
                k.dma("sp", out=mb[:, :], in_=mod_b[i * 72:(i + 1) * 72, :], w=[t_mb], st=t_mb)
                k.dma("sp", out=ng[:, :], in_=norm_g[i * 24:(i + 1) * 24, :], w=[t_ng], st=t_ng)
                for jg in range(18):
                    n = i * 18 + jg
                    wb = wbuf[n % NMW]
                    k.dma("sp" if n % 2 == 0 else "act", out=wb[:, :, :],
                          in_=mod_w[i].rearrange("(kc p) f -> p kc f", p=128)[:, :, jg * 512:(jg + 1) * 512],
                          w=[t_w[n % NMW]], st=t_w[n % NMW])
                    pr = prow[n % 2]
                    for kc in range(DC):
                        k.op("pe", lambda e: e.matmul(pr[0:1, :], lhsT=cT[:, kc:kc + 1], rhs=wb[:, kc, :],
                                                      start=(kc == 0), stop=(kc == DC - 1)),
                             r=[t_w[n % NMW], t_cT], w=[t_prow[n % 2]], inc=(kc == DC - 1))
                    k.op("act", lambda e: e.activation(out=row[0:1, jg * 512:(jg + 1) * 512], in_=pr[0:1, :], func=AF.Identity),
                         r=[t_prow[n % 2]], w=[t_row])
                for col in range(72):
                    k.op("pe", lambda e: e.matmul(pmod[:, col:col + 1], lhsT=row[0:1, col * 128:(col + 1) * 128], rhs=ones_f[0:1, 0:1],
                                                  start=True, stop=True), r=[t_row, t_cf], w=[t_pm], inc=(col == 71))
                k.op("pe", lambda e: e.matmul(pmisc[:, 0:72], lhsT=mb[:, :], rhs=ident[0:72, 0:72], start=True, stop=True),
                     r=[t_mb, t_cf], w=[t_pmisc])
                k.op("act", lambda e: e.activation(out=modT[:, i * 72:(i + 1) * 72], in_=pmisc[:, 0:72], func=AF.Identity),
                     r=[t_pmisc], w=[t_mod])
                k.op("dve", lambda e: e.tensor_tensor(out=modT[:, i * 72:(i + 1) * 72], in0=modT[:, i * 72:(i + 1) * 72],
                                                      in1=pmod[:, 0:72], op=ALU.add), r=[t_pm, t_mod], w=[t_mod])
                k.op("pe", lambda e: e.matmul(pmisc[:, 0:24], lhsT=ng[:, :], rhs=ident[0:24, 0:24], start=True, stop=True),
                     r=[t_ng, t_cf], w=[t_pmisc])
                k.op("act", lambda e: e.activation(out=ngT[:, i * 24:(i + 1) * 24], in_=pmisc[:, 0:24], func=AF.Identity),
                     r=[t_pmisc], w=[t_mod])
                for s in range(3):
                    base = i * 72 + s * 24
                    o = i * 24 + s * 8
                    k.op("dve", lambda e: e.scalar_tensor_tensor(out=mA[:, o:o + 8], in0=modT[:, base + 8:base + 16], scalar=1.0,
                                                                 in1=ngT[:, o:o + 8], op0=ALU.add, op1=ALU.mult),
                         r=[t_mod], w=[t_mod])
                    gsc = 1.0 if s == 1 else 0.5
                    k.op("dve", lambda e: e.tensor_scalar(out=mG[:, o:o + 8], in0=modT[:, base + 16:base + 24], scalar1=gsc, scalar2=None,
                                                          op0=ALU.mult), r=[t_mod], w=[t_mod])
        k.barrier("setup_mods")

    def shift_col(i, s, dc):
        c = i * 72 + s * 24 + dc
        return modT[:, c:c + 1]

    def a_col(i, s, dc):
        c = i * 24 + s * 8 + dc
        return mA[:, c:c + 1]

    def g_col(i, s, dc):
        c = i * 24 + s * 8 + dc
        return mG[:, c:c + 1]

    def transpose_in():
        with ExitStack() as st:
            xin = [sb(st, "xin", [128, D], F32) for _ in range(2)]
            xst = [sb(st, "xst", [128, DC, 512], F32) for _ in range(2)]
            pt = [ps(st, "ptr") for _ in range(4)]
            t_xin = k.toks_n(2)
            t_xst = k.toks_n(2)
            t_pt = k.toks_n(4)
            n = 0
            for tb in range(NTB):
                xb = xin[tb % 2]
                k.dma("sp", out=xb[:, :], in_=x_in[tb * 128:(tb + 1) * 128, :], w=[t_xin[tb % 2]], st=t_xin[tb % 2])
                g4 = tb // 4
                sbuf = xst[g4 % 2]
                for dg in range(2):
                    p = pt[n % 4]
                    tp = t_pt[n % 4]
                    for j in range(4):
                        dc = dg * 4 + j
                        k.op("pe", lambda e: e.matmul(p[:, j * 128:(j + 1) * 128], lhsT=xb[:, dc * 128:(dc + 1) * 128], rhs=ident,
                                                      start=True, stop=True), r=[t_xin[tb % 2], t_cf], w=[tp], inc=(j == 3))
                    eng = "act" if n % 2 == 0 else "dve"
                    dst = sbuf[:, dg * 4:(dg + 1) * 4, (tb % 4) * 128:(tb % 4 + 1) * 128]
                    src = p[:, :].rearrange("p (j t) -> p j t", j=4)
                    if eng == "act":
                        k.op("act", lambda e: e.activation(out=dst, in_=src, func=AF.Identity), r=[tp], w=[t_xst[g4 % 2]])
                    else:
                        k.op("dve", lambda e: e.tensor_copy(out=dst, in_=src), r=[tp], w=[t_xst[g4 % 2]])
                    n += 1
                if tb % 4 == 3:
                    k.dma("pool", out=xT.rearrange("(dc p) s -> p dc s", p=128)[:, :, g4 * 512:(g4 + 1) * 512], in_=sbuf[:, :, :],
                          r=[t_xst[g4 % 2]], st=t_xst[g4 % 2])
        k.barrier("transpose_in")

    def transpose_out():
        with ExitStack() as st:
            xt = [sb(st, "xo", [128, DC, 512], F32) for _ in range(2)]
            yst = [sb(st, "yst", [128, D], F32) for _ in range(2)]
            pt = [ps(st, "pto") for _ in range(4)]
            t_xt = k.toks_n(2)
            t_y = k.toks_n(2)
            t_pt = k.toks_n(4)
            n = 0
            for g4 in range(S // 512):
                xb = xt[g4 % 2]
                k.dma("sp", out=xb[:, :, :], in_=xT.rearrange("(dc p) s -> p dc s", p=128)[:, :, g4 * 512:(g4 + 1) * 512],
                      w=[t_xt[g4 % 2]], st=t_xt[g4 % 2])
                for b4 in range(4):
                    tb = g4 * 4 + b4
                    yb = yst[tb % 2]
                    for dg in range(2):
                        p = pt[n % 4]
                        tp = t_pt[n % 4]
                        for j in range(4):
                            dc = dg * 4 + j
                            k.op("pe", lambda e: e.matmul(p[:, j * 128:(j + 1) * 128], lhsT=xb[:, dc, b4 * 128:(b4 + 1) * 128], rhs=ident,
                                                          start=True, stop=True), r=[t_xt[g4 % 2], t_cf], w=[tp], inc=(j == 3))
                        dst = yb[:, dg * 512:(dg + 1) * 512]
                        if n % 2 == 0:
                            k.op("act", lambda e: e.activation(out=dst, in_=p[:, :], func=AF.Identity), r=[tp], w=[t_y[tb % 2]])
                        else:
                            k.op("dve", lambda e: e.tensor_copy(out=dst, in_=p[:, :]), r=[tp], w=[t_y[tb % 2]])
                        n += 1
                    k.dma("pool", out=y_out[tb * 128:(tb + 1) * 128, :], in_=yb[:, :], r=[t_y[tb % 2]], st=t_y[tb % 2])
        k.barrier("transpose_out")

    class NormRes:
        def __init__(self, st, T):
            self.T = T
            self.sq = [sb(st, "sq", [128, 512], BF16) for _ in range(3)]
            self.t_sq = k.toks_n(3)
            self.rstd = sb(st, "rstd", [128, T], F32)
            self.t_rstd = k.tok()
            self.tmp = [sb(st, "ntmp", [128, T], F32) for _ in range(2)]
            self.t_tmp = k.toks_n(2)
            self.pss = ps(st, "pss")
            self.t_pss = k.tok()
            self.n = 0

    def norm_tile(nr, xb, t_x, i, s, hdst, t_h):
        T = nr.T
        for hf in range(T // 512):
            for dc in range(DC):
                q = nr.n % 3
                nr.n += 1
                k.op("act", lambda e: e.activation(out=nr.sq[q][:, :], in_=xb[:, dc, hf * 512:(hf + 1) * 512], func=AF.Square),
                     r=[t_x], w=[nr.t_sq[q]])
                k.op("pe", lambda e: e.matmul(nr.pss[:, :], lhsT=ones_b, rhs=nr.sq[q][:, :], start=(dc == 0), stop=(dc == DC - 1)),
                     r=[nr.t_sq[q], t_cf], w=[nr.t_pss])
            rs_ = nr.rstd[:, hf * 512:(hf + 1) * 512]
            k.op("act", lambda e: e.activation(out=rs_, in_=nr.pss[:, :], func=AF.Ln, scale=1.0 / D, bias=eps_col),
                 r=[nr.t_pss, t_cf], w=[nr.t_rstd])
            k.op("act", lambda e: e.activation(out=rs_, in_=rs_, func=AF.Exp, scale=-0.5), r=[nr.t_rstd], w=[nr.t_rstd])
        for dc in range(DC):
            q = dc % 2
            k.op("dve", lambda e: e.tensor_tensor(out=nr.tmp[q][:, :], in0=xb[:, dc, :], in1=nr.rstd[:, :], op=ALU.mult),
                 r=[t_x, nr.t_rstd], w=[nr.t_tmp[q]])
            k.op("act", lambda e: e.activation(out=hdst(dc), in_=nr.tmp[q][:, :], func=AF.Identity, scale=a_col(i, s, dc),
                                               bias=shift_col(i, s, dc)), r=[nr.t_tmp[q], t_mod], w=[t_h])

    def ffn_phase(i, s):
        sl = 0 if s == 0 else 2
        T = 1024 if S % 1024 == 0 else 512
        NH2 = T // 512
        cv = conv_tok[f"f{i}{s}"]
        wgv = wg_s[i][s].rearrange("(kc p) f -> p kc f", p=128)
        wuv = wu_s[i][s].rearrange("(kc p) f -> p kc f", p=128)
        wdv = wd_s[i][s].rearrange("(fc p) d -> p fc d", p=128)
        xTv = xT.rearrange("(dc p) s -> p dc s", p=128)
        with ExitStack() as st:
            xt = [sb(st, "fx", [128, DC, T], F32) for _ in range(2)]
            t_x = k.toks_n(2)
            nr = NormRes(st, T)
            hT_sb = sb(st, "fh", [128, DC, T], BF16)
            t_h = k.tok()
            aT = sb(st, "fa", [128, FC, T], BF16)
            t_a = k.tok()
            NWB = 3
            wg = [sb(st, "fwg", [128, DC, 256], BF16) for _ in range(NWB)]
            wu = [sb(st, "fwu", [128, DC, 256], BF16) for _ in range(NWB)]
            t_wg = k.toks_n(NWB)
            t_wu = k.toks_n(NWB)
            wd = [sb(st, "fwd", [128, FC, 128], BF16) for _ in range(2)]
            t_wd = k.toks_n(2)
            sg = [sb(st, "fsg", [128, 512], F32) for _ in range(2)]
            t_sg = k.toks_n(2)
            pg = [ps(st, "pg") for _ in range(2)]
            pu = [ps(st, "pu") for _ in range(2)]
            t_pg = k.toks_n(2)
            t_pu = k.toks_n(2)
            po = [ps(st, "po") for _ in range(2)]
            t_po = k.toks_n(2)
            NT = S // T
            NFG = FC // 2
            wcount = [0]

            def load_w(fg):
                q = wcount[0] % NWB
                wcount[0] += 1
                k.dma("sp", out=wg[q][:, :, :], in_=wgv[:, :, fg * 256:(fg + 1) * 256], r=[cv], w=[t_wg[q]], st=t_wg[q])
                k.dma("sp", out=wu[q][:, :, :], in_=wuv[:, :, fg * 256:(fg + 1) * 256], r=[cv], w=[t_wu[q]], st=t_wu[q])
                return q

            dcount = [0]

            def load_wd(dc):
                q = dcount[0] % 2
                dcount[0] += 1
                k.dma("sp", out=wd[q][:, :, :], in_=wdv[:, :, dc * 128:(dc + 1) * 128], r=[cv], w=[t_wd[q]], st=t_wd[q])
                return q

            k.dma("sp", out=xt[0][:, :, :], in_=xTv[:, :, 0:T], w=[t_x[0]], st=t_x[0])
            n1 = 0
            n2 = 0
            for t in range(NT):
                xb = xt[t % 2]
                tx = t_x[t % 2]
                wq = [load_w(0), load_w(1)]
                if t + 1 < NT:
                    k.dma("sp", out=xt[(t + 1) % 2][:, :, :], in_=xTv[:, :, (t + 1) * T:(t + 2) * T], w=[t_x[(t + 1) % 2]],
                          st=t_x[(t + 1) % 2])
                norm_tile(nr, xb, tx, i, sl, lambda dc: hT_sb[:, dc, :], t_h)
                for fg in range(NFG):
                    if fg + 2 < NFG:
                        wq.append(load_w(fg + 2))
                    q = wq[fg]
                    for fl in range(2):
                        fc = fg * 2 + fl
                        for hf in range(NH2):
                            b = n1 % 2
                            n1 += 1
                            cs = slice(hf * 512, (hf + 1) * 512)
                            for kc in range(DC):
                                k.op("pe", lambda e: e.matmul(pg[b][:, :], lhsT=wg[q][:, kc, fl * 128:(fl + 1) * 128], rhs=hT_sb[:, kc, cs],
                                                              start=(kc == 0), stop=(kc == DC - 1)),
                                     r=[t_wg[q], t_h], w=[t_pg[b]], inc=(kc == DC - 1))
                            for kc in range(DC):
                                k.op("pe", lambda e: e.matmul(pu[b][:, :], lhsT=wu[q][:, kc, fl * 128:(fl + 1) * 128], rhs=hT_sb[:, kc, cs],
                                                              start=(kc == 0), stop=(kc == DC - 1)),
                                     r=[t_wu[q], t_h], w=[t_pu[b]], inc=(kc == DC - 1))
                            k.op("act", lambda e: e.activation(out=sg[b][:, :], in_=pg[b][:, :], func=AF.Silu), r=[t_pg[b]], w=[t_sg[b]])
                            k.op("dve", lambda e: e.tensor_tensor(out=aT[:, fc, cs], in0=sg[b][:, :], in1=pu[b][:, :], op=ALU.mult),
                                 r=[t_sg[b], t_pu[b]], w=[t_a])
                dq = [load_wd(0), load_wd(1)]
                for dc in range(DC):
                    q = dq[dc]
                    for hf in range(NH2):
                        b = n2 % 2
                        n2 += 1
                        cs = slice(hf * 512, (hf + 1) * 512)
                        for fc in range(FC):
                            k.op("pe", lambda e: e.matmul(po[b][:, :], lhsT=wd[q][:, fc, :], rhs=aT[:, fc, cs],
                                                          start=(fc == 0), stop=(fc == FC - 1)),
                                 r=[t_wd[q], t_a], w=[t_po[b]], inc=(fc == FC - 1))
                        k.op("dve", lambda e: e.scalar_tensor_tensor(out=xb[:, dc, cs], in0=po[b][:, :], scalar=g_col(i, sl, dc),
                                                                     in1=xb[:, dc, cs], op0=ALU.mult, op1=ALU.add),
                             r=[t_po[b], t_mod, tx], w=[tx])
                    if dc + 2 < DC:
                        dq.append(load_wd(dc + 2))
                k.dma("pool", out=xTv[:, :, t * T:(t + 1) * T], in_=xb[:, :, :], r=[tx], st=tx)
        k.barrier("load_wd")


    hTv = hT.rearrange("(kc p) s -> p kc s", p=128)
    xTv_g = xT.rearrange("(dc p) s -> p dc s", p=128)
    mdiag_b = cb[:, C_MDIAG:C_MDIAG + 128]
    mneg_b = cb[:, C_MNEG:C_MNEG + 128]
    ident_b = cb[:, C_IDENT:C_IDENT + 128]
    moff_b = cb[:, C_MOFF:C_MOFF + 128]
    perm_b = cb[:, C_PERM:C_PERM + 128]
    ones_col = cf[:, C_ONES:C_ONES + 1]

    def stop_at(name):
        if debug_stop == name:
            raise StopBuild()

    def h_phase(i):
        T = 1024 if S % 1024 == 0 else 512
        with ExitStack() as st:
            xt = [sb(st, "hx", [128, DC, T], F32) for _ in range(2)]
            t_x = k.toks_n(2)
            nr = NormRes(st, T)
            hs = [sb(st, "hh", [128, DC, T], BF16) for _ in range(2)]
            t_hs = k.toks_n(2)
            NT = S // T
            k.dma("sp", out=xt[0][:, :, :], in_=xTv_g[:, :, 0:T], w=[t_x[0]], st=t_x[0])
            for t in range(NT):
                if t + 1 < NT:
                    k.dma("sp", out=xt[(t + 1) % 2][:, :, :], in_=xTv_g[:, :, (t + 1) * T:(t + 2) * T], w=[t_x[(t + 1) % 2]],
                          st=t_x[(t + 1) % 2])
                hb = hs[t % 2]
                norm_tile(nr, xt[t % 2], t_x[t % 2], i, 1, lambda dc: hb[:, dc, :], t_hs[t % 2])
                k.dma("pool", out=hTv[:, :, t * T:(t + 1) * T], in_=hb[:, :, :], r=[t_hs[t % 2]], st=t_hs[t % 2])
        k.barrier("h_phase")

    def rope_tables():
        TWO_PI = 2.0 * math.pi
        C1 = 6.28125
        C2 = TWO_PI - C1
        W = 2048 if S >= 2048 else S
        with ExitStack() as st:
            pi_ = sb(st, "rp_i", [128, W], I32)
            ang = sb(st, "rp_f", [128, W], F32)
            kf = sb(st, "rp_kf", [128, W], F32)
            m = sb(st, "rp_m", [128, W], F32)
            m2 = sb(st, "rp_m2", [128, W], F32)
            gt = sb(st, "rp_gt", [128, W], F32)
            t_pi, t_ang, t_kf, t_m, t_m2, t_gt = (k.tok() for _ in range(6))

            def wrap(buf, tb):
                k.op("dve", lambda e: e.tensor_scalar(out=gt[:, :], in0=buf[:, :], scalar1=math.pi, scalar2=None, op0=ALU.is_gt),
                     r=[tb], w=[t_gt])
                k.op("dve", lambda e: e.scalar_tensor_tensor(out=buf[:, :], in0=gt[:, :], scalar=-TWO_PI, in1=buf[:, :],
                                                             op0=ALU.mult, op1=ALU.add), r=[t_gt, tb], w=[tb])
                k.op("dve", lambda e: e.tensor_scalar(out=gt[:, :], in0=buf[:, :], scalar1=-math.pi, scalar2=None, op0=ALU.is_lt),
                     r=[tb], w=[t_gt])
                k.op("dve", lambda e: e.scalar_tensor_tensor(out=buf[:, :], in0=gt[:, :], scalar=TWO_PI, in1=buf[:, :],
                                                             op0=ALU.mult, op1=ALU.add), r=[t_gt, tb], w=[tb])
                k.op("dve", lambda e: e.tensor_scalar(out=buf[:, :], in0=buf[:, :], scalar1=-math.pi, scalar2=math.pi,
                                                      op0=ALU.max, op1=ALU.min), r=[tb], w=[tb])

            for hf in range(S // W):
                cs = slice(hf * W, (hf + 1) * W)
                k.dma("sp", out=pi_[:, :], in_=pos_in[0:1, cs].partition_broadcast(128), w=[t_pi], st=t_pi)
                k.op("dve", lambda e: e.tensor_copy(out=ang[:, :], in_=pi_[:, :]), r=[t_pi], w=[t_ang])
                k.op("dve", lambda e: e.tensor_scalar(out=ang[:, :], in0=ang[:, :], scalar1=cf[:, C_ROPE:C_ROPE + 1], scalar2=None,
                                                      op0=ALU.mult), r=[t_ang, t_cf], w=[t_ang])
                k.op("dve", lambda e: e.tensor_scalar(out=kf[:, :], in0=ang[:, :], scalar1=1.0 / TWO_PI, scalar2=None, op0=ALU.mult),
                     r=[t_ang], w=[t_kf])
                k.op("dve", lambda e: e.tensor_copy(out=pi_[:, :], in_=kf[:, :]), r=[t_kf, t_pi], w=[t_pi])
                k.op("dve", lambda e: e.tensor_copy(out=kf[:, :], in_=pi_[:, :]), r=[t_pi], w=[t_kf])
                k.op("dve", lambda e: e.scalar_tensor_tensor(out=m[:, :], in0=kf[:, :], scalar=-C1, in1=ang[:, :], op0=ALU.mult, op1=ALU.add),
                     r=[t_kf, t_ang], w=[t_m])
                k.op("dve", lambda e: e.scalar_tensor_tensor(out=m[:, :], in0=kf[:, :], scalar=-C2, in1=m[:, :], op0=ALU.mult, op1=ALU.add),
                     r=[t_kf, t_m], w=[t_m])
                wrap(m, t_m)
                k.op("dve", lambda e: e.tensor_scalar(out=m2[:, :], in0=m[:, :], scalar1=0.5 * math.pi, scalar2=None, op0=ALU.add),
                     r=[t_m], w=[t_m2])
                wrap(m2, t_m2)
                k.op("act", lambda e: e.activation(out=m[:, :], in_=m[:, :], func=AF.Sin), r=[t_m], w=[t_m])
                k.op("dve", lambda e: e.tensor_scalar(out=m[:, :], in0=m[:, :], scalar1=cf[:, C_ROPE + 1:C_ROPE + 2], scalar2=None,
                                                      op0=ALU.mult), r=[t_m, t_cf], w=[t_m])
                k.dma("pool", out=cs_s[1][:, cs], in_=m[:, :], r=[t_m], st=t_m)
                k.op("act", lambda e: e.activation(out=m2[:, :], in_=m2[:, :], func=AF.Sin), r=[t_m2], w=[t_m2])
                k.dma("pool", out=cs_s[0][:, cs], in_=m2[:, :], r=[t_m2], st=t_m2)
        k.barrier("wrap")

    def mixer_phase(i):
        is_fox = (i % 2 == 0)
        j = i // 2
        cv = conv_tok[f"a{i}"]
        winv = win_s[i].rearrange("(kc p) f -> p kc f", p=128)
        dils = [1] if is_fox else [c[1] for c in DIL_CFG]
        NG = len(dils)
        NHALF = S // 2048 if S >= 2048 else 1
        HL = S // NHALF
        h_phase(i)
        stop_at(f"hph{i}")
        for g, d in enumerate(dils):
            nb = S // d // 128
            with ExitStack() as stg_:
                v_all = sb(stg_, "v_all", [128, NTB, NH, HD + 1], BF16)
                t_v = k.toks_n(NTB, "v")
                t_vone = k.tok()
                k.op("pool", lambda e: e.memset(v_all[:, :, :, HD:HD + 1], 1.0), w=[t_vone])
                sp_all = sb(stg_, "sp_all", [128, NTB, NH], F32) if is_fox else None
                t_sp = k.toks_n(NTB, "sp") if is_fox else None
                with ExitStack() as st:
                    qc0 = 0 if is_fox else g * 3 * D
                    wv = sb(st, "wv", [128, DC, D], BF16)
                    t_wv = k.tok()
                    k.dma("sp", out=wv[:, :, :], in_=winv[:, :, qc0 + 2 * D:qc0 + 3 * D], r=[cv], w=[t_wv], st=t_wv)
                    gq = sb(st, "gq", [128, 1], F32)
                    gk = sb(st, "gk", [128, 1], F32)
                    t_g = k.tok()
                    if is_fox:
                        srcq = fox_q_g[j:j + 1, :].rearrange("o d -> d o")
                        srck = fox_k_g[j:j + 1, :].rearrange("o d -> d o")
                    else:
                        srcq = dil_q_g[j, g:g + 1, :].rearrange("o d -> d o")
                        srck = dil_k_g[j, g:g + 1, :].rearrange("o d -> d o")
                    for a in range(2):
                        k.dma("sp", out=gq[a * 64:(a + 1) * 64, :], in_=srcq, w=[t_g], st=t_g)
                        k.dma("sp", out=gk[a * 64:(a + 1) * 64, :], in_=srck, w=[t_g], st=t_g)
                    k.op("dve", lambda e: e.tensor_scalar(out=gq[:, :], in0=gq[:, :], scalar1=0.125, scalar2=None, op0=ALU.mult),
                         r=[t_g], w=[t_g])
                    if is_fox:
                        wf = sb(st, "wf", [128, DC, NH], BF16)
                        bfb = sb(st, "bfb", [128, NH], F32)
                        t_wf = k.tok()
                        k.dma("sp", out=wf[:, :, :], in_=winv[:, :, 3 * D:3 * D + NH], r=[cv], w=[t_wf], st=t_wf)
                        k.dma("sp", out=bfb[:, :], in_=fox_b_f[j:j + 1, :].partition_broadcast(128), w=[t_wf], st=t_wf)
                        zt = [sb(st, "zt", [128, NH], F32) for _ in range(2)]
                        t_zt = k.toks_n(2)
                        pf = ps(st, "pf")
                        t_pf = k.tok()
                    else:
                        ct = sb(st, "ct", [128, HL], F32)
                        stt = sb(st, "stt", [128, HL], F32)
                        t_cs = k.tok()
                        t1 = [sb(st, "t1", [128, 512], F32) for _ in range(2)]
                        t2 = [sb(st, "t2", [128, 512], F32) for _ in range(2)]
                        t_t1 = k.toks_n(2)
                        t_t2 = k.toks_n(2)
                        pperm = ps(st, "pperm")
                        t_pperm = k.tok()
                    hh = sb(st, "hhalf", [128, DC, HL], BF16)
                    t_hh = k.tok()
                    wqk = [sb(st, "wqk", [128, DC, 128], BF16) for _ in range(3)]
                    t_wqk = k.toks_n(3)
                    stg = [sb(st, "stg", [128, HL], BF16) for _ in range(2)]
                    t_stg = k.toks_n(2)
                    NB = 3
                    sq = [sb(st, "bsq", [128, 512], BF16) for _ in range(NB)]
                    t_sq = k.toks_n(NB)
                    rs = [sb(st, "brs", [128, 512], F32) for _ in range(NB)]
                    t_rs = k.toks_n(NB)
                    pq = [ps(st, "pq") for _ in range(NB)]
                    t_pq = k.toks_n(NB)
                    pssq = [ps(st, "pssq") for _ in range(2)]
                    t_pssq = k.toks_n(2)
                    pv = pq[0:2]
                    t_pvp = t_pq[0:2]
                    if not is_fox:
                        qn = [sb(st, "qn", [128, 512], F32) for _ in range(NB)]
                        qnb = [sb(st, "qnb", [128, 512], BF16) for _ in range(NB)]
                        t_qn = k.toks_n(NB)
                        t_qnb = k.toks_n(NB)
                        pperm2 = [pperm, ps(st, "pperm2")]
                        t_pperm2 = [t_pperm, k.tok()]
                    NTT = HL // 512
                    for hf in range(NHALF):
                        k.dma("sp", out=hh[:, :, :], in_=hTv[:, :, hf * HL:(hf + 1) * HL], w=[t_hh], st=t_hh)
                        if not is_fox:
                            k.dma("sp", out=ct[:, :], in_=cs_s[0][:, hf * HL:(hf + 1) * HL], w=[t_cs], st=t_cs)
                            k.dma("sp", out=stt[:, :], in_=cs_s[1][:, hf * HL:(hf + 1) * HL], w=[t_cs], st=t_cs)
                        items = [(c, tt) for c in range(16) for tt in range(NTT)]
                        NI = len(items)

                        def wload(c):
                            a = c // 8
                            cc = c % 8
                            col0 = qc0 + a * D + cc * 128
                            k.dma("sp", out=wqk[c % 3][:, :, :], in_=winv[:, :, col0:col0 + 128], r=[cv], w=[t_wqk[c % 3]], st=t_wqk[c % 3])

                        def stage_a(n):
                            c, tt = items[n]
                            if tt == 0 and c + 1 < 16:
                                wload(c + 1)
                            b = n % NB
                            cs = slice(tt * 512, (tt + 1) * 512)
                            for kc in range(DC):
                                k.op("pe", lambda e: e.matmul(pq[b][:, :], lhsT=wqk[c % 3][:, kc, :], rhs=hh[:, kc, cs],
                                                              start=(kc == 0), stop=(kc == DC - 1)),
                                     r=[t_wqk[c % 3], t_hh], w=[t_pq[b]], inc=(kc == DC - 1))
                            k.op("act", lambda e: e.activation(out=sq[b][:, :], in_=pq[b][:, :], func=AF.Square), r=[t_pq[b]], w=[t_sq[b]])

                        def stage_b(n):
                            c, tt = items[n]
                            b = n % NB
                            b2 = n % 2
                            sg_ = c % 2
                            cs = slice(tt * 512, (tt + 1) * 512)
                            gain = gq if c < 8 else gk
                            k.op("pe", lambda e: e.matmul(pssq[b2][:, :], lhsT=blk_b, rhs=sq[b][:, :], start=True, stop=True),
                                 r=[t_sq[b], t_cf], w=[t_pssq[b2]])
                            k.op("act", lambda e: e.activation(out=rs[b][:, :], in_=pssq[b2][:, :], func=AF.Ln, scale=1.0 / HD, bias=eps_col),
                                 r=[t_pssq[b2], t_cf], w=[t_rs[b]])
                            k.op("act", lambda e: e.activation(out=rs[b][:, :], in_=rs[b][:, :], func=AF.Exp, scale=-0.5),
                                 r=[t_rs[b]], w=[t_rs[b]])
                            if is_fox:
                                k.op("dve", lambda e: e.scalar_tensor_tensor(out=stg[sg_][:, cs], in0=pq[b][:, :], scalar=gain[:, 0:1],
                                                                             in1=rs[b][:, :], op0=ALU.mult, op1=ALU.mult),
                                     r=[t_pq[b], t_g, t_rs[b]], w=[t_stg[sg_]])
                                if tt == NTT - 1:
                                    store(c)
                            else:
                                k.op("dve", lambda e: e.scalar_tensor_tensor(out=qn[b][:, :], in0=pq[b][:, :], scalar=gain[:, 0:1],
                                                                             in1=rs[b][:, :], op0=ALU.mult, op1=ALU.mult),
                                     r=[t_pq[b], t_g, t_rs[b]], w=[t_qn[b]])
                                k.op("act", lambda e: e.activation(out=qnb[b][:, :], in_=qn[b][:, :], func=AF.Identity),
                                     r=[t_qn[b]], w=[t_qnb[b]])

                        def stage_c(n):
                            c, tt = items[n]
                            b = n % NB
                            b2 = n % 2
                            sg_ = c % 2
                            cs = slice(tt * 512, (tt + 1) * 512)
                            k.op("pe", lambda e: e.matmul(pperm2[b2][:, :], lhsT=perm_b, rhs=qnb[b][:, :], start=True, stop=True),
                                 r=[t_qnb[b], t_cf], w=[t_pperm2[b2]])
                            k.op("pool", lambda e: e.tensor_tensor(out=t1[b2][:, :], in0=qn[b][:, :], in1=ct[:, cs], op=ALU.mult),
                                 r=[t_qn[b], t_cs], w=[t_t1[b2]])
                            k.op("dve", lambda e: e.tensor_tensor(out=t2[b2][:, :], in0=pperm2[b2][:, :], in1=stt[:, cs], op=ALU.mult),
                                 r=[t_pperm2[b2], t_cs], w=[t_t2[b2]])
                            k.op("dve", lambda e: e.tensor_tensor(out=stg[sg_][:, cs], in0=t1[b2][:, :], in1=t2[b2][:, :], op=ALU.add),
                                 r=[t_t1[b2], t_t2[b2]], w=[t_stg[sg_]])
                            if tt == NTT - 1:
                                store(c)

                        def store(c):
                            a = c // 8
                            cc = c % 8
                            sg_ = c % 2
                            k.dma("pool", out=qk_s[g][a][cc * 128:(cc + 1) * 128, hf * HL:(hf + 1) * HL], in_=stg[sg_][:, :],
                                  r=[t_stg[sg_]], st=t_stg[sg_])

                        wload(0)
                        for n in range(NI + 2):
                            if n < NI:
                                stage_a(n)
                            if 0 <= n - 1 < NI:
                                stage_b(n - 1)
                            if (not is_fox) and 0 <= n - 2 < NI:
                                stage_c(n - 2)
                        bph = HL // 128
                        for bi in range(bph):
                            if d * 128 <= HL:
                                spans = HL // (128 * d)
                                sp_i = bi // d if False else None
                            sidx = bi // d
                            r = bi % d
                            jb = hf * (HL // (128 * d)) + sidx
                            blk = r * nb + jb
                            start = sidx * 128 * d + r
                            cols = slice(start, start + 127 * d + 1, d) if d > 1 else slice(start, start + 128)
                            for hv in range(2):
                                for kc in range(DC):
                                    k.op("pe", lambda e: e.matmul(pv[hv][:, :], lhsT=hh[:, kc, cols], rhs=wv[:, kc, hv * 512:(hv + 1) * 512],
                                                                  start=(kc == 0), stop=(kc == DC - 1)),
                                         r=[t_hh, t_wv], w=[t_pvp[hv]], inc=(kc == DC - 1))
                            k.op("act", lambda e: e.activation(out=v_all[:, blk, 0:8, 0:HD], in_=pv[0][:, :].rearrange("p (h e) -> p h e", h=8),
                                                               func=AF.Identity), r=[t_pvp[0]], w=[t_v[blk]])
                            k.op("dve", lambda e: e.tensor_copy(out=v_all[:, blk, 8:16, 0:HD], in_=pv[1][:, :].rearrange("p (h e) -> p h e", h=8)),
                                 r=[t_pvp[1]], w=[t_v[blk]])
                            if is_fox:
                                zb = bi % 2
                                for kc in range(DC):
                                    k.op("pe", lambda e: e.matmul(pf[:, 0:NH], lhsT=hh[:, kc, cols], rhs=wf[:, kc, :],
                                                                  start=(kc == 0), stop=(kc == DC - 1)),
                                         r=[t_hh, t_wf], w=[t_pf], inc=(kc == DC - 1))
                                k.op("dve", lambda e: e.tensor_tensor(out=zt[zb][:, :], in0=pf[:, 0:NH], in1=bfb[:, :], op=ALU.add),
                                     r=[t_pf, t_wf], w=[t_zt[zb]])
                                k.op("act", lambda e: e.activation(out=zt[zb][:, :], in_=zt[zb][:, :], func=AF.Exp, scale=-1.0),
                                     r=[t_zt[zb]], w=[t_zt[zb]])
                                k.op("act", lambda e: e.activation(out=sp_all[:, blk, :], in_=zt[zb][:, :], func=AF.Ln, bias=ones_col),
                                     r=[t_zt[zb], t_cf], w=[t_sp[blk]])
                k.barrier("mixer_phase")
                stop_at(f"b1_{i}_{g}")
                if is_fox:
                    fox_cum(sp_all)
                    stop_at(f"cum{i}")
                if is_fox:
                    b2_fox(v_all)
                else:
                    b2_dil(g, d, v_all)
                stop_at(f"b2_{i}_{g}")
        merge_outproj(i, NG)

    def fox_cum(sp_all):
        with ExitStack() as st:
            cum = sb(st, "cum", [NH, S], F32)
            t_cum = k.tok()
            pre = sb(st, "pre", [NH, NTB], F32)
            t_pre = k.tok()
            hif = sb(st, "hif", [NH, S], F32)
            t_hif = k.tok()
            rb = [sb(st, "rb", [NH, S], BF16) for _ in range(2)]
            t_rb = k.toks_n(2)
            oneb = sb(st, "oneb", [NH, S], BF16)
            t_one = k.tok()
            pc = [ps(st, "pc") for _ in range(2)]
            t_pc = k.toks_n(2)
            for m4 in range(NTB // 4):
                b = m4 % 2
                for mm in range(4):
                    m = m4 * 4 + mm
                    k.op("pe", lambda e: e.matmul(pc[b][0:NH, mm * 128:(mm + 1) * 128], lhsT=sp_all[:, m, :], rhs=utri_f, start=True, stop=True),
                         r=[t_cf], w=[t_pc[b]], inc=(mm == 3))
                k.op("act", lambda e: e.activation(out=cum[:, m4 * 512:(m4 + 1) * 512], in_=pc[b][0:NH, :], func=AF.Identity),
                     r=[t_pc[b]], w=[t_cum])
            k.op("dve", lambda e: e.memset(pre[:, 0:1], 0.0), w=[t_pre])
            for m in range(1, NTB):
                k.op("dve", lambda e: e.tensor_tensor(out=pre[:, m:m + 1], in0=pre[:, m - 1:m], in1=cum[:, m * 128 - 1:m * 128], op=ALU.add),
                     r=[t_cum, t_pre], w=[t_pre])
            for m in range(1, NTB):
                k.op("dve", lambda e: e.tensor_scalar(out=cum[:, m * 128:(m + 1) * 128], in0=cum[:, m * 128:(m + 1) * 128],
                                                      scalar1=pre[:, m:m + 1], scalar2=None, op0=ALU.add), r=[t_pre, t_cum], w=[t_cum])
            k.op("pool", lambda e: e.memset(oneb[:, :], 1.0), w=[t_one])
            for row in range(3):
                k.dma("sp", out=aug_s[1][:, row, :], in_=oneb[:, :], r=[t_one], st=t_one)
                k.dma("sp", out=aug_s[0][:, 3 + row, :], in_=oneb[:, :], r=[t_one], st=t_one)
            for part in range(3):
                kb_, qb_ = rb[0], rb[1]
                k.op("dve", lambda e: e.tensor_copy(out=kb_[:, :], in_=cum[:, :]), r=[t_cum], w=[t_rb[0]])
                k.op("act", lambda e: e.activation(out=qb_[:, :], in_=kb_[:, :], func=AF.Identity, scale=-1.0), r=[t_rb[0]], w=[t_rb[1]])
                k.dma("sp", out=aug_s[1][:, 3 + part, :], in_=kb_[:, :], r=[t_rb[0]], st=t_rb[0])
                k.dma("sp", out=aug_s[0][:, part, :], in_=qb_[:, :], r=[t_rb[1]], st=t_rb[1])
                if part < 2:
                    k.op("dve", lambda e: e.tensor_copy(out=hif[:, :], in_=kb_[:, :]), r=[t_rb[0]], w=[t_hif])
                    k.op("dve", lambda e: e.tensor_tensor(out=cum[:, :], in0=cum[:, :], in1=hif[:, :], op=ALU.subtract),
                         r=[t_hif, t_cum], w=[t_cum])
        k.barrier("fox_cum")

    def b2_fox(v_all):
        NQT = S // 512
        with ExitStack() as st:
            qa = [sb(st, "qa", [HD + 6, S], BF16) for _ in range(2)]
            ka = [sb(st, "ka", [HD + 6, S], BF16) for _ in range(2)]
            t_qa = k.toks_n(2)
            t_ka = k.toks_n(2)
            NP = 4
            pt = [sb(st, "pt", [128, 512], BF16) for _ in range(NP)]
            t_pt = k.toks_n(NP)
            ost = [sb(st, "ost", [HD + 1, S], F32) for _ in range(2)]
            t_ost = k.toks_n(2)
            NSP = 4
            sps = [ps(st, "sps") for _ in range(NSP)]
            t_sps = k.toks_n(NSP)
            ops_ = [ps(st, "ops") for _ in range(2)]
            t_ops = k.toks_n(2)

            def load_head(h):
                b = h % 2
                k.dma("sp", out=qa[b][0:HD, :], in_=qk_s[0][0][h * HD:(h + 1) * HD, :], w=[t_qa[b]], st=t_qa[b])
                k.dma("sp", out=qa[b][HD:HD + 6, :], in_=aug_s[0][h], w=[t_qa[b]], st=t_qa[b])
                k.dma("sp", out=ka[b][0:HD, :], in_=qk_s[0][1][h * HD:(h + 1) * HD, :], w=[t_ka[b]], st=t_ka[b])
                k.dma("sp", out=ka[b][HD:HD + 6, :], in_=aug_s[1][h], w=[t_ka[b]], st=t_ka[b])

            load_head(0)
            nstep = 0
            for h in range(NH):
                hb = h % 2
                if h + 1 < NH:
                    load_head(h + 1)
                steps = [(jt, kb) for jt in range(NQT) for kb in range(4 * jt + 4)]
                LA = 3

                def emit_qk(idx):
                    jt, kb = steps[idx]
                    sidx = (nstep + idx) % NSP
                    qlo = max(kb, 4 * jt) * 128
                    W = (4 * jt + 4) * 128 - qlo
                    diag = kb >= 4 * jt
                    k.op("pe", lambda e: e.matmul(sps[sidx][:, 0:W], lhsT=ka[hb][:, kb * 128:(kb + 1) * 128], rhs=qa[hb][:, qlo:qlo + W],
                                                  start=True, stop=not diag), r=[t_ka[hb], t_qa[hb]], w=[t_sps[sidx]], inc=not diag)
                    if diag:
                        k.op("pe", lambda e: e.matmul(sps[sidx][:, 0:128], lhsT=ident_b, rhs=mneg_b, start=False, stop=True),
                             r=[t_cf], w=[t_sps[sidx]])

                for idx in range(min(LA, len(steps))):
                    emit_qk(idx)
                for idx, (jt, kb) in enumerate(steps):
                    if idx + LA < len(steps):
                        emit_qk(idx + LA)
                    sidx = (nstep + idx) % NSP
                    pidx = (nstep + idx) % NP
                    qlo = max(kb, 4 * jt) * 128
                    W = (4 * jt + 4) * 128 - qlo
                    off = qlo - 4 * jt * 128
                    ob = jt % 2
                    k.op("act", lambda e: e.activation(out=pt[pidx][:, 0:W], in_=sps[sidx][:, 0:W], func=AF.Exp), r=[t_sps[sidx]], w=[t_pt[pidx]])
                    last = (kb == 4 * jt + 3)
                    k.op("pe", lambda e: e.matmul(ops_[ob][0:HD + 1, off:off + W], lhsT=v_all[:, kb, h, :], rhs=pt[pidx][:, 0:W],
                                                  start=(kb == 0), stop=last), r=[t_pt[pidx]], w=[t_ops[ob]], inc=last)
                    if last:
                        k.op("dve", lambda e: e.tensor_copy(out=ost[hb][:, jt * 512:(jt + 1) * 512], in_=ops_[ob][0:HD + 1, :]),
                             r=[t_ops[ob]], w=[t_ost[hb]])
                nstep += len(steps)
                k.dma("pool", out=oun_o[0][h * HD:(h + 1) * HD, :], in_=ost[hb][0:HD, :], r=[t_ost[hb]], st=t_ost[hb])
                k.dma("pool", out=oun_d[0][h:h + 1, :], in_=ost[hb][HD:HD + 1, :], r=[t_ost[hb]], st=t_ost[hb])
        k.barrier("emit_qk")

    def b2_dil(g, d, v_all):
        nb = S // d // 128
        DBG = 0
        with ExitStack() as st:
            qa = [sb(st, "dq", [HD, S], BF16) for _ in range(2)]
            ka = [sb(st, "dk", [HD, S], BF16) for _ in range(2)]
            t_qa = k.toks_n(2)
            t_ka = k.toks_n(2)
            NP = 4
            pt = [sb(st, "dpt", [128, 256], BF16) for _ in range(NP)]
            t_pt = k.toks_n(NP)
            ost = [sb(st, "dost", [HD + 1, S], F32) for _ in range(2)]
            t_ost = k.toks_n(2)
            NSP = 4
            sps = [ps(st, "dsps") for _ in range(NSP)]
            t_sps = k.toks_n(NSP)
            NOS = 4
            ops_ = [ps(st, "dops") for _ in range(NOS)]
            t_os = k.toks_n(NOS)

            def oslot(n):
                c = (n // NOS) % 4
                return ops_[n % NOS][0:HD + 1, c * 128:(c + 1) * 128]

            def load_head(h):
                b = h % 2
                k.dma("sp", out=qa[b][:, :], in_=qk_s[g][0][h * HD:(h + 1) * HD, :], w=[t_qa[b]], st=t_qa[b])
                k.dma("sp", out=ka[b][:, :], in_=qk_s[g][1][h * HD:(h + 1) * HD, :], w=[t_ka[b]], st=t_ka[b])

            def sl(start, cnt):
                return slice(start, start + (cnt - 1) * d + 1, d) if d > 1 else slice(start, start + cnt)

            load_head(0)
            nstep = 0
            nos = 0
            for h in range(NH):
                hb = h % 2
                if h + 1 < NH:
                    load_head(h + 1)
                steps = [(r, jb) for r in range(d) for jb in range(nb)]
                LA = 3

                def emit_qk(idx):
                    r, jb = steps[idx]
                    sidx = (nstep + idx) % NSP
                    cnt = 256 if jb + 1 < nb else 128
                    kst = r + d * 128 * jb
                    k.op("pe", lambda e: e.matmul(sps[sidx][:, 0:cnt], lhsT=ka[hb][:, sl(kst, 128)], rhs=qa[hb][:, sl(kst, cnt)],
                                                  start=True, stop=True), r=[t_ka[hb], t_qa[hb]], w=[t_sps[sidx]])

                for idx in range(min(LA, len(steps))):
                    emit_qk(idx)
                for idx, (r, jb) in enumerate(steps):
                    if idx + LA < len(steps):
                        emit_qk(idx + LA)
                    sidx = (nstep + idx) % NSP
                    pidx = (nstep + idx) % NP
                    cnt = 256 if jb + 1 < nb else 128
                    kst = r + d * 128 * jb
                    blk = r * nb + jb
                    k.op("act", lambda e: e.activation(out=pt[pidx][:, 0:cnt], in_=sps[sidx][:, 0:cnt], func=AF.Exp), r=[t_sps[sidx]], w=[t_pt[pidx]])
                    k.op("dve", lambda e: e.tensor_tensor(out=pt[pidx][:, 0:cnt], in0=pt[pidx][:, 0:cnt], in1=cb[:, C_MDIAG:C_MDIAG + cnt], op=ALU.mult),
                         r=[t_pt[pidx], t_cf], w=[t_pt[pidx]])
                    s0 = nos + jb
                    k.op("pe", lambda e: e.matmul(oslot(s0), lhsT=v_all[:, blk, h, :], rhs=pt[pidx][:, 0:128], start=(jb == 0) or DBG == 1, stop=True),
                         r=[t_pt[pidx]], w=[t_os[s0 % NOS]])
                    if DBG != 3:
                        k.op("dve", lambda e: e.tensor_copy(out=ost[hb][:, sl(kst, 128)], in_=oslot(s0)), r=[t_os[s0 % NOS]], w=[t_ost[hb]])
                    if cnt == 256:
                        s1 = nos + jb + 1
                        k.op("pe", lambda e: e.matmul(oslot(s1), lhsT=v_all[:, blk, h, :], rhs=pt[pidx][:, 128:256], start=True, stop=(DBG == 1)),
                             r=[t_pt[pidx]], w=[t_os[s1 % NOS]])
                    if jb == nb - 1:
                        nos += nb
                nstep += len(steps)
                k.dma("pool", out=oun_o[g][h * HD:(h + 1) * HD, :], in_=ost[hb][0:HD, :], r=[t_ost[hb]], st=t_ost[hb])
                k.dma("pool", out=oun_d[g][h:h + 1, :], in_=ost[hb][HD:HD + 1, :], r=[t_ost[hb]], st=t_ost[hb])
        k.barrier("emit_qk")

    def merge_outproj(i, NG):
        cv = conv_tok[f"a{i}"]
        T = 512
        woutv = wout_s[i].rearrange("(c p) d -> p c d", p=128)
        with ExitStack() as st:
            wo = sb(st, "wo", [128, DC, D], BF16)
            t_wo = k.tok()
            k.dma("sp", out=wo[:, :, :], in_=woutv[:, :, :], r=[cv], w=[t_wo], st=t_wo)
            xt = [sb(st, "mx", [128, DC, T], F32) for _ in range(2)]
            t_x = k.toks_n(2)
            on = [[sb(st, "on", [128, DC, T], F32) for _ in range(NG)] for _ in range(2)]
            t_on = [k.toks_n(NG) for _ in range(2)]
            dn = [sb(st, "dn", [NH, NG, T], F32) for _ in range(2)]
            t_dn = k.toks_n(2)
            rec = sb(st, "rec", [NH, T], F32)
            t_rec = k.tok()
            osum = [sb(st, "osum", [128, T], F32) for _ in range(2)]
            t_osum = k.toks_n(2)
            oT = sb(st, "oT", [128, DC, T], BF16)
            t_oT = k.tok()
            py = [ps(st, "py") for _ in range(2)]
            t_py = k.toks_n(2)
            pbc = [ps(st, "pbc") for _ in range(2)]
            t_pbc = k.toks_n(2)
            NT = S // T

            def load_t(t):
                b = t % 2
                cs = slice(t * T, (t + 1) * T)
                k.dma("sp", out=xt[b][:, :, :], in_=xTv_g[:, :, cs], w=[t_x[b]], st=t_x[b])
                for g in range(NG):
                    k.dma("sp", out=on[b][g][:, :, :], in_=oun_o[g].rearrange("(c p) s -> p c s", p=128)[:, :, cs],
                          w=[t_on[b][g]], st=t_on[b][g])
                    k.dma("sp", out=dn[b][:, g, :], in_=oun_d[g][:, cs], w=[t_dn[b]], st=t_dn[b])

            load_t(0)
            ny = 0
            nb_ = 0
            for t in range(NT):
                b = t % 2
                xb = xt[b]
                tx = t_x[b]
                if t + 1 < NT:
                    load_t(t + 1)
                for g in range(1, NG):
                    k.op("dve", lambda e: e.tensor_tensor(out=dn[b][:, 0, :], in0=dn[b][:, 0, :], in1=dn[b][:, g, :], op=ALU.add),
                         r=[t_dn[b]], w=[t_dn[b]])
                k.op("dve", lambda e: e.reciprocal(out=rec[:, :], in_=dn[b][:, 0, :]), r=[t_dn[b]], w=[t_rec])
                for c in range(DC):
                    bb = nb_ % 2
                    nb_ += 1
                    k.op("pe", lambda e: e.matmul(pbc[bb][:, :], lhsT=cf[0:NH, C_ESEL + c * 128:C_ESEL + (c + 1) * 128], rhs=rec[:, :],
                                                  start=True, stop=True), r=[t_rec, t_cf], w=[t_pbc[bb]])
                    src = on[b][0][:, c, :]
                    rd = [t_on[b][0]]
                    if NG == 3:
                        k.op("pool", lambda e: e.tensor_tensor(out=osum[bb][:, :], in0=on[b][1][:, c, :], in1=on[b][2][:, c, :], op=ALU.add),
                             r=[t_on[b][1], t_on[b][2]], w=[t_osum[bb]])
                        k.op("dve", lambda e: e.tensor_tensor(out=osum[bb][:, :], in0=osum[bb][:, :], in1=on[b][0][:, c, :], op=ALU.add),
                             r=[t_on[b][0], t_osum[bb]], w=[t_osum[bb]])
                        src = osum[bb][:, :]
                        rd = [t_osum[bb]]
                    k.op("dve", lambda e: e.tensor_tensor(out=oT[:, c, :], in0=src, in1=pbc[bb][:, :], op=ALU.mult),
                         r=rd + [t_pbc[bb]], w=[t_oT])
                for dc in range(DC):
                    yb = ny % 2
                    ny += 1
                    for c in range(DC):
                        k.op("pe", lambda e: e.matmul(py[yb][:, :], lhsT=wo[:, c, dc * 128:(dc + 1) * 128], rhs=oT[:, c, :],
                                                      start=(c == 0), stop=(c == DC - 1)), r=[t_wo, t_oT], w=[t_py[yb]], inc=(c == DC - 1))
                    k.op("dve", lambda e: e.scalar_tensor_tensor(out=xb[:, dc, :], in0=py[yb][:, :], scalar=g_col(i, 1, dc),
                                                                 in1=xb[:, dc, :], op0=ALU.mult, op1=ALU.add),
                         r=[t_py[yb], t_mod, tx], w=[tx])
                k.dma("pool", out=xTv_g[:, :, t * T:(t + 1) * T], in_=xb[:, :, :], r=[tx], st=tx)
        k.barrier("merge_outproj")

    def program():
        if debug_stop == "conv":
            k.barrier("program")
            return
        setup_mods()
        if debug_stop == "mods":
            return
        transpose_in()
        if debug_stop == "tin":
            transpose_out()
            return
        if depth >= 2:
            rope_tables()
            stop_at("rope")
        for i in range(depth):
            conv_upto(3 * i + 3)
            ffn_phase(i, 0)
            if debug_stop == f"ffn{i}0":
                break
            conv_upto(3 * i + 4)
            mixer_phase(i)
            if debug_stop == f"mix{i}":
                break
            conv_upto(3 * i + 5)
            ffn_phase(i, 1)
        transpose_out()

    stopped = False
    try:
        program()
    except StopBuild:
        stopped = True
    for key in sorted(k.bgkeys):
        if k.seen["sp"].get(key, 0) < k.cnt[key]:
            nc.sync.wait_ge(k.sems[key], k.cnt[key])
    if not stopped:
        top.close()
    nc._phase_names = k.phase_names
    bad = k.check()
    if bad:
        raise RuntimeError(f"semaphore protocol deadlock: {bad}")
    return nc


_CACHE = {}


def _prep_inputs(inputs, b, S, depth, cfc):
    n_fox = (depth + 1) // 2
    n_dil = depth // 2
    f = lambda a: np.ascontiguousarray(a, dtype=np.float32)
    m = {
        "x": f(inputs["x"][b, :S]),
        "c": f(inputs["c"][b]).reshape(DC, 128),
        "positions": np.ascontiguousarray(inputs["positions"][b, :S], dtype=np.int32).reshape(1, S),
        "mod_w": f(inputs["mod_w"][:depth]),
        "mod_b": f(inputs["mod_b"][:depth]).reshape(depth * 72, 128),
        "norm_g": f(inputs["norm_g"][:depth]).reshape(depth * 24, 128),
        "ffn_w_gate": f(inputs["ffn_w_gate"][:depth]),
        "ffn_w_up": f(inputs["ffn_w_up"][:depth]),
        "ffn_w_down": f(inputs["ffn_w_down"][:depth]),
        "fox_w_in": f(inputs["fox_w_in"][:max(n_fox, 1)]),
        "fox_b_f": f(inputs["fox_b_f"][:max(n_fox, 1)]),
        "fox_q_g": f(inputs["fox_q_g"][:max(n_fox, 1)]),
        "fox_k_g": f(inputs["fox_k_g"][:max(n_fox, 1)]),
        "fox_w_out": f(inputs["fox_w_out"][:max(n_fox, 1)]),
        "dil_w_in": f(inputs["dil_w_in"][:max(n_dil, 1)]),
        "dil_q_g": f(inputs["dil_q_g"][:max(n_dil, 1)]),
        "dil_k_g": f(inputs["dil_k_g"][:max(n_dil, 1)]),
        "dil_w_out": f(inputs["dil_w_out"][:max(n_dil, 1)]),
        "cf": cfc,
    }
    return m


def run(inputs, S=4096, depth=4, n_cores=8, debug_stop=None, trace=False):
    key = (S, depth, debug_stop)
    if key not in _CACHE:
        _CACHE[key] = build(S, depth, debug_stop)
    nc = _CACHE[key]
    cfc = make_consts()
    in_maps = [_prep_inputs(inputs, b, S, depth, cfc) for b in range(n_cores)]
    res = run_bass_kernel_spmd(nc, in_maps, core_ids=list(range(n_cores)), **({"trace": True} if trace else {}))
    out = np.stack([np.asarray(r["y"], dtype=np.float32) for r in res.results], axis=0)
    return out, res


def kernel(**inputs):
    out, _ = run(inputs)
    return out
```

```python
import math
from contextlib import ExitStack

import numpy as np
import concourse.bass as bass
import concourse.mybir as mybir
from concourse.bass_utils import run_bass_kernel_spmd

F32 = mybir.dt.float32
BF16 = mybir.dt.bfloat16
I32 = mybir.dt.int32
AF = mybir.ActivationFunctionType
ALU = mybir.AluOpType

D = 1024
DC = 8
HD = 64
NH = 16
DFF = 2816
FC = 22
EPS = 1e-6
FOX_IN = 3 * D + NH
DIL_IN = 9 * D
DIL_CFG = ((128, 1), (512, 4), (2048, 16))
ROPE_THETA = 500000.0

C_IDENT = 0
C_ONES = 128
C_BLK = 256
C_UTRI = 384
C_MDIAG = 512
C_MOFF = 640
C_PERM = 768
C_ROPE = 896
C_EPS = 898
C_MNEG = 900
C_ESEL = 1028
NCF = 1028 + 1024


def make_consts():
    cf = np.zeros((128, NCF), np.float32)
    cf[:, C_IDENT:C_IDENT + 128] = np.eye(128, dtype=np.float32)
    cf[:, C_ONES:C_ONES + 128] = 1.0
    for a in range(2):
        cf[a * 64:(a + 1) * 64, C_BLK + a * 64:C_BLK + (a + 1) * 64] = 1.0
    k = np.arange(128)[:, None]
    q = np.arange(128)[None, :]
    cf[:, C_UTRI:C_UTRI + 128] = (k <= q)
    cf[:, C_MDIAG:C_MDIAG + 128] = (q >= k)
    cf[:, C_MOFF:C_MOFF + 128] = (q <= k)
    half = 8
    inv_freq = (np.float32(ROPE_THETA) ** (-(np.arange(half, dtype=np.float32) * np.float32(2.0) / np.float32(16.0)))).astype(np.float32)
    for p in range(128):
        d = p % 64
        a = p // 64
        if d < 8:
            cf[a * 64 + d + 8, C_PERM + p] = 1.0
            cf[p, C_ROPE] = inv_freq[d]
            cf[p, C_ROPE + 1] = -1.0
        elif d < 16:
            cf[a * 64 + d - 8, C_PERM + p] = 1.0
            cf[p, C_ROPE] = inv_freq[d - 8]
            cf[p, C_ROPE + 1] = 1.0
    cf[:, C_EPS] = EPS
    cf[:, C_MNEG:C_MNEG + 128] = np.where(q < k, -30000.0, 0.0)
    for c in range(8):
        for m in range(128):
            cf[2 * c + m // 64, C_ESEL + c * 128 + m] = 1.0
    return cf


class StopBuild(Exception):
    pass


class Tok:
    __slots__ = ("w", "r", "dsem", "persist", "name")

    def __init__(self, name="", persist=False):
        self.w = None
        self.r = {}
        self.dsem = None
        self.persist = persist
        self.name = name


class K:
    ENG = ("pe", "act", "dve", "pool", "sp")

    def __init__(self, nc):
        self.nc = nc
        self.e = dict(pe=nc.tensor, act=nc.scalar, dve=nc.vector, pool=nc.gpsimd, sp=nc.sync)
        self.sems = {}
        self.cnt = {}
        self.seen = {e: {} for e in self.ENG}
        self.toks = []
        self.bgkeys = set()
        self.uid = 0
        self.free_dsems = []
        self.log = {e: [] for e in self.ENG}
        self.phase_names = []
        for e in self.ENG:
            self._mk(e)
        self._mk("bar")

    def _mk(self, key):
        self.sems[key] = self.nc.alloc_semaphore(name="s_" + key)
        self.cnt[key] = 0

    def tok(self, name="", persist=False):
        t = Tok(name, persist)
        self.toks.append(t)
        return t

    def toks_n(self, n, name=""):
        return [self.tok(f"{name}{i}") for i in range(n)]

    def name(self, base):
        self.uid += 1
        return f"{base}_{self.uid}"

    def _wait(self, eng, deps):
        for key, val in deps:
            if key == "pe" and eng == "pe":
                continue
            if self.seen[eng].get(key, 0) >= val:
                continue
            self.e[eng].wait_ge(self.sems[key], val)
            self.log[eng].append(("w", key, val))
            self.seen[eng][key] = val

    def check(self):
        val = {key: 0 for key in self.cnt}
        pc = {e: 0 for e in self.ENG}
        progress = True
        while progress:
            progress = False
            for e in self.ENG:
                lg = self.log[e]
                while pc[e] < len(lg):
                    ev = lg[pc[e]]
                    if ev[0] == "w":
                        if val[ev[1]] >= ev[2]:
                            pc[e] += 1
                            progress = True
                        else:
                            break
                    else:
                        val[ev[1]] += ev[2]
                        pc[e] += 1
                        progress = True
        bad = {e: (pc[e], len(self.log[e]), self.log[e][pc[e]], val[self.log[e][pc[e]][1]]) for e in self.ENG if pc[e] < len(self.log[e])}
        return bad

    @staticmethod
    def _deps(r, w):
        d = []
        for b in r:
            if b.w is not None:
                d.append(b.w)
        for b in w:
            if b.w is not None:
                d.append(b.w)
            d.extend(b.r.items())
        return d

    def op(self, eng, fn, r=(), w=(), inc=True):
        self._wait(eng, self._deps(r, w))
        ins = fn(self.e[eng])
        if inc:
            self.cnt[eng] += 1
            ins.then_inc(self.sems[eng], 1)
            self.log[eng].append(("i", eng, 1))
            tag = (eng, self.cnt[eng])
        else:
            tag = (eng, self.cnt[eng] + 1)
        for b in w:
            b.w = tag
            b.r = {}
        for b in r:
            if b.r.get(tag[0], 0) < tag[1]:
                b.r[tag[0]] = tag[1]
        return ins

    def dma(self, q, out, in_, r=(), w=(), st=None, **kw):
        self._wait(q, self._deps(r, w))
        if st.dsem is None:
            if self.free_dsems:
                st.dsem = self.free_dsems.pop()
            else:
                self.uid += 1
                st.dsem = f"d{self.uid}"
                self._mk(st.dsem)
            if st.persist:
                self.bgkeys.add(st.dsem)
        key = st.dsem
        self.cnt[key] += 16
        ins = self.e[q].dma_start(out=out, in_=in_, **kw)
        ins.then_inc(self.sems[key], 16)
        self.log[q].append(("i", key, 16))
        tag = (key, self.cnt[key])
        for b in w:
            b.w = tag
            b.r = {}
        for b in r:
            if b.r.get(key, 0) < tag[1]:
                b.r[key] = tag[1]
        return ins

    def barrier(self, name=""):
        self.phase_names.append(name)
        sp = self.e["sp"]
        for key in list(self.cnt.keys()):
            if key in ("sp", "bar") or key in self.bgkeys:
                continue
            if self.seen["sp"].get(key, 0) < self.cnt[key]:
                sp.wait_ge(self.sems[key], self.cnt[key])
                self.log["sp"].append(("w", key, self.cnt[key]))
                self.seen["sp"][key] = self.cnt[key]
        self.cnt["bar"] += 1
        sp.sem_inc(self.sems["bar"], 1)
        self.log["sp"].append(("i", "bar", 1))
        for e in self.ENG:
            if e != "sp":
                self.e[e].wait_ge(self.sems["bar"], self.cnt["bar"])
                self.log[e].append(("w", "bar", self.cnt["bar"]))
            for key in self.cnt:
                if key in self.bgkeys:
                    continue
                self.seen[e][key] = self.cnt[key]
        keep = []
        for t in self.toks:
            if t.persist:
                keep.append(t)
            else:
                t.w = None
                t.r = {}
                if t.dsem is not None:
                    self.free_dsems.append(t.dsem)
                    t.dsem = None
        self.toks = keep


def build(S=4096, depth=4, debug_stop=None):
    nc = bass.Bass("TRN2", target_bir_lowering=False)
    NTB = S // 128
    n_fox = (depth + 1) // 2
    n_dil = depth // 2

    def din(name, shape, dt=F32):
        return nc.dram_tensor(name, list(shape), dt, kind="ExternalInput")

    x_in = din("x", [S, D]).ap()
    c_in = din("c", [DC, 128]).ap()
    pos_in = din("positions", [1, S], I32).ap()
    mod_w = din("mod_w", [depth, D, 9 * D]).ap()
    mod_b = din("mod_b", [depth * 72, 128]).ap()
    norm_g = din("norm_g", [depth * 24, 128]).ap()
    w_gate = din("ffn_w_gate", [depth, 2, D, DFF]).ap()
    w_up = din("ffn_w_up", [depth, 2, D, DFF]).ap()
    w_down = din("ffn_w_down", [depth, 2, DFF, D]).ap()
    fox_w_in = din("fox_w_in", [max(n_fox, 1), D, FOX_IN]).ap()
    fox_b_f = din("fox_b_f", [max(n_fox, 1), NH]).ap()
    fox_q_g = din("fox_q_g", [max(n_fox, 1), HD]).ap()
    fox_k_g = din("fox_k_g", [max(n_fox, 1), HD]).ap()
    fox_w_out = din("fox_w_out", [max(n_fox, 1), D, D]).ap()
    dil_w_in = din("dil_w_in", [max(n_dil, 1), D, DIL_IN]).ap()
    dil_q_g = din("dil_q_g", [max(n_dil, 1), 3, HD]).ap()
    dil_k_g = din("dil_k_g", [max(n_dil, 1), 3, HD]).ap()
    dil_w_out = din("dil_w_out", [max(n_dil, 1), D, D]).ap()
    cf_in = din("cf", [128, NCF]).ap()
    y_out = nc.dram_tensor("y", [S, D], F32, kind="ExternalOutput").ap()

    def dscr(name, shape, dt):
        if debug_stop is not None and not name.startswith("w"):
            return nc.dram_tensor(name, list(shape), dt, kind="ExternalOutput")
        return nc.dram_tensor(name, list(shape), dt)

    xT_h = dscr("xT_s", [D, S], F32)
    xT = xT_h.ap()
    hT = dscr("hT_s", [D, S], BF16).ap()
    qk_s = [[dscr(f"qk_s{g}_{a}", [D, S], BF16).ap() for a in range(2)] for g in range(3)]
    aug_s = [dscr(f"aug_s{a}", [NH, 6, S], BF16).ap() for a in range(2)]
    oun_o = [dscr(f"oun_o{g}", [D, S], F32).ap() for g in range(3)]
    oun_d_h = [dscr(f"oun_d{g}", [NH, S], F32) for g in range(3)]
    oun_d = [h.ap() for h in oun_d_h]
    cs_s = [dscr(f"cs_s{a}", [128, S], F32).ap() for a in range(2)]
    wg_s = [[dscr(f"wg_s{i}_{s}", [D, DFF], BF16).ap() for s in range(2)] for i in range(depth)]
    wu_s = [[dscr(f"wu_s{i}_{s}", [D, DFF], BF16).ap() for s in range(2)] for i in range(depth)]
    wd_s = [[dscr(f"wd_s{i}_{s}", [DFF, D], BF16).ap() for s in range(2)] for i in range(depth)]
    win_s = [dscr(f"win_s{i}", [D, FOX_IN if i % 2 == 0 else DIL_IN], BF16).ap() for i in range(depth)]
    wout_s = [dscr(f"wout_s{i}", [D, D], BF16).ap() for i in range(depth)]

    k = K(nc)
    top = ExitStack()

    def sb(st, name, shape, dt):
        return st.enter_context(nc.sbuf_tensor(k.name(name), list(shape), dt))

    def ps(st, name, shape=(128, 512), dt=F32):
        return st.enter_context(nc.psum_tensor(k.name(name), list(shape), dt))

    cf = sb(top, "cf", [128, NCF], F32)
    cb = sb(top, "cb", [128, C_ESEL], BF16)
    modT = sb(top, "modT", [128, depth * 72], F32)
    ngT = sb(top, "ngT", [128, depth * 24], F32)
    mA = sb(top, "mA", [128, depth * 24], F32)
    mG = sb(top, "mG", [128, depth * 24], F32)
    t_cf = k.tok("cf", True)
    t_mod = k.tok("mod", True)

    ident = cf[:, C_IDENT:C_IDENT + 128]
    ones_f = cf[:, C_ONES:C_ONES + 128]
    blk_f = cf[:, C_BLK:C_BLK + 128]
    utri_f = cf[:, C_UTRI:C_UTRI + 128]
    eps_col = cf[:, C_EPS:C_EPS + 1]
    ones_b = cb[:, C_ONES:C_ONES + 128]
    blk_b = cb[:, C_BLK:C_BLK + 128]

    conv_tok = {}

    def conv(key, dst, src, rows, cols):
        t = conv_tok.setdefault(key, k.tok("cv" + key, True))
        a = src.rearrange("(p a) c -> p (a c)", p=128)
        b = dst.rearrange("(p a) c -> p (a c)", p=128)
        n = (rows // 128) * cols
        step = 8192
        for o in range(0, n, step):
            e = min(n, o + step)
            k.dma("pool", out=b[:, o:e], in_=a[:, o:e], st=t)
            t.w = (t.dsem, k.cnt[t.dsem])

    k.dma("sp", out=cf[:, :], in_=cf_in[:, :], w=[t_cf], st=t_cf)
    k.dma("pool", out=cb[:, :], in_=cf_in[:, 0:C_ESEL], w=[t_cf], st=t_cf)
    conv_jobs = []
    for i in range(depth):
        j = i // 2
        conv_jobs.append([(f"f{i}0", wg_s[i][0], w_gate[i, 0], D, DFF), (f"f{i}0", wu_s[i][0], w_up[i, 0], D, DFF),
                          (f"f{i}0", wd_s[i][0], w_down[i, 0], DFF, D)])
        if i % 2 == 0:
            conv_jobs.append([(f"a{i}", win_s[i], fox_w_in[j], D, FOX_IN), (f"a{i}", wout_s[i], fox_w_out[j], D, D)])
        else:
            conv_jobs.append([(f"a{i}", win_s[i], dil_w_in[j], D, DIL_IN), (f"a{i}", wout_s[i], dil_w_out[j], D, D)])
        conv_jobs.append([(f"f{i}1", wg_s[i][1], w_gate[i, 1], D, DFF), (f"f{i}1", wu_s[i][1], w_up[i, 1], D, DFF),
                          (f"f{i}1", wd_s[i][1], w_down[i, 1], DFF, D)])
    conv_next = [0]

    def conv_upto(n):
        while conv_next[0] < min(n, len(conv_jobs)):
            for job in conv_jobs[conv_next[0]]:
                conv(*job)
            conv_next[0] += 1

    conv_upto(1)

    def setup_mods():
        with ExitStack() as st:
            craw = sb(st, "craw", [DC, 128], F32)
            cT = sb(st, "cT", [128, DC], F32)
            mb = sb(st, "mb", [72, 128], F32)
            ng = sb(st, "ng", [24, 128], F32)
            NMW = 4
            wbuf = [sb(st, "mw", [128, DC, 512], F32) for _ in range(NMW)]
            pmod = ps(st, "pmod")
            pmisc = ps(st, "pmisc")
            prow = [ps(st, "prow") for _ in range(2)]
            t_prow = k.toks_n(2)
            row = sb(st, "mrow", [1, 9 * D], F32)
            t_row = k.tok()
            t_c, t_cT, t_mb, t_ng, t_pm, t_pmisc = (k.tok() for _ in range(6))
            t_w = k.toks_n(NMW)
            k.dma("sp", out=craw[:, :], in_=c_in[:, :], w=[t_c], st=t_c)
            k.op("pe", lambda e: e.matmul(pmisc[:, 0:DC], lhsT=craw[:, :], rhs=ident[0:DC, 0:DC], start=True, stop=True),
                 r=[t_c, t_cf], w=[t_pmisc])
            k.op("act", lambda e: e.activation(out=cT[:, :], in_=pmisc[:, 0:DC], func=AF.Silu), r=[t_pmisc], w=[t_cT])
            for i in range(depth):
                k.dma("sp", out=mb[:, :], in_=mod_b[i * 72:(i + 1) * 72, :], w=[t_mb], st=t_mb)
                k.dma("sp", out=ng[:, :], in_=norm_g[i * 24:(i + 1) * 24, :], w=[t_ng], st=t_ng)
                for jg in range(18):
                    n = i * 18 + jg
                    wb = wbuf[n % NMW]
                    k.dma("sp" if n % 2 == 0 else "act", out=wb[:, :, :],
                          in_=mod_w[i].rearrange("(kc p) f -> p kc f", p=128)[:, :, jg * 512:(jg + 1) * 512],
                          w=[t_w[n % NMW]], st=t_w[n % NMW])
                    pr = prow[n % 2]
                    for kc in range(DC):
                        k.op("pe", lambda e: e.matmul(pr[0:1, :], lhsT=cT[:, kc:kc + 1], rhs=wb[:, kc, :],
                                                      start=(kc == 0), stop=(kc == DC - 1)),
                             r=[t_w[n % NMW], t_cT], w=[t_prow[n % 2]], inc=(kc == DC - 1))
                    k.op("act", lambda e: e.activation(out=row[0:1, jg * 512:(jg + 1) * 512], in_=pr[0:1, :], func=AF.Identity),
                         r=[t_prow[n % 2]], w=[t_row])
                for col in range(72):
                    k.op("pe", lambda e: e.matmul(pmod[:, col:col + 1], lhsT=row[0:1, col * 128:(col + 1) * 128], rhs=ones_f[0:1, 0:1],
                                                  start=True, stop=True), r=[t_row, t_cf], w=[t_pm], inc=(col == 71))
                k.op("pe", lambda e: e.matmul(pmisc[:, 0:72], lhsT=mb[:, :], rhs=ident[0:72, 0:72], start=True, stop=True),
                     r=[t_mb, t_cf], w=[t_pmisc])
                k.op("act", lambda e: e.activation(out=modT[:, i * 72:(i + 1) * 72], in_=pmisc[:, 0:72], func=AF.Identity),
                     r=[t_pmisc], w=[t_mod])
                k.op("dve", lambda e: e.tensor_tensor(out=modT[:, i * 72:(i + 1) * 72], in0=modT[:, i * 72:(i + 1) * 72],
                                                      in1=pmod[:, 0:72], op=ALU.add), r=[t_pm, t_mod], w=[t_mod])
                k.op("pe", lambda e: e.matmul(pmisc[:, 0:24], lhsT=ng[:, :], rhs=ident[0:24, 0:24], start=True, stop=True),
                     r=[t_ng, t_cf], w=[t_pmisc])
                k.op("act", lambda e: e.activation(out=ngT[:, i * 24:(i + 1) * 24], in_=pmisc[:, 0:24], func=AF.Identity),
                     r=[t_pmisc], w=[t_mod])
                for s in range(3):
                    base = i * 72 + s * 24
                    o = i * 24 + s * 8
                    k.op("dve", lambda e: e.scalar_tensor_tensor(out=mA[:, o:o + 8], in0=modT[:, base + 8:base + 16], scalar=1.0,
                                                                 in1=ngT[:, o:o + 8], op0=ALU.add, op1=ALU.mult),
                         r=[t_mod], w=[t_mod])
                    gsc = 1.0 if s == 1 else 0.5
                    k.op("dve", lambda e: e.tensor_scalar(out=mG[:, o:o + 8], in0=modT[:, base + 16:base + 24], scalar1=gsc, scalar2=None,
                                                          op0=ALU.mult), r=[t_mod], w=[t_mod])
        k.barrier("setup_mods")

    def shift_col(i, s, dc):
        c = i * 72 + s * 24 + dc
        return modT[:, c:c + 1]

    def a_col(i, s, dc):
        c = i * 24 + s * 8 + dc
        return mA[:, c:c + 1]

    def g_col(i, s, dc):
        c = i * 24 + s * 8 + dc
        return mG[:, c:c + 1]

    def transpose_in():
        with ExitStack() as st:
            xin = [sb(st, "xin", [128, D], F32) for _ in range(2)]
            xst = [sb(st, "xst", [128, DC, 512], F32) for _ in range(2)]
            pt = [ps(st, "ptr") for _ in range(4)]
            t_xin = k.toks_n(2)
            t_xst = k.toks_n(2)
            t_pt = k.toks_n(4)
            n = 0
            for tb in range(NTB):
                xb = xin[tb % 2]
                k.dma("sp", out=xb[:, :], in_=x_in[tb * 128:(tb + 1) * 128, :], w=[t_xin[tb % 2]], st=t_xin[tb % 2])
                g4 = tb // 4
                sbuf = xst[g4 % 2]
                for dg in range(2):
                    p = pt[n % 4]
                    tp = t_pt[n % 4]
                    for j in range(4):
                        dc = dg * 4 + j
                        k.op("pe", lambda e: e.matmul(p[:, j * 128:(j + 1) * 128], lhsT=xb[:, dc * 128:(dc + 1) * 128], rhs=ident,
                                                      start=True, stop=True), r=[t_xin[tb % 2], t_cf], w=[tp], inc=(j == 3))
                    eng = "act" if n % 2 == 0 else "dve"
                    dst = sbuf[:, dg * 4:(dg + 1) * 4, (tb % 4) * 128:(tb % 4 + 1) * 128]
                    src = p[:, :].rearrange("p (j t) -> p j t", j=4)
                    if eng == "act":
                        k.op("act", lambda e: e.activation(out=dst, in_=src, func=AF.Identity), r=[tp], w=[t_xst[g4 % 2]])
                    else:
                        k.op("dve", lambda e: e.tensor_copy(out=dst, in_=src), r=[tp], w=[t_xst[g4 % 2]])
                    n += 1
                if tb % 4 == 3:
                    k.dma("pool", out=xT.rearrange("(dc p) s -> p dc s", p=128)[:, :, g4 * 512:(g4 + 1) * 512], in_=sbuf[:, :, :],
                          r=[t_xst[g4 % 2]], st=t_xst[g4 % 2])
        k.barrier("transpose_in")

    def transpose_out():
        with ExitStack() as st:
            xt = [sb(st, "xo", [128, DC, 512], F32) for _ in range(2)]
            yst = [sb(st, "yst", [128, D], F32) for _ in range(2)]
            pt = [ps(st, "pto") for _ in range(4)]
            t_xt = k.toks_n(2)
            t_y = k.toks_n(2)
            t_pt = k.toks_n(4)
            n = 0
            for g4 in range(S // 512):
                xb = xt[g4 % 2]
                k.dma("sp", out=xb[:, :, :], in_=xT.rearrange("(dc p) s -> p dc s", p=128)[:, :, g4 * 512:(g4 + 1) * 512],
                      w=[t_xt[g4 % 2]], st=t_xt[g4 % 2])
                for b4 in range(4):
                    tb = g4 * 4 + b4
                    yb = yst[tb % 2]
                    for dg in range(2):
                        p = pt[n % 4]
                        tp = t_pt[n % 4]
                        for j in range(4):
                            dc = dg * 4 + j
                            k.op("pe", lambda e: e.matmul(p[:, j * 128:(j + 1) * 128], lhsT=xb[:, dc, b4 * 128:(b4 + 1) * 128], rhs=ident,
                                                          start=True, stop=True), r=[t_xt[g4 % 2], t_cf], w=[tp], inc=(j == 3))
                        dst = yb[:, dg * 512:(dg + 1) * 512]
                        if n % 2 == 0:
                            k.op("act", lambda e: e.activation(out=dst, in_=p[:, :], func=AF.Identity), r=[tp], w=[t_y[tb % 2]])
                        else:
                            k.op("dve", lambda e: e.tensor_copy(out=dst, in_=p[:, :]), r=[tp], w=[t_y[tb % 2]])
                        n += 1
                    k.dma("pool", out=y_out[tb * 128:(tb + 1) * 128, :], in_=yb[:, :], r=[t_y[tb % 2]], st=t_y[tb % 2])
        k.barrier("transpose_out")

    class NormRes:
        def __init__(self, st, T):
            self.T = T
            self.sq = [sb(st, "sq", [128, 512], BF16) for _ in range(3)]
            self.t_sq = k.toks_n(3)
            self.rstd = sb(st, "rstd", [128, T], F32)
            self.t_rstd = k.tok()
            self.tmp = [sb(st, "ntmp", [128, 512], F32) for _ in range(2)]
            self.t_tmp = k.toks_n(2)
            self.pss = ps(st, "pss")
            self.t_pss = k.tok()
            self.n = 0

    def norm_tile(nr, xb, t_x, i, s, hdst, t_h):
        T = nr.T
        for hf in range(T // 512):
            for dc in range(DC):
                q = nr.n % 3
                nr.n += 1
                k.op("act", lambda e: e.activation(out=nr.sq[q][:, :], in_=xb[:, dc, hf * 512:(hf + 1) * 512], func=AF.Square),
                     r=[t_x], w=[nr.t_sq[q]])
                k.op("pe", lambda e: e.matmul(nr.pss[:, :], lhsT=ones_b, rhs=nr.sq[q][:, :], start=(dc == 0), stop=(dc == DC - 1)),
                     r=[nr.t_sq[q], t_cf], w=[nr.t_pss])
            rs_ = nr.rstd[:, hf * 512:(hf + 1) * 512]
            k.op("act", lambda e: e.activation(out=rs_, in_=nr.pss[:, :], func=AF.Ln, scale=1.0 / D, bias=eps_col),
                 r=[nr.t_pss, t_cf], w=[nr.t_rstd])
            k.op("act", lambda e: e.activation(out=rs_, in_=rs_, func=AF.Exp, scale=-0.5), r=[nr.t_rstd], w=[nr.t_rstd])
        n2 = 0
        for dc in range(DC):
            for hf in range(T // 512):
                q = n2 % 2
                n2 += 1
                cs = slice(hf * 512, (hf + 1) * 512)
                k.op("dve", lambda e: e.tensor_tensor(out=nr.tmp[q][:, :], in0=xb[:, dc, cs], in1=nr.rstd[:, cs], op=ALU.mult),
                     r=[t_x, nr.t_rstd], w=[nr.t_tmp[q]])
                k.op("act", lambda e: e.activation(out=hdst(dc)[:, cs], in_=nr.tmp[q][:, :], func=AF.Identity, scale=a_col(i, s, dc),
                                                   bias=shift_col(i, s, dc)), r=[nr.t_tmp[q], t_mod], w=[t_h])

    def ffn_phase(i, s):
        sl = 0 if s == 0 else 2
        T = 1024 if S % 1024 == 0 else 512
        NH2 = T // 512
        cv = conv_tok[f"f{i}{s}"]
        wgv = wg_s[i][s].rearrange("(kc p) f -> p kc f", p=128)
        wuv = wu_s[i][s].rearrange("(kc p) f -> p kc f", p=128)
        wdv = wd_s[i][s].rearrange("(fc p) d -> p fc d", p=128)
        xTv = xT.rearrange("(dc p) s -> p dc s", p=128)
        with ExitStack() as st:
            xt = [sb(st, "fx", [128, DC, T], F32) for _ in range(2)]
            t_x = k.toks_n(2)
            nr = NormRes(st, T)
            hT2 = [sb(st, "fh", [128, DC, T], BF16) for _ in range(2)]
            t_h2 = k.toks_n(2)
            aT = sb(st, "fa", [128, FC, T], BF16)
            t_a = k.tok()
            NWB = 3
            wg = [sb(st, "fwg", [128, DC, 256], BF16) for _ in range(NWB)]
            wu = [sb(st, "fwu", [128, DC, 256], BF16) for _ in range(NWB)]
            t_wg = k.toks_n(NWB)
            t_wu = k.toks_n(NWB)
            wd = [sb(st, "fwd", [128, FC, 128], BF16) for _ in range(2)]
            t_wd = k.toks_n(2)
            sg = [sb(st, "fsg", [128, 512], F32) for _ in range(2)]
            t_sg = k.toks_n(2)
            pg = [ps(st, "pg") for _ in range(2)]
            pu = [ps(st, "pu") for _ in range(2)]
            t_pg = k.toks_n(2)
            t_pu = k.toks_n(2)
            po = [ps(st, "po") for _ in range(2)]
            t_po = k.toks_n(2)
            NT = S // T
            NFG = FC // 2
            wcount = [0]

            def load_w(fg):
                q = wcount[0] % NWB
                wcount[0] += 1
                k.dma("sp", out=wg[q][:, :, :], in_=wgv[:, :, fg * 256:(fg + 1) * 256], r=[cv], w=[t_wg[q]], st=t_wg[q])
                k.dma("sp", out=wu[q][:, :, :], in_=wuv[:, :, fg * 256:(fg + 1) * 256], r=[cv], w=[t_wu[q]], st=t_wu[q])
                return q

            dcount = [0]

            def load_wd(dc):
                q = dcount[0] % 2
                dcount[0] += 1
                k.dma("sp", out=wd[q][:, :, :], in_=wdv[:, :, dc * 128:(dc + 1) * 128], r=[cv], w=[t_wd[q]], st=t_wd[q])
                return q

            k.dma("sp", out=xt[0][:, :, :], in_=xTv[:, :, 0:T], w=[t_x[0]], st=t_x[0])
            n1 = 0
            n2 = 0

            def do_norm(t):
                hb_ = hT2[t % 2]
                norm_tile(nr, xt[t % 2], t_x[t % 2], i, sl, lambda dc: hb_[:, dc, :], t_h2[t % 2])

            for t in range(NT):
                xb = xt[t % 2]
                tx = t_x[t % 2]
                hT_sb = hT2[t % 2]
                t_h = t_h2[t % 2]
                wq = [load_w(0), load_w(1)]
                if t + 1 < NT:
                    k.dma("sp", out=xt[(t + 1) % 2][:, :, :], in_=xTv[:, :, (t + 1) * T:(t + 2) * T], w=[t_x[(t + 1) % 2]],
                          st=t_x[(t + 1) % 2])
                if t == 0:
                    do_norm(0)
                for fg in range(NFG):
                    if fg + 2 < NFG:
                        wq.append(load_w(fg + 2))
                    q = wq[fg]
                    for fl in range(2):
                        fc = fg * 2 + fl
                        for hf in range(NH2):
                            b = n1 % 2
                            n1 += 1
                            cs = slice(hf * 512, (hf + 1) * 512)
                            for kc in range(DC):
                                k.op("pe", lambda e: e.matmul(pg[b][:, :], lhsT=wg[q][:, kc, fl * 128:(fl + 1) * 128], rhs=hT_sb[:, kc, cs],
                                                              start=(kc == 0), stop=(kc == DC - 1)),
                                     r=[t_wg[q], t_h], w=[t_pg[b]], inc=(kc == DC - 1))
                            for kc in range(DC):
                                k.op("pe", lambda e: e.matmul(pu[b][:, :], lhsT=wu[q][:, kc, fl * 128:(fl + 1) * 128], rhs=hT_sb[:, kc, cs],
                                                              start=(kc == 0), stop=(kc == DC - 1)),
                                     r=[t_wu[q], t_h], w=[t_pu[b]], inc=(kc == DC - 1))
                            k.op("act", lambda e: e.activation(out=sg[b][:, :], in_=pg[b][:, :], func=AF.Silu), r=[t_pg[b]], w=[t_sg[b]])
                            k.op("dve", lambda e: e.tensor_tensor(out=aT[:, fc, cs], in0=sg[b][:, :], in1=pu[b][:, :], op=ALU.mult),
                                 r=[t_sg[b], t_pu[b]], w=[t_a])
                if t + 1 < NT:
                    do_norm(t + 1)
                dq = [load_wd(0), load_wd(1)]
                for dc in range(DC):
                    q = dq[dc]
                    for hf in range(NH2):
                        b = n2 % 2
                        n2 += 1
                        cs = slice(hf * 512, (hf + 1) * 512)
                        for fc in range(FC):
                            k.op("pe", lambda e: e.matmul(po[b][:, :], lhsT=wd[q][:, fc, :], rhs=aT[:, fc, cs],
                                                          start=(fc == 0), stop=(fc == FC - 1)),
                                 r=[t_wd[q], t_a], w=[t_po[b]], inc=(fc == FC - 1))
                        k.op("dve", lambda e: e.scalar_tensor_tensor(out=xb[:, dc, cs], in0=po[b][:, :], scalar=g_col(i, sl, dc),
                                                                     in1=xb[:, dc, cs], op0=ALU.mult, op1=ALU.add),
                             r=[t_po[b], t_mod, tx], w=[tx])
                    if dc + 2 < DC:
                        dq.append(load_wd(dc + 2))
                k.dma("pool", out=xTv[:, :, t * T:(t + 1) * T], in_=xb[:, :, :], r=[tx], st=tx)
        k.barrier("load_wd")


    hTv = hT.rearrange("(kc p) s -> p kc s", p=128)
    xTv_g = xT.rearrange("(dc p) s -> p dc s", p=128)
    mdiag_b = cb[:, C_MDIAG:C_MDIAG + 128]
    mneg_b = cb[:, C_MNEG:C_MNEG + 128]
    ident_b = cb[:, C_IDENT:C_IDENT + 128]
    moff_b = cb[:, C_MOFF:C_MOFF + 128]
    perm_b = cb[:, C_PERM:C_PERM + 128]
    ones_col = cf[:, C_ONES:C_ONES + 1]

    def stop_at(name):
        if debug_stop == name:
            raise StopBuild()

    def h_phase(i):
        T = 1024 if S % 1024 == 0 else 512
        with ExitStack() as st:
            xt = [sb(st, "hx", [128, DC, T], F32) for _ in range(2)]
            t_x = k.toks_n(2)
            nr = NormRes(st, T)
            hs = [sb(st, "hh", [128, DC, T], BF16) for _ in range(2)]
            t_hs = k.toks_n(2)
            NT = S // T
            k.dma("sp", out=xt[0][:, :, :], in_=xTv_g[:, :, 0:T], w=[t_x[0]], st=t_x[0])
            for t in range(NT):
                if t + 1 < NT:
                    k.dma("sp", out=xt[(t + 1) % 2][:, :, :], in_=xTv_g[:, :, (t + 1) * T:(t + 2) * T], w=[t_x[(t + 1) % 2]],
                          st=t_x[(t + 1) % 2])
                hb = hs[t % 2]
                norm_tile(nr, xt[t % 2], t_x[t % 2], i, 1, lambda dc: hb[:, dc, :], t_hs[t % 2])
                k.dma("pool", out=hTv[:, :, t * T:(t + 1) * T], in_=hb[:, :, :], r=[t_hs[t % 2]], st=t_hs[t % 2])
        k.barrier("h_phase")

    def rope_tables():
        TWO_PI = 2.0 * math.pi
        C1 = 6.28125
        C2 = TWO_PI - C1
        W = 2048 if S >= 2048 else S
        with ExitStack() as st:
            pi_ = sb(st, "rp_i", [128, W], I32)
            ang = sb(st, "rp_f", [128, W], F32)
            kf = sb(st, "rp_kf", [128, W], F32)
            m = sb(st, "rp_m", [128, W], F32)
            m2 = sb(st, "rp_m2", [128, W], F32)
            gt = sb(st, "rp_gt", [128, W], F32)
            t_pi, t_ang, t_kf, t_m, t_m2, t_gt = (k.tok() for _ in range(6))

            def wrap(buf, tb):
                k.op("dve", lambda e: e.tensor_scalar(out=gt[:, :], in0=buf[:, :], scalar1=math.pi, scalar2=None, op0=ALU.is_gt),
                     r=[tb], w=[t_gt])
                k.op("dve", lambda e: e.scalar_tensor_tensor(out=buf[:, :], in0=gt[:, :], scalar=-TWO_PI, in1=buf[:, :],
                                                             op0=ALU.mult, op1=ALU.add), r=[t_gt, tb], w=[tb])
                k.op("dve", lambda e: e.tensor_scalar(out=gt[:, :], in0=buf[:, :], scalar1=-math.pi, scalar2=None, op0=ALU.is_lt),
                     r=[tb], w=[t_gt])
                k.op("dve", lambda e: e.scalar_tensor_tensor(out=buf[:, :], in0=gt[:, :], scalar=TWO_PI, in1=buf[:, :],
                                                             op0=ALU.mult, op1=ALU.add), r=[t_gt, tb], w=[tb])
                k.op("dve", lambda e: e.tensor_scalar(out=buf[:, :], in0=buf[:, :], scalar1=-math.pi, scalar2=math.pi,
                                                      op0=ALU.max, op1=ALU.min), r=[tb], w=[tb])

            for hf in range(S // W):
                cs = slice(hf * W, (hf + 1) * W)
                k.dma("sp", out=pi_[:, :], in_=pos_in[0:1, cs].partition_broadcast(128), w=[t_pi], st=t_pi)
                k.op("dve", lambda e: e.tensor_copy(out=ang[:, :], in_=pi_[:, :]), r=[t_pi], w=[t_ang])
                k.op("dve", lambda e: e.tensor_scalar(out=ang[:, :], in0=ang[:, :], scalar1=cf[:, C_ROPE:C_ROPE + 1], scalar2=None,
                                                      op0=ALU.mult), r=[t_ang, t_cf], w=[t_ang])
                k.op("dve", lambda e: e.tensor_scalar(out=kf[:, :], in0=ang[:, :], scalar1=1.0 / TWO_PI, scalar2=None, op0=ALU.mult),
                     r=[t_ang], w=[t_kf])
                k.op("dve", lambda e: e.tensor_copy(out=pi_[:, :], in_=kf[:, :]), r=[t_kf, t_pi], w=[t_pi])
                k.op("dve", lambda e: e.tensor_copy(out=kf[:, :], in_=pi_[:, :]), r=[t_pi], w=[t_kf])
                k.op("dve", lambda e: e.scalar_tensor_tensor(out=m[:, :], in0=kf[:, :], scalar=-C1, in1=ang[:, :], op0=ALU.mult, op1=ALU.add),
                     r=[t_kf, t_ang], w=[t_m])
                k.op("dve", lambda e: e.scalar_tensor_tensor(out=m[:, :], in0=kf[:, :], scalar=-C2, in1=m[:, :], op0=ALU.mult, op1=ALU.add),
                     r=[t_kf, t_m], w=[t_m])
                wrap(m, t_m)
                k.op("dve", lambda e: e.tensor_scalar(out=m2[:, :], in0=m[:, :], scalar1=0.5 * math.pi, scalar2=None, op0=ALU.add),
                     r=[t_m], w=[t_m2])
                wrap(m2, t_m2)
                k.op("act", lambda e: e.activation(out=m[:, :], in_=m[:, :], func=AF.Sin), r=[t_m], w=[t_m])
                k.op("dve", lambda e: e.tensor_scalar(out=m[:, :], in0=m[:, :], scalar1=cf[:, C_ROPE + 1:C_ROPE + 2], scalar2=None,
                                                      op0=ALU.mult), r=[t_m, t_cf], w=[t_m])
                k.dma("pool", out=cs_s[1][:, cs], in_=m[:, :], r=[t_m], st=t_m)
                k.op("act", lambda e: e.activation(out=m2[:, :], in_=m2[:, :], func=AF.Sin), r=[t_m2], w=[t_m2])
                k.dma("pool", out=cs_s[0][:, cs], in_=m2[:, :], r=[t_m2], st=t_m2)
        k.barrier("wrap")

    def mixer_phase(i):
        is_fox = (i % 2 == 0)
        j = i // 2
        cv = conv_tok[f"a{i}"]
        winv = win_s[i].rearrange("(kc p) f -> p kc f", p=128)
        dils = [1] if is_fox else [c[1] for c in DIL_CFG]
        NG = len(dils)
        NHALF = S // 2048 if S >= 2048 else 1
        HL = S // NHALF
        h_phase(i)
        stop_at(f"hph{i}")
        for g, d in enumerate(dils):
            nb = S // d // 128
            with ExitStack() as stg_:
                v_all = sb(stg_, "v_all", [128, NTB, NH, HD + 1], BF16)
                t_v = k.toks_n(NTB, "v")
                t_vone = k.tok()
                k.op("pool", lambda e: e.memset(v_all[:, :, :, HD:HD + 1], 1.0), w=[t_vone])
                sp_all = sb(stg_, "sp_all", [128, NTB, NH], F32) if is_fox else None
                t_sp = k.toks_n(NTB, "sp") if is_fox else None
                with ExitStack() as st:
                    qc0 = 0 if is_fox else g * 3 * D
                    wv = sb(st, "wv", [128, DC, D], BF16)
                    t_wv = k.tok()
                    k.dma("sp", out=wv[:, :, :], in_=winv[:, :, qc0 + 2 * D:qc0 + 3 * D], r=[cv], w=[t_wv], st=t_wv)
                    gq = sb(st, "gq", [128, 1], F32)
                    gk = sb(st, "gk", [128, 1], F32)
                    t_g = k.tok()
                    if is_fox:
                        srcq = fox_q_g[j:j + 1, :].rearrange("o d -> d o")
                        srck = fox_k_g[j:j + 1, :].rearrange("o d -> d o")
                    else:
                        srcq = dil_q_g[j, g:g + 1, :].rearrange("o d -> d o")
                        srck = dil_k_g[j, g:g + 1, :].rearrange("o d -> d o")
                    for a in range(2):
                        k.dma("sp", out=gq[a * 64:(a + 1) * 64, :], in_=srcq, w=[t_g], st=t_g)
                        k.dma("sp", out=gk[a * 64:(a + 1) * 64, :], in_=srck, w=[t_g], st=t_g)
                    k.op("dve", lambda e: e.tensor_scalar(out=gq[:, :], in0=gq[:, :], scalar1=0.125, scalar2=None, op0=ALU.mult),
                         r=[t_g], w=[t_g])
                    if is_fox:
                        wf = sb(st, "wf", [128, DC, NH], BF16)
                        bfb = sb(st, "bfb", [128, NH], F32)
                        t_wf = k.tok()
                        k.dma("sp", out=wf[:, :, :], in_=winv[:, :, 3 * D:3 * D + NH], r=[cv], w=[t_wf], st=t_wf)
                        k.dma("sp", out=bfb[:, :], in_=fox_b_f[j:j + 1, :].partition_broadcast(128), w=[t_wf], st=t_wf)
                        zt = [sb(st, "zt", [128, NH], F32) for _ in range(2)]
                        t_zt = k.toks_n(2)
                        pf = ps(st, "pf")
                        t_pf = k.tok()
                    else:
                        ct = sb(st, "ct", [128, HL], F32)
                        stt = sb(st, "stt", [128, HL], F32)
                        t_cs = k.tok()
                        t1 = [sb(st, "t1", [128, 512], F32) for _ in range(2)]
                        t2 = [sb(st, "t2", [128, 512], F32) for _ in range(2)]
                        t_t1 = k.toks_n(2)
                        t_t2 = k.toks_n(2)
                        pperm = ps(st, "pperm")
                        t_pperm = k.tok()
                    hh = sb(st, "hhalf", [128, DC, HL], BF16)
                    t_hh = k.tok()
                    wqk = [sb(st, "wqk", [128, DC, 128], BF16) for _ in range(3)]
                    t_wqk = k.toks_n(3)
                    stg = [sb(st, "stg", [128, HL], BF16) for _ in range(2)]
                    t_stg = k.toks_n(2)
                    NB = 3
                    sq = [sb(st, "bsq", [128, 512], BF16) for _ in range(NB)]
                    t_sq = k.toks_n(NB)
                    rs = [sb(st, "brs", [128, 512], F32) for _ in range(NB)]
                    t_rs = k.toks_n(NB)
                    pq = [ps(st, "pq") for _ in range(NB)]
                    t_pq = k.toks_n(NB)
                    pssq = [ps(st, "pssq") for _ in range(2)]
                    t_pssq = k.toks_n(2)
                    pv = pq[0:2]
                    t_pvp = t_pq[0:2]
                    if not is_fox:
                        qn = [sb(st, "qn", [128, 512], F32) for _ in range(NB)]
                        qnb = [sb(st, "qnb", [128, 512], BF16) for _ in range(NB)]
                        t_qn = k.toks_n(NB)
                        t_qnb = k.toks_n(NB)
                        pperm2 = [pperm, ps(st, "pperm2")]
                        t_pperm2 = [t_pperm, k.tok()]
                    NTT = HL // 512
                    for hf in range(NHALF):
                        k.dma("sp", out=hh[:, :, :], in_=hTv[:, :, hf * HL:(hf + 1) * HL], w=[t_hh], st=t_hh)
                        if not is_fox:
                            k.dma("sp", out=ct[:, :], in_=cs_s[0][:, hf * HL:(hf + 1) * HL], w=[t_cs], st=t_cs)
                            k.dma("sp", out=stt[:, :], in_=cs_s[1][:, hf * HL:(hf + 1) * HL], w=[t_cs], st=t_cs)
                        items = [(c, tt) for c in range(16) for tt in range(NTT)]
                        NI = len(items)

                        def wload(c):
                            a = c // 8
                            cc = c % 8
                            col0 = qc0 + a * D + cc * 128
                            k.dma("sp", out=wqk[c % 3][:, :, :], in_=winv[:, :, col0:col0 + 128], r=[cv], w=[t_wqk[c % 3]], st=t_wqk[c % 3])

                        def stage_a(n):
                            c, tt = items[n]
                            if tt == 0 and c + 1 < 16:
                                wload(c + 1)
                            b = n % NB
                            cs = slice(tt * 512, (tt + 1) * 512)
                            for kc in range(DC):
                                k.op("pe", lambda e: e.matmul(pq[b][:, :], lhsT=wqk[c % 3][:, kc, :], rhs=hh[:, kc, cs],
                                                              start=(kc == 0), stop=(kc == DC - 1)),
                                     r=[t_wqk[c % 3], t_hh], w=[t_pq[b]], inc=(kc == DC - 1))
                            k.op("act", lambda e: e.activation(out=sq[b][:, :], in_=pq[b][:, :], func=AF.Square), r=[t_pq[b]], w=[t_sq[b]])

                        def stage_b(n):
                            c, tt = items[n]
                            b = n % NB
                            b2 = n % 2
                            sg_ = c % 2
                            cs = slice(tt * 512, (tt + 1) * 512)
                            gain = gq if c < 8 else gk
                            k.op("pe", lambda e: e.matmul(pssq[b2][:, :], lhsT=blk_b, rhs=sq[b][:, :], start=True, stop=True),
                                 r=[t_sq[b], t_cf], w=[t_pssq[b2]])
                            k.op("act", lambda e: e.activation(out=rs[b][:, :], in_=pssq[b2][:, :], func=AF.Ln, scale=1.0 / HD, bias=eps_col),
                                 r=[t_pssq[b2], t_cf], w=[t_rs[b]])
                            k.op("act", lambda e: e.activation(out=rs[b][:, :], in_=rs[b][:, :], func=AF.Exp, scale=-0.5),
                                 r=[t_rs[b]], w=[t_rs[b]])
                            if is_fox:
                                k.op("dve", lambda e: e.scalar_tensor_tensor(out=stg[sg_][:, cs], in0=pq[b][:, :], scalar=gain[:, 0:1],
                                                                             in1=rs[b][:, :], op0=ALU.mult, op1=ALU.mult),
                                     r=[t_pq[b], t_g, t_rs[b]], w=[t_stg[sg_]])
                                if tt == NTT - 1:
                                    store(c)
                            else:
                                k.op("dve", lambda e: e.scalar_tensor_tensor(out=qn[b][:, :], in0=pq[b][:, :], scalar=gain[:, 0:1],
                                                                             in1=rs[b][:, :], op0=ALU.mult, op1=ALU.mult),
                                     r=[t_pq[b], t_g, t_rs[b]], w=[t_qn[b]])
                                k.op("act", lambda e: e.activation(out=qnb[b][:, :], in_=qn[b][:, :], func=AF.Identity),
                                     r=[t_qn[b]], w=[t_qnb[b]])

                        def stage_c(n):
                            c, tt = items[n]
                            b = n % NB
                            b2 = n % 2
                            sg_ = c % 2
                            cs = slice(tt * 512, (tt + 1) * 512)
                            k.op("pe", lambda e: e.matmul(pperm2[b2][:, :], lhsT=perm_b, rhs=qnb[b][:, :], start=True, stop=True),
                                 r=[t_qnb[b], t_cf], w=[t_pperm2[b2]])
                            k.op("pool", lambda e: e.tensor_tensor(out=t1[b2][:, :], in0=qn[b][:, :], in1=ct[:, cs], op=ALU.mult),
                                 r=[t_qn[b], t_cs], w=[t_t1[b2]])
                            k.op("dve", lambda e: e.tensor_tensor(out=t2[b2][:, :], in0=pperm2[b2][:, :], in1=stt[:, cs], op=ALU.mult),
                                 r=[t_pperm2[b2], t_cs], w=[t_t2[b2]])
                            k.op("dve", lambda e: e.tensor_tensor(out=stg[sg_][:, cs], in0=t1[b2][:, :], in1=t2[b2][:, :], op=ALU.add),
                                 r=[t_t1[b2], t_t2[b2]], w=[t_stg[sg_]])
                            if tt == NTT - 1:
                                store(c)

                        def store(c):
                            a = c // 8
                            cc = c % 8
                            sg_ = c % 2
                            k.dma("pool", out=qk_s[g][a][cc * 128:(cc + 1) * 128, hf * HL:(hf + 1) * HL], in_=stg[sg_][:, :],
                                  r=[t_stg[sg_]], st=t_stg[sg_])

                        wload(0)
                        for n in range(NI + 2):
                            if n < NI:
                                stage_a(n)
                            if 0 <= n - 1 < NI:
                                stage_b(n - 1)
                            if (not is_fox) and 0 <= n - 2 < NI:
                                stage_c(n - 2)
                        bph = HL // 128
                        for bi in range(bph):
                            if d * 128 <= HL:
                                spans = HL // (128 * d)
                                sp_i = bi // d if False else None
                            sidx = bi // d
                            r = bi % d
                            jb = hf * (HL // (128 * d)) + sidx
                            blk = r * nb + jb
                            start = sidx * 128 * d + r
                            cols = slice(start, start + 127 * d + 1, d) if d > 1 else slice(start, start + 128)
                            for hv in range(2):
                                for kc in range(DC):
                                    k.op("pe", lambda e: e.matmul(pv[hv][:, :], lhsT=hh[:, kc, cols], rhs=wv[:, kc, hv * 512:(hv + 1) * 512],
                                                                  start=(kc == 0), stop=(kc == DC - 1)),
                                         r=[t_hh, t_wv], w=[t_pvp[hv]], inc=(kc == DC - 1))
                            k.op("act", lambda e: e.activation(out=v_all[:, blk, 0:8, 0:HD], in_=pv[0][:, :].rearrange("p (h e) -> p h e", h=8),
                                                               func=AF.Identity), r=[t_pvp[0]], w=[t_v[blk]])
                            k.op("dve", lambda e: e.tensor_copy(out=v_all[:, blk, 8:16, 0:HD], in_=pv[1][:, :].rearrange("p (h e) -> p h e", h=8)),
                                 r=[t_pvp[1]], w=[t_v[blk]])
                            if is_fox:
                                zb = bi % 2
                                for kc in range(DC):
                                    k.op("pe", lambda e: e.matmul(pf[:, 0:NH], lhsT=hh[:, kc, cols], rhs=wf[:, kc, :],
                                                                  start=(kc == 0), stop=(kc == DC - 1)),
                                         r=[t_hh, t_wf], w=[t_pf], inc=(kc == DC - 1))
                                k.op("dve", lambda e: e.tensor_tensor(out=zt[zb][:, :], in0=pf[:, 0:NH], in1=bfb[:, :], op=ALU.add),
                                     r=[t_pf, t_wf], w=[t_zt[zb]])
                                k.op("act", lambda e: e.activation(out=zt[zb][:, :], in_=zt[zb][:, :], func=AF.Exp, scale=-1.0),
                                     r=[t_zt[zb]], w=[t_zt[zb]])
                                k.op("act", lambda e: e.activation(out=sp_all[:, blk, :], in_=zt[zb][:, :], func=AF.Ln, bias=ones_col),
                                     r=[t_zt[zb], t_cf], w=[t_sp[blk]])
                k.barrier("mixer_phase")
                stop_at(f"b1_{i}_{g}")
                if is_fox:
                    fox_cum(sp_all)
                    stop_at(f"cum{i}")
                if is_fox:
                    b2_fox(v_all)
                else:
                    b2_dil(g, d, v_all)
                stop_at(f"b2_{i}_{g}")
        merge_outproj(i, NG)

    def fox_cum(sp_all):
        with ExitStack() as st:
            cum = sb(st, "cum", [NH, S], F32)
            t_cum = k.tok()
            pre = sb(st, "pre", [NH, NTB], F32)
            t_pre = k.tok()
            hif = sb(st, "hif", [NH, S], F32)
            t_hif = k.tok()
            rb = [sb(st, "rb", [NH, S], BF16) for _ in range(2)]
            t_rb = k.toks_n(2)
            oneb = sb(st, "oneb", [NH, S], BF16)
            t_one = k.tok()
            pc = [ps(st, "pc") for _ in range(2)]
            t_pc = k.toks_n(2)
            for m4 in range(NTB // 4):
                b = m4 % 2
                for mm in range(4):
                    m = m4 * 4 + mm
                    k.op("pe", lambda e: e.matmul(pc[b][0:NH, mm * 128:(mm + 1) * 128], lhsT=sp_all[:, m, :], rhs=utri_f, start=True, stop=True),
                         r=[t_cf], w=[t_pc[b]], inc=(mm == 3))
                k.op("act", lambda e: e.activation(out=cum[:, m4 * 512:(m4 + 1) * 512], in_=pc[b][0:NH, :], func=AF.Identity),
                     r=[t_pc[b]], w=[t_cum])
            k.op("dve", lambda e: e.memset(pre[:, 0:1], 0.0), w=[t_pre])
            for m in range(1, NTB):
                k.op("dve", lambda e: e.tensor_tensor(out=pre[:, m:m + 1], in0=pre[:, m - 1:m], in1=cum[:, m * 128 - 1:m * 128], op=ALU.add),
                     r=[t_cum, t_pre], w=[t_pre])
            for m in range(1, NTB):
                k.op("dve", lambda e: e.tensor_scalar(out=cum[:, m * 128:(m + 1) * 128], in0=cum[:, m * 128:(m + 1) * 128],
                                                      scalar1=pre[:, m:m + 1], scalar2=None, op0=ALU.add), r=[t_pre, t_cum], w=[t_cum])
            k.op("pool", lambda e: e.memset(oneb[:, :], 1.0), w=[t_one])
            for row in range(3):
                k.dma("sp", out=aug_s[1][:, row, :], in_=oneb[:, :], r=[t_one], st=t_one)
                k.dma("sp", out=aug_s[0][:, 3 + row, :], in_=oneb[:, :], r=[t_one], st=t_one)
            for part in range(3):
                kb_, qb_ = rb[0], rb[1]
                k.op("dve", lambda e: e.tensor_copy(out=kb_[:, :], in_=cum[:, :]), r=[t_cum], w=[t_rb[0]])
                k.op("act", lambda e: e.activation(out=qb_[:, :], in_=kb_[:, :], func=AF.Identity, scale=-1.0), r=[t_rb[0]], w=[t_rb[1]])
                k.dma("sp", out=aug_s[1][:, 3 + part, :], in_=kb_[:, :], r=[t_rb[0]], st=t_rb[0])
                k.dma("sp", out=aug_s[0][:, part, :], in_=qb_[:, :], r=[t_rb[1]], st=t_rb[1])
                if part < 2:
                    k.op("dve", lambda e: e.tensor_copy(out=hif[:, :], in_=kb_[:, :]), r=[t_rb[0]], w=[t_hif])
                    k.op("dve", lambda e: e.tensor_tensor(out=cum[:, :], in0=cum[:, :], in1=hif[:, :], op=ALU.subtract),
                         r=[t_hif, t_cum], w=[t_cum])
        k.barrier("fox_cum")

    def b2_fox(v_all):
        NQT = S // 512
        with ExitStack() as st:
            qa = [sb(st, "qa", [HD + 6, S], BF16) for _ in range(2)]
            ka = [sb(st, "ka", [HD + 6, S], BF16) for _ in range(2)]
            t_qa = k.toks_n(2)
            t_ka = k.toks_n(2)
            NP = 4
            pt = [sb(st, "pt", [128, 512], BF16) for _ in range(NP)]
            t_pt = k.toks_n(NP)
            ost = [sb(st, "ost", [HD + 1, S], F32) for _ in range(2)]
            t_ost = k.toks_n(2)
            NSP = 4
            sps = [ps(st, "sps") for _ in range(NSP)]
            t_sps = k.toks_n(NSP)
            ops_ = [ps(st, "ops") for _ in range(2)]
            t_ops = k.toks_n(2)

            def load_head(h):
                b = h % 2
                k.dma("sp", out=qa[b][0:HD, :], in_=qk_s[0][0][h * HD:(h + 1) * HD, :], w=[t_qa[b]], st=t_qa[b])
                k.dma("sp", out=qa[b][HD:HD + 6, :], in_=aug_s[0][h], w=[t_qa[b]], st=t_qa[b])
                k.dma("sp", out=ka[b][0:HD, :], in_=qk_s[0][1][h * HD:(h + 1) * HD, :], w=[t_ka[b]], st=t_ka[b])
                k.dma("sp", out=ka[b][HD:HD + 6, :], in_=aug_s[1][h], w=[t_ka[b]], st=t_ka[b])

            load_head(0)
            nstep = 0
            for h in range(NH):
                hb = h % 2
                if h + 1 < NH:
                    load_head(h + 1)
                steps = [(jt, kb) for jt in range(NQT) for kb in range(4 * jt + 4)]
                LA = 3

                def emit_qk(idx):
                    jt, kb = steps[idx]
                    sidx = (nstep + idx) % NSP
                    qlo = max(kb, 4 * jt) * 128
                    W = (4 * jt + 4) * 128 - qlo
                    diag = kb >= 4 * jt
                    k.op("pe", lambda e: e.matmul(sps[sidx][:, 0:W], lhsT=ka[hb][:, kb * 128:(kb + 1) * 128], rhs=qa[hb][:, qlo:qlo + W],
                                                  start=True, stop=not diag), r=[t_ka[hb], t_qa[hb]], w=[t_sps[sidx]], inc=not diag)
                    if diag:
                        k.op("pe", lambda e: e.matmul(sps[sidx][:, 0:128], lhsT=ident_b, rhs=mneg_b, start=False, stop=True),
                             r=[t_cf], w=[t_sps[sidx]])

                for idx in range(min(LA, len(steps))):
                    emit_qk(idx)
                for idx, (jt, kb) in enumerate(steps):
                    if idx + LA < len(steps):
                        emit_qk(idx + LA)
                    sidx = (nstep + idx) % NSP
                    pidx = (nstep + idx) % NP
                    qlo = max(kb, 4 * jt) * 128
                    W = (4 * jt + 4) * 128 - qlo
                    off = qlo - 4 * jt * 128
                    ob = jt % 2
                    k.op("act", lambda e: e.activation(out=pt[pidx][:, 0:W], in_=sps[sidx][:, 0:W], func=AF.Exp), r=[t_sps[sidx]], w=[t_pt[pidx]])
                    last = (kb == 4 * jt + 3)
                    k.op("pe", lambda e: e.matmul(ops_[ob][0:HD + 1, off:off + W], lhsT=v_all[:, kb, h, :], rhs=pt[pidx][:, 0:W],
                                                  start=(kb == 0), stop=last), r=[t_pt[pidx]], w=[t_ops[ob]], inc=last)
                    if last:
                        k.op("dve", lambda e: e.tensor_copy(out=ost[hb][:, jt * 512:(jt + 1) * 512], in_=ops_[ob][0:HD + 1, :]),
                             r=[t_ops[ob]], w=[t_ost[hb]])
                nstep += len(steps)
                k.dma("pool", out=oun_o[0][h * HD:(h + 1) * HD, :], in_=ost[hb][0:HD, :], r=[t_ost[hb]], st=t_ost[hb])
                k.dma("pool", out=oun_d[0][h:h + 1, :], in_=ost[hb][HD:HD + 1, :], r=[t_ost[hb]], st=t_ost[hb])
        k.barrier("emit_qk")

    def b2_dil(g, d, v_all):
        nb = S // d // 128
        DBG = 0
        with ExitStack() as st:
            qa = [sb(st, "dq", [HD, S], BF16) for _ in range(2)]
            ka = [sb(st, "dk", [HD, S], BF16) for _ in range(2)]
            t_qa = k.toks_n(2)
            t_ka = k.toks_n(2)
            NP = 4
            pt = [sb(st, "dpt", [128, 256], BF16) for _ in range(NP)]
            t_pt = k.toks_n(NP)
            ost = [sb(st, "dost", [HD + 1, S], F32) for _ in range(2)]
            t_ost = k.toks_n(2)
            NSP = 4
            sps = [ps(st, "dsps") for _ in range(NSP)]
            t_sps = k.toks_n(NSP)
            NOS = 4
            ops_ = [ps(st, "dops") for _ in range(NOS)]
            t_os = k.toks_n(NOS)

            def oslot(n):
                c = (n // NOS) % 4
                return ops_[n % NOS][0:HD + 1, c * 128:(c + 1) * 128]

            def load_head(h):
                b = h % 2
                k.dma("sp", out=qa[b][:, :], in_=qk_s[g][0][h * HD:(h + 1) * HD, :], w=[t_qa[b]], st=t_qa[b])
                k.dma("sp", out=ka[b][:, :], in_=qk_s[g][1][h * HD:(h + 1) * HD, :], w=[t_ka[b]], st=t_ka[b])

            def sl(start, cnt):
                return slice(start, start + (cnt - 1) * d + 1, d) if d > 1 else slice(start, start + cnt)

            load_head(0)
            nstep = 0
            nos = 0
            for h in range(NH):
                hb = h % 2
                if h + 1 < NH:
                    load_head(h + 1)
                steps = [(r, jb) for r in range(d) for jb in range(nb)]
                LA = 3

                def emit_qk(idx):
                    r, jb = steps[idx]
                    sidx = (nstep + idx) % NSP
                    cnt = 256 if jb + 1 < nb else 128
                    kst = r + d * 128 * jb
                    k.op("pe", lambda e: e.matmul(sps[sidx][:, 0:cnt], lhsT=ka[hb][:, sl(kst, 128)], rhs=qa[hb][:, sl(kst, cnt)],
                                                  start=True, stop=True), r=[t_ka[hb], t_qa[hb]], w=[t_sps[sidx]])

                for idx in range(min(LA, len(steps))):
                    emit_qk(idx)
                for idx, (r, jb) in enumerate(steps):
                    if idx + LA < len(steps):
                        emit_qk(idx + LA)
                    sidx = (nstep + idx) % NSP
                    pidx = (nstep + idx) % NP
                    cnt = 256 if jb + 1 < nb else 128
                    kst = r + d * 128 * jb
                    blk = r * nb + jb
                    k.op("act", lambda e: e.activation(out=pt[pidx][:, 0:cnt], in_=sps[sidx][:, 0:cnt], func=AF.Exp), r=[t_sps[sidx]], w=[t_pt[pidx]])
                    k.op("dve", lambda e: e.tensor_tensor(out=pt[pidx][:, 0:cnt], in0=pt[pidx][:, 0:cnt], in1=cb[:, C_MDIAG:C_MDIAG + cnt], op=ALU.mult),
                         r=[t_pt[pidx], t_cf], w=[t_pt[pidx]])
                    s0 = nos + jb
                    k.op("pe", lambda e: e.matmul(oslot(s0), lhsT=v_all[:, blk, h, :], rhs=pt[pidx][:, 0:128], start=(jb == 0) or DBG == 1, stop=True),
                         r=[t_pt[pidx]], w=[t_os[s0 % NOS]])
                    if DBG != 3:
                        k.op("dve", lambda e: e.tensor_copy(out=ost[hb][:, sl(kst, 128)], in_=oslot(s0)), r=[t_os[s0 % NOS]], w=[t_ost[hb]])
                    if cnt == 256:
                        s1 = nos + jb + 1
                        k.op("pe", lambda e: e.matmul(oslot(s1), lhsT=v_all[:, blk, h, :], rhs=pt[pidx][:, 128:256], start=True, stop=(DBG == 1)),
                             r=[t_pt[pidx]], w=[t_os[s1 % NOS]])
                    if jb == nb - 1:
                        nos += nb
                nstep += len(steps)
                k.dma("pool", out=oun_o[g][h * HD:(h + 1) * HD, :], in_=ost[hb][0:HD, :], r=[t_ost[hb]], st=t_ost[hb])
                k.dma("pool", out=oun_d[g][h:h + 1, :], in_=ost[hb][HD:HD + 1, :], r=[t_ost[hb]], st=t_ost[hb])
        k.barrier("emit_qk")

    def merge_outproj(i, NG):
        cv = conv_tok[f"a{i}"]
        T = 512
        woutv = wout_s[i].rearrange("(c p) d -> p c d", p=128)
        with ExitStack() as st:
            wo = sb(st, "wo", [128, DC, D], BF16)
            t_wo = k.tok()
            k.dma("sp", out=wo[:, :, :], in_=woutv[:, :, :], r=[cv], w=[t_wo], st=t_wo)
            xt = [sb(st, "mx", [128, DC, T], F32) for _ in range(2)]
            t_x = k.toks_n(2)
            on = [[sb(st, "on", [128, DC, T], F32) for _ in range(NG)] for _ in range(2)]
            t_on = [k.toks_n(NG) for _ in range(2)]
            dn = [sb(st, "dn", [NH, NG, T], F32) for _ in range(2)]
            t_dn = k.toks_n(2)
            rec = sb(st, "rec", [NH, T], F32)
            t_rec = k.tok()
            osum = [sb(st, "osum", [128, T], F32) for _ in range(2)]
            t_osum = k.toks_n(2)
            oT = sb(st, "oT", [128, DC, T], BF16)
            t_oT = k.tok()
            py = [ps(st, "py") for _ in range(2)]
            t_py = k.toks_n(2)
            pbc = [ps(st, "pbc") for _ in range(2)]
            t_pbc = k.toks_n(2)
            NT = S // T

            def load_t(t):
                b = t % 2
                cs = slice(t * T, (t + 1) * T)
                k.dma("sp", out=xt[b][:, :, :], in_=xTv_g[:, :, cs], w=[t_x[b]], st=t_x[b])
                for g in range(NG):
                    k.dma("sp", out=on[b][g][:, :, :], in_=oun_o[g].rearrange("(c p) s -> p c s", p=128)[:, :, cs],
                          w=[t_on[b][g]], st=t_on[b][g])
                    k.dma("sp", out=dn[b][:, g, :], in_=oun_d[g][:, cs], w=[t_dn[b]], st=t_dn[b])

            load_t(0)
            ny = 0
            nb_ = 0
            for t in range(NT):
                b = t % 2
                xb = xt[b]
                tx = t_x[b]
                if t + 1 < NT:
                    load_t(t + 1)
                for g in range(1, NG):
                    k.op("dve", lambda e: e.tensor_tensor(out=dn[b][:, 0, :], in0=dn[b][:, 0, :], in1=dn[b][:, g, :], op=ALU.add),
                         r=[t_dn[b]], w=[t_dn[b]])
                k.op("dve", lambda e: e.reciprocal(out=rec[:, :], in_=dn[b][:, 0, :]), r=[t_dn[b]], w=[t_rec])
                for c in range(DC):
                    bb = nb_ % 2
                    nb_ += 1
                    k.op("pe", lambda e: e.matmul(pbc[bb][:, :], lhsT=cf[0:NH, C_ESEL + c * 128:C_ESEL + (c + 1) * 128], rhs=rec[:, :],
                                                  start=True, stop=True), r=[t_rec, t_cf], w=[t_pbc[bb]])
                    src = on[b][0][:, c, :]
                    rd = [t_on[b][0]]
                    if NG == 3:
                        k.op("pool", lambda e: e.tensor_tensor(out=osum[bb][:, :], in0=on[b][1][:, c, :], in1=on[b][2][:, c, :], op=ALU.add),
                             r=[t_on[b][1], t_on[b][2]], w=[t_osum[bb]])
                        k.op("dve", lambda e: e.tensor_tensor(out=osum[bb][:, :], in0=osum[bb][:, :], in1=on[b][0][:, c, :], op=ALU.add),
                             r=[t_on[b][0], t_osum[bb]], w=[t_osum[bb]])
                        src = osum[bb][:, :]
                        rd = [t_osum[bb]]
                    k.op("dve", lambda e: e.tensor_tensor(out=oT[:, c, :], in0=src, in1=pbc[bb][:, :], op=ALU.mult),
                         r=rd + [t_pbc[bb]], w=[t_oT])
                for dc in range(DC):
                    yb = ny % 2
                    ny += 1
                    for c in range(DC):
                        k.op("pe", lambda e: e.matmul(py[yb][:, :], lhsT=wo[:, c, dc * 128:(dc + 1) * 128], rhs=oT[:, c, :],
                                                      start=(c == 0), stop=(c == DC - 1)), r=[t_wo, t_oT], w=[t_py[yb]], inc=(c == DC - 1))
                    k.op("dve", lambda e: e.scalar_tensor_tensor(out=xb[:, dc, :], in0=py[yb][:, :], scalar=g_col(i, 1, dc),
                                                                 in1=xb[:, dc, :], op0=ALU.mult, op1=ALU.add),
                         r=[t_py[yb], t_mod, tx], w=[tx])
                k.dma("pool", out=xTv_g[:, :, t * T:(t + 1) * T], in_=xb[:, :, :], r=[tx], st=tx)
        k.barrier("merge_outproj")

    def program():
        if debug_stop == "conv":
            k.barrier("program")
            return
        setup_mods()
        if debug_stop == "mods":
            return
        transpose_in()
        if debug_stop == "tin":
            transpose_out()
            return
        if depth >= 2:
            rope_tables()
            stop_at("rope")
        for i in range(depth):
            conv_upto(3 * i + 3)
            ffn_phase(i, 0)
            if debug_stop == f"ffn{i}0":
                break
            conv_upto(3 * i + 4)
            mixer_phase(i)
            if debug_stop == f"mix{i}":
                break
            conv_upto(3 * i + 5)
            ffn_phase(i, 1)
        transpose_out()

    stopped = False
    try:
        program()
    except StopBuild:
        stopped = True
    for key in sorted(k.bgkeys):
        if k.seen["sp"].get(key, 0) < k.cnt[key]:
            nc.sync.wait_ge(k.sems[key], k.cnt[key])
    if not stopped:
        top.close()
    nc._phase_names = k.phase_names
    bad = k.check()
    if bad:
        raise RuntimeError(f"semaphore protocol deadlock: {bad}")
    return nc


_CACHE = {}


def _prep_inputs(inputs, b, S, depth, cfc):
    n_fox = (depth + 1) // 2
    n_dil = depth // 2
    f = lambda a: np.ascontiguousarray(a, dtype=np.float32)
    m = {
        "x": f(inputs["x"][b, :S]),
        "c": f(inputs["c"][b]).reshape(DC, 128),
        "positions": np.ascontiguousarray(inputs["positions"][b, :S], dtype=np.int32).reshape(1, S),
        "mod_w": f(inputs["mod_w"][:depth]),
        "mod_b": f(inputs["mod_b"][:depth]).reshape(depth * 72, 128),
        "norm_g": f(inputs["norm_g"][:depth]).reshape(depth * 24, 128),
        "ffn_w_gate": f(inputs["ffn_w_gate"][:depth]),
        "ffn_w_up": f(inputs["ffn_w_up"][:depth]),
        "ffn_w_down": f(inputs["ffn_w_down"][:depth]),
        "fox_w_in": f(inputs["fox_w_in"][:max(n_fox, 1)]),
        "fox_b_f": f(inputs["fox_b_f"][:max(n_fox, 1)]),
        "fox_q_g": f(inputs["fox_q_g"][:max(n_fox, 1)]),
        "fox_k_g": f(inputs["fox_k_g"][:max(n_fox, 1)]),
        "fox_w_out": f(inputs["fox_w_out"][:max(n_fox, 1)]),
        "dil_w_in": f(inputs["dil_w_in"][:max(n_dil, 1)]),
        "dil_q_g": f(inputs["dil_q_g"][:max(n_dil, 1)]),
        "dil_k_g": f(inputs["dil_k_g"][:max(n_dil, 1)]),
        "dil_w_out": f(inputs["dil_w_out"][:max(n_dil, 1)]),
        "cf": cfc,
    }
    return m


def run(inputs, S=4096, depth=4, n_cores=8, debug_stop=None, trace=False):
    key = (S, depth, debug_stop)
    if key not in _CACHE:
        _CACHE[key] = build(S, depth, debug_stop)
    nc = _CACHE[key]
    cfc = make_consts()
    in_maps = [_prep_inputs(inputs, b, S, depth, cfc) for b in range(n_cores)]
    res = run_bass_kernel_spmd(nc, in_maps, core_ids=list(range(n_cores)), **({"trace": True} if trace else {}))
    out = np.stack([np.asarray(r["y"], dtype=np.float32) for r in res.results], axis=0)
    return out, res


def kernel(**inputs):
    out, _ = run(inputs)
    return out
```
